# Optimizing a Trainium2 kernel written in Bass

```python
import jax, jax.numpy as jnp
from jax import lax
import numpy as np

D_MODEL = 4096
BATCH = 4
SEQ = 4096
DEPTH = 2

MIX_WIDTH = D_MODEL
RET_HEADS = 16
SB_HEADS = 16
HEAD_DIM = MIX_WIDTH // (RET_HEADS + SB_HEADS)
RET_WIDTH = RET_HEADS * HEAD_DIM
SB_WIDTH = SB_HEADS * HEAD_DIM
IN_WIDTH = 4 * RET_WIDTH + 3 * SB_WIDTH
D_FF = 2 * D_MODEL
RET_CHUNK = 128
SB_BLOCK = 128
ROPE_BASE = 10000.0
N_MOD = 9
EPS = 1e-6

kernel_name = "hybrid_retention_stickbreaking_macaron_adaln"


def rmsnorm(x, g):
    xf = x.astype(jnp.float32)
    y = xf * lax.rsqrt(jnp.mean(xf * xf, axis=-1, keepdims=True) + EPS)
    return (y * g.astype(jnp.float32)).astype(x.dtype)


def modulate(h, shift, scale):
    return h * (1.0 + scale[:, None, :]) + shift[:, None, :]


def swiglu(h, w_gu, w_down):
    gate, up = jnp.split(h @ w_gu, 2, axis=-1)
    return (jax.nn.silu(gate) * up) @ w_down


def rope(x):
    S, d = x.shape[1], x.shape[3]
    half = d // 2
    inv = ROPE_BASE ** (-jnp.arange(half, dtype=jnp.float32) / half)
    ang = jnp.arange(S, dtype=jnp.float32)[:, None] * inv[None, :]
    cos = jnp.cos(ang)[None, :, None, :]
    sin = jnp.sin(ang)[None, :, None, :]
    xf = x.astype(jnp.float32)
    x1, x2 = xf[..., :half], xf[..., half:]
    out = jnp.concatenate([x1 * cos - x2 * sin, x1 * sin + x2 * cos], axis=-1)
    return out.astype(x.dtype)


def retention(q, k, v):
    B, H, S, dk = q.shape
    dv = v.shape[-1]
    C = RET_CHUNK
    n = S // C
    dt = q.dtype
    lg = jnp.log1p(-jnp.exp2(-5.0 - jnp.arange(H, dtype=jnp.float32)))
    idx = jnp.arange(C, dtype=jnp.float32)
    diff = idx[:, None] - idx[None, :]
    dmask = jnp.where(diff >= 0, jnp.exp(lg[:, None, None] * jnp.maximum(diff, 0.0)), 0.0).astype(dt)
    xi = jnp.exp(lg[:, None] * (idx[None, :] + 1.0)).astype(dt)
    zeta = jnp.exp(lg[:, None] * (C - 1.0 - idx[None, :])).astype(dt)
    chunk_decay = jnp.exp(lg * C).astype(dt)
    k = k * (dk ** -0.5)
    qc = q.reshape(B, H, n, C, dk)
    kc = k.reshape(B, H, n, C, dk)
    vc = v.reshape(B, H, n, C, dv)
    scores = jnp.einsum('bhnid,bhnjd->bhnij', qc, kc) * dmask[None, :, None]
    inner = jnp.einsum('bhnij,bhnjv->bhniv', scores, vc)
    kv = jnp.einsum('bhnjd,bhnjv->nbhdv', kc * zeta[None, :, None, :, None], vc)

    def step(state, kv_n):
        new_state = state * chunk_decay[None, :, None, None] + kv_n
        return new_state, state

    _, states = lax.scan(step, jnp.zeros((B, H, dk, dv), dtype=kv.dtype), kv)
    cross = jnp.einsum('bhnid,nbhdv->bhniv', qc, states) * xi[None, :, None, :, None]
    return (inner + cross).reshape(B, H, S, dv)


def stick_breaking(q, k, v):
    S, d = q.shape[2], q.shape[3]
    scale = d ** -0.5
    outs = []
    for blk in range(S // SB_BLOCK):
        start = blk * SB_BLOCK
        end = start + SB_BLOCK
        qb = q[:, :, start:end]
        kb = k[:, :, :end]
        vb = v[:, :, :end]
        z = jnp.einsum('bhqd,bhkd->bhqk', qb, kb).astype(jnp.float32) * scale
        t_idx = start + jnp.arange(SB_BLOCK)[:, None]
        s_idx = jnp.arange(end)[None, :]
        mask = s_idx < t_idx
        log1m = jnp.where(mask, jax.nn.log_sigmoid(-z), 0.0)
        between = lax.cumsum(log1m, axis=3, reverse=True) - log1m
        a = jnp.where(mask, jnp.exp(jax.nn.log_sigmoid(z) + between), 0.0)
        outs.append(jnp.einsum('bhqk,bhkd->bhqd', a.astype(v.dtype), vb))
    return jnp.concatenate(outs, axis=2)


def token_mixing(h, w_in, ret_gn_g, sb_norm_g, w_out):
    B, S, _ = h.shape
    split_points = np.cumsum([RET_WIDTH] * 4 + [SB_WIDTH] * 2).tolist()
    rq, rk, rv, rg, sq, sk, sv = jnp.split(h @ w_in, split_points, axis=-1)

    def heads(t, n_heads):
        return t.reshape(B, S, n_heads, HEAD_DIM)

    def to_bhsd(t):
        return jnp.transpose(t, (0, 2, 1, 3))

    ret = retention(to_bhsd(rope(heads(rq, RET_HEADS))), to_bhsd(rope(heads(rk, RET_HEADS))),
                    to_bhsd(heads(rv, RET_HEADS)))
    ret = jnp.transpose(ret, (0, 2, 1, 3)).astype(jnp.float32)
    mu = jnp.mean(ret, axis=-1, keepdims=True)
    var = jnp.mean(jnp.square(ret - mu), axis=-1, keepdims=True)
    ret = ((ret - mu) * lax.rsqrt(var + EPS)).reshape(B, S, RET_WIDTH)
    ret = (ret * ret_gn_g.astype(jnp.float32)).astype(h.dtype) * jax.nn.silu(rg)

    sb = stick_breaking(to_bhsd(heads(sq, SB_HEADS)), to_bhsd(heads(sk, SB_HEADS)),
                        to_bhsd(heads(sv, SB_HEADS)))
    sb = jnp.transpose(sb, (0, 2, 1, 3)).astype(jnp.float32)
    sb = sb * lax.rsqrt(jnp.mean(sb * sb, axis=-1, keepdims=True) + EPS)
    sb = (sb.reshape(B, S, SB_WIDTH) * sb_norm_g.astype(jnp.float32)).astype(h.dtype)

    return jnp.concatenate([ret, sb], axis=-1) @ w_out


def setup_inputs(seed: int = 0) -> dict:
    key = jax.random.key(seed)
    ks = jax.random.split(key, 16)
    f32 = jnp.float32
    nrm = lambda k, shape: jax.random.normal(k, shape, dtype=f32)
    return {
        "x": nrm(ks[0], (BATCH, SEQ, D_MODEL)),
        "c": nrm(ks[1], (BATCH, D_MODEL)),
        "ada_w": nrm(ks[2], (DEPTH, D_MODEL, N_MOD * D_MODEL)) * (0.3 * D_MODEL ** -0.5),
        "ada_b": nrm(ks[3], (DEPTH, N_MOD * D_MODEL)) * 0.01,
        "norm_g": 1.0 + 0.05 * nrm(ks[4], (DEPTH, 3, D_MODEL)),
        "w_in": nrm(ks[5], (DEPTH, D_MODEL, IN_WIDTH)) * D_MODEL ** -0.5,
        "ret_gn_g": 1.0 + 0.05 * nrm(ks[6], (DEPTH, RET_WIDTH)),
        "sb_norm_g": 1.0 + 0.05 * nrm(ks[7], (DEPTH, SB_WIDTH)),
        "w_out": nrm(ks[8], (DEPTH, MIX_WIDTH, D_MODEL)) * MIX_WIDTH ** -0.5,
        "ffn1_w_gu": nrm(ks[9], (DEPTH, D_MODEL, 2 * D_FF)) * D_MODEL ** -0.5,
        "ffn1_w_down": nrm(ks[10], (DEPTH, D_FF, D_MODEL)) * D_FF ** -0.5,
        "ffn2_w_gu": nrm(ks[11], (DEPTH, D_MODEL, 2 * D_FF)) * D_MODEL ** -0.5,
        "ffn2_w_down": nrm(ks[12], (DEPTH, D_FF, D_MODEL)) * D_FF ** -0.5,
        "final_g": 1.0 + 0.05 * nrm(ks[13], (D_MODEL,)),
    }


def reference(x, c, ada_w, ada_b, norm_g, w_in, ret_gn_g, sb_norm_g, w_out,
              ffn1_w_gu, ffn1_w_down, ffn2_w_gu, ffn2_w_down, final_g):
    B = x.shape[0]
    c_act = jax.nn.silu(c)
    for l in range(DEPTH):
        mod = (c_act @ ada_w[l] + ada_b[l]).reshape(B, N_MOD, D_MODEL)
        h = modulate(rmsnorm(x, norm_g[l, 0]), mod[:, 0], mod[:, 1])
        x = x + 0.5 * mod[:, 2][:, None, :] * swiglu(h, ffn1_w_gu[l], ffn1_w_down[l])
        h = modulate(rmsnorm(x, norm_g[l, 1]), mod[:, 3], mod[:, 4])
        x = x + mod[:, 5][:, None, :] * token_mixing(h, w_in[l], ret_gn_g[l], sb_norm_g[l], w_out[l])
        h = modulate(rmsnorm(x, norm_g[l, 2]), mod[:, 6], mod[:, 7])
        x = x + 0.5 * mod[:, 8][:, None, :] * swiglu(h, ffn2_w_gu[l], ffn2_w_down[l])
    return rmsnorm(x, final_g)
```

```python
import math
from contextlib import ExitStack

import numpy as np
import concourse.bass as bass
import concourse.mybir as mybir
from concourse.bass_utils import run_bass_kernel_spmd

F32 = mybir.dt.float32
BF16 = mybir.dt.bfloat16
AF = mybir.ActivationFunctionType
ALU = mybir.AluOpType

EPS = 1e-6
ROPE_BASE = 10000.0
NEG_BIG = -30000.0
N_CORES = 8
PAIRS = [[0, 1], [2, 3], [4, 5], [6, 7]]
QUADS = [[0, 1, 2, 3], [4, 5, 6, 7]]
STR4 = [[0, 4], [1, 5], [2, 6], [3, 7]]


class Cfg:
    def __init__(self, D=4096, S=4096, DEPTH=2, HR=16, HS=16, debug=False):
        self.D, self.S, self.DEPTH, self.HR, self.HS = D, S, DEPTH, HR, HS
        self.B = 4
        self.T = S // 2
        self.TT = min(512, self.T)
        self.NT = self.T // self.TT
        self.NCD = D // 128
        self.F = 2 * D
        self.NFC = self.F // 128
        self.RW, self.SW = HR * 128, HS * 128
        assert self.RW + self.SW == D
        self.INW = 4 * self.RW + 3 * self.SW
        self.NB = self.T // 128
        self.NG = self.T // 512
        self.KA = D // 8
        self.KAP = min(128, self.KA)
        self.KAC = self.KA // self.KAP
        self.NMOD = DEPTH * 9 * D
        self.debug = debug
        assert self.T % 512 == 0 and self.TT == 512

    @staticmethod
    def wpw(K, N):
        rows = K // 8
        return 1024 if (N % 1024 == 0 and rows * 1024 * 2 <= (1 << 20)) else 512


class Op:
    __slots__ = ("eng", "fn", "deps", "sig", "dma_sem", "inc", "idx")


class Tracker:
    ENGS = ("pe", "act", "dve", "pool", "sp")

    def __init__(self):
        self.ops = []
        self.last_writer = {}
        self.readers = {}
        self.last_on_eng = {}
        self.last_on_dsem = {}
        self.barrier_deps = {}
        self.bar_fn = None
        self.nbar = 0
        self.all_dsem = {}

    def add(self, eng, fn, reads=(), writes=(), dma_sem=None, inc=16, prefetch=False):
        op = Op()
        op.eng, op.fn, op.dma_sem, op.inc = eng, fn, dma_sem, inc
        op.idx = len(self.ops)
        op.sig = None
        deps = set()
        for k in reads:
            lw = self.last_writer.get(k)
            if lw is not None:
                deps.add(lw)
        for k in writes:
            lw = self.last_writer.get(k)
            if lw is not None:
                deps.add(lw)
            deps.update(self.readers.get(k, ()))
        if eng in self.barrier_deps:
            deps.update(self.barrier_deps.pop(eng))
        op.deps = deps
        for k in reads:
            self.readers.setdefault(k, []).append(op.idx)
        for k in writes:
            self.last_writer[k] = op.idx
            self.readers[k] = []
        self.ops.append(op)
        if dma_sem is not None:
            self.all_dsem[dma_sem] = op.idx
        if not prefetch:
            self.last_on_eng[eng] = op.idx
            if dma_sem is not None:
                self.last_on_dsem[dma_sem] = op.idx
        return op.idx

    def barrier(self, everything=False):
        deps = set(self.last_on_eng.values()) | set(self.last_on_dsem.values())
        if everything:
            deps |= set(self.all_dsem.values())
        if self.bar_fn is None:
            for e in self.ENGS:
                self.barrier_deps[e] = set(deps)
            return
        self.barrier_deps["sp"] = set(deps)
        self.nbar += 1
        idx = self.add("sp", self.bar_fn, (), (), dma_sem=("bar", self.nbar % 2))
        for e in self.ENGS:
            self.barrier_deps[e] = {idx}

    def emit(self, nc, es):
        ops = self.ops
        needed = set()
        for op in ops:
            for d in op.deps:
                if ops[d].eng == "pe" and op.eng == "pe" and ops[d].dma_sem is None:
                    continue
                needed.add(d)
        eng_sem = {e: es.enter_context(nc.semaphore("sem_" + e)) for e in self.ENGS}
        dsem = {}
        eng_cnt = {e: 0 for e in self.ENGS}
        dcnt = {}
        for op in ops:
            if op.idx not in needed and op.dma_sem is None:
                continue
            if op.dma_sem is not None:
                if op.dma_sem not in dsem:
                    dsem[op.dma_sem] = es.enter_context(nc.semaphore("d%d" % len(dsem)))
                    dcnt[op.dma_sem] = 0
                dcnt[op.dma_sem] += op.inc
                op.sig = (dsem[op.dma_sem], dcnt[op.dma_sem], ("d", op.dma_sem))
            else:
                eng_cnt[op.eng] += 1
                op.sig = (eng_sem[op.eng], eng_cnt[op.eng], ("e", op.eng))
        self.n_sems = len(dsem) + 5
        per_eng = {e: [] for e in self.ENGS}
        for op in ops:
            per_eng[op.eng].append(op)
        block = es.enter_context(nc.Block())

        def run(engine, lst):
            waited = {}
            for op in lst:
                w = {}
                for d in op.deps:
                    s = ops[d].sig
                    if s is None:
                        continue
                    if w.get(s[2], (None, 0))[1] < s[1]:
                        w[s[2]] = (s[0], s[1])
                for key, (sem, val) in w.items():
                    if waited.get(key, 0) >= val:
                        continue
                    waited[key] = val
                    engine.wait_ge(sem, val)
                ins = op.fn(engine)
                if op.sig is not None:
                    if op.dma_sem is not None and op.inc == 1:
                        ins.then_inc(op.sig[0])
                    elif op.dma_sem is not None:
                        ins.then_inc(op.sig[0], op.inc)
                    else:
                        ins.then_inc(op.sig[0], 1)

        @block.tensor
        def _(e):
            run(e, per_eng["pe"])

        @block.scalar
        def _(e):
            run(e, per_eng["act"])

        @block.vector
        def _(e):
            run(e, per_eng["dve"])

        @block.gpsimd
        def _(e):
            run(e, per_eng["pool"])

        @block.sync
        def _(e):
            run(e, per_eng["sp"])


def gammas(HR):
    return [1.0 - 2.0 ** (-5.0 - h) for h in range(HR)]


def build_program(cfg):
    c = cfg
    D, T, TT, NT, NCD, F, NFC = c.D, c.T, c.TT, c.NT, c.NCD, c.F, c.NFC
    HR, HS, RW, SW, INW, NB, NG = c.HR, c.HS, c.RW, c.SW, c.INW, c.NB, c.NG
    DEPTH = c.DEPTH
    nc = bass.Bass("TRN2", target_bir_lowering=False)
    tr = Tracker()
    es = ExitStack()

    def din(name, shape, dt=F32):
        return nc.dram_tensor(name, list(shape), dt, kind="ExternalInput").ap()

    def dint(name, shape, dt):
        if c.debug and name in getattr(c, "debug_out", ()):
            return nc.dram_tensor(name, list(shape), dt, kind="ExternalOutput").ap()
        return nc.dram_tensor(name, list(shape), dt).ap()

    xT = din("xT", [D, T])
    cT = din("cT", [c.KAP, c.KAC * 4])
    adaw = din("adaw", [DEPTH * c.KA, 9 * D])
    adabT_in = din("adabT", [128, c.NMOD // 128])
    sel4_in = din("sel4", [128, 4])
    flag_in = din("flag", [128, 1])
    ngT_in = din("ngT", [128, DEPTH * 3 * NCD])
    rgnT_in = din("rgnT", [128, DEPTH * HR])
    sgnT_in = din("sgnT", [128, DEPTH * HS])
    fgT_in = din("fgT", [128, NCD])
    wdims = {"gu1": (D, 2 * F), "d1": (F, D), "win": (D, INW), "wout": (D, D), "gu2": (D, 2 * F), "d2": (F, D)}
    wsh = {}
    for nm, (K, N) in wdims.items():
        pw_ = c.wpw(K, N)
        wsh[nm] = din(nm, [DEPTH * (N // pw_) * (K // 8), pw_])
    ropeq = din("ropeq", [2 * 128, T])
    ropek = din("ropek", [2 * 128, T])
    cmat_in = din("cmat", [128, 5 * 128])
    negm_in = din("negm", [128, 4 * 512])
    dmask4_in = din("dmask4", [HR * 128, 512])
    xi4_in = din("xi4", [HR * 128, 512])
    zeta_in = din("zeta", [128, HR])
    outT = nc.dram_tensor("outT", [D, T], F32, kind="ExternalOutput").ap()

    wbf = {}
    wfull = {}
    whalf = {}
    wpw = {}
    for nm, (K, N) in wdims.items():
        rows = K // 8
        wpw[nm] = 1024 if (N % 1024 == 0 and rows * 1024 * 2 <= (1 << 20)) else 512
        assert N % wpw[nm] == 0 and rows % 128 == 0
    for l in range(DEPTH):
        for nm, (K, N) in wdims.items():
            PW = wpw[nm]
            NP_ = N // PW
            wbf[(nm, l)] = dint("wb_%s%d" % (nm, l), [NP_ * (K // 8), PW], BF16)
            whalf[(nm, l)] = dint("wh_%s%d" % (nm, l), [NP_ * (K // 2), PW], BF16)
            wfull[(nm, l)] = dint("wf_%s%d" % (nm, l), [NP_ * K, PW], BF16)
    BO = [0, 1, 4, 5, 2, 3, 6, 7]

    def kperm(nm, kcp):
        rb = wdims[nm][0] // 8 // 128
        return BO[kcp // rb] * rb + kcp % rb

    def wpanel(nm, l, c0, w):
        K, N = wdims[nm]
        PW = wpw[nm]
        pn, off = c0 // PW, c0 % PW
        assert off + w <= PW
        reg = wfull[(nm, l)][pn * K:(pn + 1) * K, off:off + w]
        return reg.rearrange("(kc p) n -> p kc n", p=128)

    xs = dint("xs", [D, T], F32)
    NB4L = 9 * NCD * 4
    modp = [dint("modp%d" % l, [128, NB4L], F32) for l in range(DEPTH)]
    modg = [dint("modg%d" % l, [8 * 128, NB4L], F32) for l in range(DEPTH)]
    modh = [dint("modh%d" % l, [4 * 128, NB4L], F32) for l in range(DEPTH)]
    qr_s = dint("qr_s", [RW, T], BF16)
    kr_s = dint("kr_s", [RW, T], BF16)
    vr_s = dint("vr_s", [T, RW], BF16)
    rg_s = dint("rg_s", [RW, T], BF16)
    qs_s = dint("qs_s", [SW, T], BF16)
    kx_own = dint("kx_own", [SW, T], BF16)
    kx_pair = dint("kx_pair", [2 * SW, T], BF16)
    vx_own = dint("vx_own", [T, SW], BF16)
    vx_pair = dint("vx_pair", [2 * T, SW], BF16)
    st_own = dint("st_own", [RW, 128], F32)
    st_pair = dint("st_pair", [2 * RW, 128], F32)
    mix_s = dint("mix_s", [D, T], BF16)

    ARENA_BYTES = 204 * 1024
    arena = es.enter_context(nc.sbuf_tensor("arena", [128, ARENA_BYTES // 4], F32))
    arena_ap = arena[:]
    state = {"off": 0, "base": 0}

    def alloc(shape_free, dt, parts=128):
        n = 1
        for s in shape_free:
            n *= s
        nbytes = n * (4 if dt == F32 else 2)
        nbytes = (nbytes + 31) // 32 * 32
        off = state["off"]
        state["off"] += nbytes
        assert state["off"] <= ARENA_BYTES, ("SBUF arena overflow", state["off"])
        v = arena_ap[0:parts, off // 4:(off + nbytes) // 4]
        if dt == BF16:
            v = v.bitcast(BF16)
        v = v[:, 0:n]
        if len(shape_free) == 2:
            v = v.rearrange("p (a b) -> p a b", b=shape_free[1])
        elif len(shape_free) == 3:
            v = v.rearrange("p (a b c) -> p a b c", b=shape_free[1], c=shape_free[2])
        return v

    def phase_reset():
        state["off"] = state["base"]

    psum = [es.enter_context(nc.psum_tensor("ps%d" % i, [128, 512], F32)) for i in range(8)]
    PS = [p[:] for p in psum]

    def mm(out, lhsT, rhs, start, stop, reads, writes):
        tr.add("pe", lambda e: e.matmul(out, lhsT, rhs, start=start, stop=stop), reads, writes)

    def transp(out, in_, ident, reads, writes):
        tr.add("pe", lambda e: e.transpose(out, in_, ident), reads, writes)

    def act(out, in_, func, reads, writes, bias=None, scale=None):
        kw = {}
        if bias is not None:
            kw["bias"] = bias
        if scale is not None:
            kw["scale"] = scale
        tr.add("act", lambda e: e.activation(out, in_, func, **kw), reads, writes)

    def tt(eng, out, in0, in1, op, reads, writes):
        tr.add(eng, lambda e: e.tensor_tensor(out, in0, in1, op), reads, writes)

    def ts(eng, out, in0, s1, s2, op0, op1, reads, writes):
        if op1 is None:
            tr.add(eng, lambda e: e.tensor_scalar(out, in0, s1, None, op0), reads, writes)
        else:
            tr.add(eng, lambda e: e.tensor_scalar(out, in0, s1, s2, op0, op1), reads, writes)

    def stt(out, in0, scalar, in1, op0, op1, reads, writes):
        tr.add("dve", lambda e: e.scalar_tensor_tensor(out, in0, scalar, in1, op0, op1), reads, writes)

    def cp(eng, out, in_, reads, writes):
        tr.add(eng, lambda e: e.tensor_copy(out, in_), reads, writes)

    def dma(eng, out, in_, reads, writes, sem, prefetch=False):
        tr.add(eng, lambda e: e.dma_start(out=out, in_=in_), reads, writes, dma_sem=sem, prefetch=prefetch)

    def memset(eng, ap, val, writes):
        tr.add(eng, lambda e: e.memset(ap, val), (), writes)

    def allgather(groups, src, dst, reads, writes, name, prefetch=False):
        tr.add("pool", lambda e: e.collective_compute("AllGather", ALU.bypass, replica_groups=groups,
                                                      ins=[src.opt()], outs=[dst.opt()]),
               reads, writes, dma_sem=("cc",), inc=1, prefetch=prefetch)

    cmat = alloc([5, 128], BF16)
    PERM, NEGTRI, NEGONES, IDENT, ONES = (cmat[:, i, :] for i in range(5))
    negm = alloc([4, 512], BF16)
    modT = alloc([DEPTH * 9 * NCD], F32)
    ngT = alloc([DEPTH * 3 * NCD], F32)
    Amod = alloc([DEPTH * 3 * NCD], F32)
    Gmod = alloc([DEPTH * 3 * NCD], F32)
    rgnT = alloc([DEPTH * HR], F32)
    sgnT = alloc([DEPTH * HS], F32)
    fgT = alloc([NCD], F32)
    zeta = alloc([HR], F32)
    flag = alloc([1], F32)
    onec = alloc([1], F32)
    epsc = alloc([1], F32)
    state["base"] = state["off"]

    dma("pool", cmat.rearrange("p a b -> p (a b)"), cmat_in, (), [("cmat",)], "c0")
    dma("pool", negm.rearrange("p a b -> p (a b)"), negm_in, (), [("negm",)], "c1")
    dma("sp", ngT, ngT_in, (), [("ngT",)], "c2")
    dma("sp", rgnT, rgnT_in, (), [("rgnT",)], "c3")
    dma("sp", sgnT, sgnT_in, (), [("sgnT",)], "c4")
    dma("sp", fgT, fgT_in, (), [("fgT",)], "c5")
    dma("sp", zeta, zeta_in, (), [("zeta",)], "c6")
    dma("sp", flag, flag_in, (), [("flag",)], "c7")
    memset("dve", onec, 1.0, [("onec",)])
    memset("dve", epsc, EPS, [("epsc",)])

    def rsqrt(out, in_, mult, reads, wkey):
        act(out, in_, AF.Ln, list(reads) + [("epsc",)], [wkey], bias=epsc[:, 0:1], scale=mult)
        act(out, out, AF.Exp, [wkey], [wkey], scale=-0.5)

    def prep_weight(nm, l):
        if getattr(c, "no_prep", False):
            return
        K, N = wdims[nm]
        rows = K // 8
        PW = wpw[nm]
        NP_ = N // PW
        tot = NP_ * rows
        src = wsh[nm][l * tot:(l + 1) * tot, :]
        dst = wbf[(nm, l)]
        wh, wf = whalf[(nm, l)], wfull[(nm, l)]
        for pn in range(NP_):
            r0 = pn * rows
            for q0 in range(0, rows, 128):
                dma("pool", dst[r0 + q0:r0 + q0 + 128, :], src[r0 + q0:r0 + q0 + 128, :], (), [("wbf", nm, l, pn, q0)],
                    ("wcast", nm, l), prefetch=True)
        for pn in range(NP_):
            r0 = pn * rows
            keys = [("wbf", nm, l, p2, q0) for p2 in range(NP_) for q0 in range(0, rows, 128)]
            allgather(QUADS, dst[r0:r0 + rows, :], wh[pn * 4 * rows:(pn + 1) * 4 * rows, :], keys,
                      [("whalf", nm, l, pn)], ("wq", nm, l, pn), prefetch=True)
            for hh in range(2):
                allgather(STR4, wh[pn * 4 * rows + hh * 2 * rows:pn * 4 * rows + (hh + 1) * 2 * rows, :],
                          wf[pn * K + hh * 4 * rows:pn * K + (hh + 1) * 4 * rows, :], [("whalf", nm, l, pn)],
                          [("wfull", nm, l, pn, hh)], ("w", nm, l, pn, hh), prefetch=True)

    def wkeys(nm, l, c0, w):
        pn = c0 // wpw[nm]
        return [("wfull", nm, l, pn, 0), ("wfull", nm, l, pn, 1)]

    def phase_mod():
        phase_reset()
        KAP, KAC = c.KAP, c.KAC
        NBLK = c.NMOD // 128
        cact = alloc([KAC * 4], F32, parts=KAP)
        cactb = alloc([KAC * 4], BF16, parts=KAP)
        dma("sp", cact, cT, (), [("cact",)], "m0")
        act(cact, cact, AF.Silu, [("cact",)], [("cact",)])
        cp("dve", cactb, cact, [("cact",)], [("cactb",)])
        NPAN = 3
        pan = [alloc([KAC, 512], BF16, parts=KAP) for _ in range(NPAN)]
        stg = [alloc([512], F32) for _ in range(2)]
        acc = alloc([NBLK * 4], F32)
        rk = [alloc([NB4L], F32) for _ in range(2)]
        sel4 = alloc([4], F32)
        adabT = alloc([NBLK], F32)
        dma("sp", sel4, sel4_in, (), [("sel4",)], "m1")
        dma("sp", adabT, adabT_in, (), [("adabT",)], "m2")
        ci = 0
        NBL = 9 * NCD
        for l in range(DEPTH):
            modp_keys = []
            for n0 in range(0, 9 * D, 512):
                s = ci % NPAN
                src = adaw[l * c.KA:(l + 1) * c.KA, n0:n0 + 512].rearrange("(kc p) n -> p kc n", p=KAP)
                dma("pool", pan[s], src, (), [("apan", s)], ("apan", s))
                for q in range(4):
                    blk = n0 // 128 + q
                    gb = l * NBL + blk
                    bank = (gb // 128) % 2
                    o = (gb % 128) * 4
                    for kc in range(KAC):
                        mm(PS[bank][0:128, o:o + 4], pan[s][:, kc, q * 128:(q + 1) * 128], cactb[:, kc * 4:(kc + 1) * 4],
                           kc == 0, kc == KAC - 1, [("apan", s), ("cactb",)], [("ps", bank)])
                    if gb % 128 == 127 or blk == NBL - 1:
                        g0 = max((gb // 128) * 128, l * NBL)
                        nb = gb - g0 + 1
                        b0 = g0 - l * NBL
                        sgi = (gb // 128) % 2
                        po = (g0 % 128) * 4
                        cp("dve", stg[sgi][:, 0:nb * 4], PS[bank][:, po:po + nb * 4], [("ps", bank)], [("stg", sgi)])
                        dma("sp", modp[l][:, b0 * 4:(b0 + nb) * 4], stg[sgi][:, 0:nb * 4], [("stg", sgi)],
                            [("modp", l, b0)], ("stg", sgi))
                        modp_keys.append(("modp", l, b0))
                ci += 1
            allgather(QUADS, modp[l], modh[l], modp_keys, [("modh", l)], ("modq", l))
            for hh in range(2):
                allgather(STR4, modh[l][hh * 256:(hh + 1) * 256, :], modg[l][hh * 512:(hh + 1) * 512, :],
                          [("modh", l)], [("modg", l, hh)], ("mod", l, hh))
        for l in range(DEPTH):
            accl = acc[:, l * NB4L:(l + 1) * NB4L]
            mk = [("modg", l, 0), ("modg", l, 1)]
            dma("sp", accl, modg[l][0:128, :], mk, [("acc", l)], ("m3", l))
            for r in range(1, 8):
                ri = (l * 7 + r) % 2
                dma("sp", rk[ri][:, 0:NB4L], modg[l][r * 128:(r + 1) * 128, :], mk, [("rk", ri)], ("rk", ri))
                tt("dve", accl, accl, rk[ri][:, 0:NB4L], ALU.add, [("acc", l), ("rk", ri)], [("acc", l)])
        acc3 = acc.rearrange("p (n b) -> p n b", b=4)
        akeys = [("acc", l) for l in range(DEPTH)]
        ts("dve", modT, acc3[:, :, 0], sel4[:, 0:1], None, ALU.mult, None, akeys + [("sel4",)], [("modT",)])
        for bb in range(1, 4):
            stt(modT, acc3[:, :, bb], sel4[:, bb:bb + 1], modT, ALU.mult, ALU.add, akeys + [("sel4",), ("modT",)],
                [("modT",)])
        tt("dve", modT, modT, adabT, ALU.add, [("modT",), ("adabT",)], [("modT",)])
        for l in range(DEPTH):
            for j in range(3):
                base = (l * 9 + 3 * j) * NCD
                sc_ = modT[:, base + NCD:base + 2 * NCD]
                gt_ = modT[:, base + 2 * NCD:base + 3 * NCD]
                o = (l * 3 + j) * NCD
                stt(Amod[:, o:o + NCD], sc_, 1.0, ngT[:, o:o + NCD], ALU.add, ALU.mult,
                    [("modT",), ("ngT",)], [("Amod",)])
                ts("dve", Gmod[:, o:o + NCD], gt_, 0.5 if j != 1 else 1.0, None, ALU.mult, None,
                   [("modT",)], [("Gmod",)])

    def mod_cols(l, j):
        o = (l * 3 + j) * NCD
        base = (l * 9 + 3 * j) * NCD
        return (lambda ch: Amod[:, o + ch:o + ch + 1], lambda ch: modT[:, base + ch:base + ch + 1],
                lambda ch: Gmod[:, o + ch:o + ch + 1])

    def norm_tile(src, t0, hbuf, bufs, Acol, Bcol, out_f32=None, tagp="n"):
        xc, sq, tmp, rstd = bufs["xc"], bufs["sq"], bufs["tmp"], bufs["rstd"]
        nx = len(xc)
        NP = getattr(c, "norm_parts", 9)
        for ch in range(NCD):
            s = ch % nx
            dma("sp", xc[s], src[ch * 128:(ch + 1) * 128, t0:t0 + TT], [("xs", ch, t0)], [("xc", s)], ("xc", s))
            if NP >= 1:
                act(sq[ch % 2], xc[s], AF.Square, [("xc", s)], [("sq", ch % 2)])
            if NP >= 2:
                mm(PS[7], ONES, sq[ch % 2], ch == 0, ch == NCD - 1, [("sq", ch % 2), ("cmat",)], [("ps", 7)])
        if NP >= 3:
            rsqrt(rstd, PS[7], 1.0 / D, [("ps", 7)], ("rstd",))
        for ch in range(NCD if NP >= 4 else 0):
            s = ch % nx
            dma("sp", xc[s], src[ch * 128:(ch + 1) * 128, t0:t0 + TT], [("xs", ch, t0)], [("xc", s)], ("xc", s))
            if out_f32 is None:
                tt("dve", tmp[ch % 2], xc[s], rstd, ALU.mult, [("xc", s), ("rstd",)], [("tmp", ch % 2)])
                act(hbuf[:, ch, :], tmp[ch % 2], AF.Identity, [("tmp", ch % 2), ("Amod",), ("modT",)], [("hbuf", ch)],
                    bias=Bcol(ch), scale=Acol(ch))
            else:
                stt(tmp[ch % 2], xc[s], Acol(ch), rstd, ALU.mult, ALU.mult, [("xc", s), ("rstd",), ("fgT",)],
                    [("tmp", ch % 2)])
                dma("pool", out_f32[ch * 128:(ch + 1) * 128, t0:t0 + TT], tmp[ch % 2], [("tmp", ch % 2)],
                    [("out", ch, t0)], ("tmpst", ch % 2))

    def residual_out(ps_bank, ch, t0, Gcol, xo, oi, src, dst):
        s = oi % len(xo)
        dma("sp", xo[s], src[ch * 128:(ch + 1) * 128, t0:t0 + TT], [("xs", ch, t0)], [("xo", s)], ("xo", s))
        stt(xo[s], PS[ps_bank], Gcol(ch), xo[s], ALU.mult, ALU.add, [("ps", ps_bank), ("xo", s), ("Gmod",)],
            [("xo", s)])
        dma("pool", dst[ch * 128:(ch + 1) * 128, t0:t0 + TT], xo[s], [("xo", s)], [("xs", ch, t0)], ("xost", s))

    def phase_ffn(l, which, src):
        phase_reset()
        j = 0 if which == 1 else 2
        Acol, Bcol, Gcol = mod_cols(l, j)
        gun, dn = "gu%d" % which, "d%d" % which
        actb = alloc([NFC, TT], BF16)
        hbuf = alloc([NCD, TT], BF16)
        WSLOT = max(NCD * 2 * 256, NFC * 256)
        wsl = [alloc([WSLOT], BF16) for _ in range(2)]
        bufs = {"xc": [alloc([TT], F32) for _ in range(2)], "sq": [alloc([TT], BF16) for _ in range(2)],
                "tmp": [alloc([TT], F32) for _ in range(2)], "rstd": alloc([TT], F32)}
        sg = [alloc([TT], F32) for _ in range(2)]
        xo = [alloc([TT], F32) for _ in range(2)]
        wi = 0
        oi = 0
        pb = 0
        for t_i in range(NT):
            t0 = t_i * TT
            norm_tile(src, t0, hbuf, bufs, Acol, Bcol)
            for m in range(NFC if getattr(c, "ffn_parts", 3) >= 2 else 0):
                if m % 2 == 0:
                    s = wi % 2
                    wi += 1
                    wv = wsl[s][:, 0:NCD * 512].rearrange("p (kc g n) -> p kc g n", g=2, n=256)
                    c0 = (m // 2) * 256
                    dma("sp", wv[:, :, 0, :], wpanel(gun, l, c0, 256), wkeys(gun, l, c0, 256), [("wsl", s)],
                        ("wsl", s))
                    dma("sp", wv[:, :, 1, :], wpanel(gun, l, F + c0, 256), wkeys(gun, l, F + c0, 256),
                        [("wsl", s)], ("wsl", s))
                o = (m % 2) * 128
                bg, bu = pb % 6, (pb + 1) % 6
                pb += 2
                for kc in range(NCD):
                    mm(PS[bg], wv[:, kc, 0, o:o + 128], hbuf[:, kperm(gun, kc), :], kc == 0, kc == NCD - 1,
                       [("wsl", s)] + ([("hbuf", kk) for kk in range(NCD)] if kc == 0 else []), [("ps", bg)])
                for kc in range(NCD):
                    mm(PS[bu], wv[:, kc, 1, o:o + 128], hbuf[:, kperm(gun, kc), :], kc == 0, kc == NCD - 1, [("wsl", s)],
                       [("ps", bu)])
                act(sg[m % 2], PS[bg], AF.Silu, [("ps", bg)], [("sg", m % 2)])
                tt("dve", actb[:, m, :], sg[m % 2], PS[bu], ALU.mult, [("sg", m % 2), ("ps", bu)], [("actb", m)])
            for dc in range(NCD if getattr(c, "ffn_parts", 3) >= 3 else 0):
                if dc % 2 == 0:
                    s = wi % 2
                    wi += 1
                    wv2 = wsl[s][:, 0:NFC * 256].rearrange("p (kc n) -> p kc n", n=256)
                    c0 = (dc // 2) * 256
                    dma("sp", wv2, wpanel(dn, l, c0, 256), wkeys(dn, l, c0, 256), [("wsl", s)], ("wsl", s))
                o = (dc % 2) * 128
                bo = pb % 6
                pb += 1
                for kc in range(NFC):
                    mm(PS[bo], wv2[:, kc, o:o + 128], actb[:, kperm(dn, kc), :], kc == 0, kc == NFC - 1,
                       [("wsl", s)] + ([("actb", kk) for kk in range(NFC)] if kc == 0 else []), [("ps", bo)])
                residual_out(bo, dc, t0, Gcol, xo, oi, src, xs)
                oi += 1
        tr.barrier()

    def phase_mix_in(l):
        phase_reset()
        Acol, Bcol, _ = mod_cols(l, 1)
        hbuf = alloc([NCD, TT], BF16)
        wsl = [alloc([NCD, 512], BF16) for _ in range(2)]
        bufs = {"xc": [alloc([TT], F32) for _ in range(2)], "sq": [alloc([TT], BF16) for _ in range(2)],
                "tmp": [alloc([TT], F32) for _ in range(2)], "rstd": alloc([TT], F32)}
        rope_t = alloc([4, TT], F32)
        qsb = [alloc([TT], BF16) for _ in range(2)]
        t1 = [alloc([TT], F32) for _ in range(2)]
        t2 = [alloc([TT], F32) for _ in range(2)]
        ob = [alloc([512], BF16) for _ in range(4)]
        wi = 0
        pb = 0
        oi = 0
        DK = 128.0 ** -0.5
        for t_i in range(NT):
            t0 = t_i * TT
            norm_tile(xs, t0, hbuf, bufs, Acol, Bcol)
            dma("sp", rope_t[:, 0:2, :], ropeq[:, t0:t0 + TT].rearrange("(a p) t -> p a t", p=128), (), [("rope",)],
                ("rope",))
            dma("sp", rope_t[:, 2:4, :], ropek[:, t0:t0 + TT].rearrange("(a p) t -> p a t", p=128), (), [("rope",)],
                ("rope",))
            hreads = [("hbuf", kk) for kk in range(NCD)]
            for pn in range(INW // 512):
                s = wi % 2
                wi += 1
                col0 = pn * 512
                dma("sp", wsl[s], wpanel("win", l, col0, 512), wkeys("win", l, col0, 512), [("wsl", s)], ("wsl", s))
                seg = col0 // RW if col0 < 4 * RW else 4 + (col0 - 4 * RW) // SW
                if seg not in getattr(c, "segs", (0, 1, 2, 3, 4, 5, 6)):
                    continue
                if seg in (2, 6):
                    for tb in range(TT // 128):
                        bo = pb % 5
                        pb += 1
                        for kc in range(NCD):
                            mm(PS[bo], hbuf[:, kperm("win", kc), tb * 128:(tb + 1) * 128], wsl[s][:, kc, :], kc == 0,
                               kc == NCD - 1,
                               [("wsl", s)] + (hreads if kc == 0 else []), [("ps", bo)])
                        o_ = ob[oi % 4]
                        okey = ("ob", oi % 4)
                        if oi % 2 == 0:
                            act(o_, PS[bo], AF.Identity, [("ps", bo)], [okey])
                        else:
                            cp("dve", o_, PS[bo], [("ps", bo)], [okey])
                        r0 = t0 + tb * 128
                        if seg == 2:
                            cc0 = col0 - 2 * RW
                            dma("pool", vr_s[r0:r0 + 128, cc0:cc0 + 512], o_, [okey], [("vr_s", r0, cc0)],
                                ("obst", oi % 4))
                        else:
                            cc0 = col0 - 4 * RW - 2 * SW
                            dma("pool", vx_own[r0:r0 + 128, cc0:cc0 + 512], o_, [okey], [("vx_own",)],
                                ("obst", oi % 4))
                        oi += 1
                    continue
                for q in range(4):
                    col = col0 + q * 128
                    bo = pb % 5
                    pb += 1
                    for kc in range(NCD):
                        mm(PS[bo], wsl[s][:, kc, q * 128:(q + 1) * 128], hbuf[:, kperm("win", kc), :], kc == 0,
                           kc == NCD - 1, [("wsl", s)] + (hreads if kc == 0 else []), [("ps", bo)])
                    o_ = ob[oi % 4]
                    okey = ("ob", oi % 4)
                    if seg in (0, 1):
                        ri = 0 if seg == 0 else 2
                        qi = oi % 2
                        RM = getattr(c, "rope_mode", 0)
                        if RM != 2:
                            act(qsb[qi], PS[bo], AF.Identity, [("ps", bo)], [("qsb", qi)])
                        if RM == 0:
                            mm(PS[5 + qi], PERM, qsb[qi], True, True, [("qsb", qi), ("cmat",)], [("ps", 5 + qi)])
                        tt("dve", t1[qi], PS[bo], rope_t[:, ri, :], ALU.mult,
                           [("ps", bo), ("rope",)] + ([("qsb", qi)] if RM != 2 else []), [("t1", qi)])
                        if RM == 0:
                            tt("dve", t2[qi], PS[5 + qi], rope_t[:, ri + 1, :], ALU.mult, [("ps", 5 + qi), ("rope",)],
                               [("t2", qi)])
                        else:
                            tt("dve", t2[qi], PS[bo], rope_t[:, ri + 1, :], ALU.mult, [("ps", bo), ("rope",)],
                               [("t2", qi)])
                        tt(getattr(c, "rope_eng", "dve"), o_, t1[qi], t2[qi], ALU.add, [("t1", qi), ("t2", qi)], [okey])
                        dstt = qr_s if seg == 0 else kr_s
                        r0 = col - seg * RW
                        dma("pool", dstt[r0:r0 + 128, t0:t0 + TT], o_, [okey], [("qk_s", seg, r0, t0)],
                            ("obst", oi % 4))
                    elif seg == 3:
                        act(o_, PS[bo], AF.Silu, [("ps", bo)], [okey])
                        r0 = col - 3 * RW
                        dma("pool", rg_s[r0:r0 + 128, t0:t0 + TT], o_, [okey], [("rg_s", r0, t0)], ("obst", oi % 4))
                    elif seg == 4:
                        act(o_, PS[bo], AF.Identity, [("ps", bo)], [okey], scale=DK)
                        r0 = col - 4 * RW
                        dma("pool", qs_s[r0:r0 + 128, t0:t0 + TT], o_, [okey], [("qs_s", r0, t0)], ("obst", oi % 4))
                    else:
                        cp("dve", o_, PS[bo], [("ps", bo)], [okey])
                        r0 = col - 4 * RW - SW
                        dma("pool", kx_own[r0:r0 + 128, t0:t0 + TT], o_, [okey], [("kx_own",)], ("obst", oi % 4))
                    oi += 1
        tr.barrier()

    GAM = gammas(HR)

    def phase_ret(l, outputs):
        phase_reset()
        kT = [alloc([T], BF16) for _ in range(2)]
        vt = [alloc([NB, 128], BF16) for _ in range(2)]
        S = [alloc([128], F32) for _ in range(2)]
        Sb = [alloc([128], BF16) for _ in range(2)]
        kz = [alloc([128], BF16) for _ in range(2)]
        if outputs:
            qT = [alloc([T], BF16) for _ in range(2)]
            dm4 = [alloc([512], F32) for _ in range(2)]
            xi4 = [alloc([512], F32) for _ in range(2)]
            sc = [alloc([512], BF16) for _ in range(2)]
            qx = [alloc([512], BF16) for _ in range(2)]
            sinit = alloc([128], F32)
            o_bf = alloc([512], BF16)
            osq = alloc([512], BF16)
            mean = alloc([512], F32)
            m2 = alloc([512], F32)
            var = alloc([512], F32)
            cen = alloc([512], F32)
            yb = alloc([512], F32)
            rgt = [alloc([512], BF16) for _ in range(2)]
            outb = [alloc([512], BF16) for _ in range(2)]
        si = 0
        gi = 0
        for h in range(HR):
            hs = h % 2
            cd = GAM[h] ** 128
            dma("sp", kT[hs], kr_s[h * 128:(h + 1) * 128, :], [("qk_s_all",)], [("kT", hs)], ("kT", hs))
            dma("sp", vt[hs], vr_s[:, h * 128:(h + 1) * 128].rearrange("(n p) d -> p n d", p=128), [("vr_all",)],
                [("vt", hs)], ("vt", hs))
            if outputs:
                dma("sp", qT[hs], qr_s[h * 128:(h + 1) * 128, :], [("qk_s_all",)], [("qT", hs)], ("qT", hs))
                dma("sp", dm4[hs], dmask4_in[h * 128:(h + 1) * 128, :], (), [("dm4", hs)], ("dm4", hs))
                dma("sp", xi4[hs], xi4_in[h * 128:(h + 1) * 128, :], (), [("xi4", hs)], ("xi4", hs))
                dma("sp", sinit, st_pair[h * 128:(h + 1) * 128, :], [("st_pair",)], [("sinit",)], ("sinit",))
                ts("dve", S[si % 2], sinit, flag[:, 0:1], None, ALU.mult, None, [("sinit",), ("flag",)],
                   [("S", si % 2)])
                cp("pool", Sb[si % 2], S[si % 2], [("S", si % 2)], [("Sb", si % 2)])
            else:
                memset("dve", S[si % 2], 0.0, [("S", si % 2)])
            for gq in range(NG):
                if outputs:
                    for cc in range(4):
                        n = gq * 4 + cc
                        mm(PS[0][:, cc * 128:(cc + 1) * 128], kT[hs][:, n * 128:(n + 1) * 128],
                           qT[hs][:, n * 128:(n + 1) * 128], True, True, [("kT", hs), ("qT", hs)], [("ps", 0)])
                    g2 = gi % 2
                    tt("dve", sc[g2], PS[0], dm4[hs], ALU.mult, [("ps", 0), ("dm4", hs)], [("sc", g2)])
                    tt("pool", qx[g2], qT[hs][:, gq * 512:(gq + 1) * 512], xi4[hs], ALU.mult,
                       [("qT", hs), ("xi4", hs)], [("qx", g2)])
                for cc in range(4):
                    n = gq * 4 + cc
                    cur, nxt = si % 2, (si + 1) % 2
                    if outputs:
                        mm(PS[1][:, cc * 128:(cc + 1) * 128], vt[hs][:, n, :], sc[g2][:, cc * 128:(cc + 1) * 128],
                           True, False, [("vt", hs), ("sc", g2)], [("ps", 1)])
                        mm(PS[1][:, cc * 128:(cc + 1) * 128], Sb[cur], qx[g2][:, cc * 128:(cc + 1) * 128],
                           False, True, [("Sb", cur), ("qx", g2)], [("ps", 1)])
                    pk = PS[2 + si % 2].bitcast(BF16)[:, 0:128]
                    transp(pk, kT[hs][:, n * 128:(n + 1) * 128], IDENT, [("kT", hs), ("cmat",)], [("ps", 2 + si % 2)])
                    act(kz[si % 2], pk, AF.Identity, [("ps", 2 + si % 2), ("zeta",)], [("kz", si % 2)],
                        scale=zeta[:, h:h + 1])
                    mm(PS[4 + si % 2][:, 0:128], kz[si % 2], vt[hs][:, n, :], True, True, [("kz", si % 2), ("vt", hs)],
                       [("ps", 4 + si % 2)])
                    stt(S[nxt], S[cur], cd, PS[4 + si % 2][:, 0:128], ALU.mult, ALU.add,
                        [("S", cur), ("ps", 4 + si % 2)], [("S", nxt)])
                    if outputs:
                        cp("pool", Sb[nxt], S[nxt], [("S", nxt)], [("Sb", nxt)])
                    si += 1
                if outputs:
                    act(o_bf, PS[1], AF.Identity, [("ps", 1)], [("o_bf",)])
                    act(osq, PS[1], AF.Square, [("ps", 1)], [("osq",)])
                    mm(PS[6], ONES, o_bf, True, True, [("o_bf",), ("cmat",)], [("ps", 6)])
                    mm(PS[7], ONES, osq, True, True, [("osq",), ("cmat",)], [("ps", 7)])
                    ts("dve", mean, PS[6], 1.0 / 128, None, ALU.mult, None, [("ps", 6)], [("mean",)])
                    tt("pool", m2, mean, mean, ALU.mult, [("mean",)], [("m2",)])
                    stt(var, PS[7], 1.0 / 128, m2, ALU.mult, ALU.subtract, [("ps", 7), ("m2",)], [("var",)])
                    rsqrt(var, var, 1.0, [("var",)], ("var",))
                    tt("dve", cen, PS[1], mean, ALU.subtract, [("ps", 1), ("mean",)], [("cen",)])
                    stt(yb, cen, rgnT[:, l * HR + h:l * HR + h + 1], var, ALU.mult, ALU.mult,
                        [("cen",), ("var",), ("rgnT",)], [("yb",)])
                    dma("sp", rgt[g2], rg_s[h * 128:(h + 1) * 128, gq * 512:(gq + 1) * 512], [("rg_all",)],
                        [("rgt", g2)], ("rgt", g2))
                    tt("pool", outb[g2], yb, rgt[g2], ALU.mult, [("yb",), ("rgt", g2)], [("outb", g2)])
                    dma("pool", mix_s[h * 128:(h + 1) * 128, gq * 512:(gq + 1) * 512], outb[g2], [("outb", g2)],
                        [("mix_s", h, gq)], ("outbst", g2))
                    gi += 1
            if not outputs:
                dma("pool", st_own[h * 128:(h + 1) * 128, :], S[si % 2], [("S", si % 2)], [("st_own",)],
                    ("Sst", si % 2))
        tr.barrier()

    def phase_sb(l):
        phase_reset()
        qT = [alloc([T], BF16) for _ in range(2)]
        kTo = [alloc([T], BF16) for _ in range(2)]
        kTp = [alloc([T], BF16) for _ in range(2)]
        vo = [alloc([NB, 128], BF16) for _ in range(2)]
        vp = [alloc([NB, 128], BF16) for _ in range(2)]
        eb = [alloc([512], F32) for _ in range(2)]
        spb = [alloc([512], BF16) for _ in range(2)]
        R = [alloc([512], BF16) for _ in range(2)]
        ab = [alloc([512], BF16) for _ in range(2)]
        osq = alloc([512], BF16)
        rstd = alloc([512], F32)
        obuf = [alloc([512], BF16) for _ in range(2)]
        ui = 0
        gi = 0
        for h in range(HS):
            hs = h % 2
            dma("sp", qT[hs], qs_s[h * 128:(h + 1) * 128, :], [("qs_all",)], [("qT", hs)], ("qT", hs))
            dma("sp", kTo[hs], kx_own[h * 128:(h + 1) * 128, :], [("kx_own",)], [("kTo", hs)], ("kTo", hs))
            dma("sp", kTp[hs], kx_pair[h * 256:h * 256 + 128, :], [("kx_pair", h)], [("kTp", hs)], ("kTp", hs))
            dma("sp", vo[hs], vx_own[:, h * 128:(h + 1) * 128].rearrange("(n p) d -> p n d", p=128), [("vx_own",)],
                [("vo", hs)], ("vo", hs))
            vpk = []
            for cc_ in range(T // 256):
                dma("sp", vp[hs][:, 2 * cc_:2 * cc_ + 2, :],
                    vx_pair[cc_ * 512:cc_ * 512 + 256, h * 128:(h + 1) * 128].rearrange("(n p) d -> p n d", p=128),
                    [("vx_pair", cc_)], [("vp", hs)], ("vp", hs))
            ts("pool", vp[hs], vp[hs], flag[:, 0:1], None, ALU.mult, None, vpk + [("vp", hs), ("flag",)],
               vpk + [("vp", hs)])
            for g in range(NG):
                units = [("o", j) for j in range(4 * g + 3, -1, -1)] + [("p", j) for j in range(NB - 1, -1, -1)]
                qg = qT[hs][:, g * 512:(g + 1) * 512]
                po = 6 + gi % 2
                for u, (kind, j) in enumerate(units):
                    u2 = ui % 2
                    kt = (kTo if kind == "o" else kTp)[hs][:, j * 128:(j + 1) * 128]
                    vv = (vo if kind == "o" else vp)[hs][:, j, :]
                    kkey = ("kTo", hs) if kind == "o" else ("kTp", hs)
                    vkey = ("vo", hs) if kind == "o" else ("vp", hs)
                    diag = kind == "o" and j >= 4 * g
                    r = j - 4 * g
                    pz, pzi = u2, 2 + u2
                    mm(PS[pz], kt, qg, True, not diag, [kkey, ("qT", hs)], [("ps", pz)])
                    if diag:
                        mm(PS[pz], IDENT, negm[:, r, :], False, True, [("cmat",), ("negm",)], [("ps", pz)])
                    act(eb[u2], PS[pz], AF.Exp, [("ps", pz)], [("eb", u2)])
                    act(spb[u2], eb[u2], AF.Ln, [("eb", u2), ("onec",)], [("spb", u2)], bias=onec[:, 0:1])
                    mm(PS[pzi], NEGTRI, spb[u2], True, False, [("spb", u2), ("cmat",)], [("ps", pzi)])
                    if u > 0:
                        mm(PS[pzi], NEGONES, R[u2], False, False, [("R", u2)], [("ps", pzi)])
                    mm(PS[pzi], kt, qg, False, not diag, [kkey, ("qT", hs)], [("ps", pzi)])
                    if diag:
                        mm(PS[pzi], IDENT, negm[:, r, :], False, True, [("cmat",), ("negm",)], [("ps", pzi)])
                    if u + 1 < len(units):
                        n2 = (ui + 1) % 2
                        if u == 0:
                            cp("pool", R[n2], spb[u2], [("spb", u2)], [("R", n2)])
                        else:
                            tt("pool", R[n2], R[u2], spb[u2], ALU.add, [("R", u2), ("spb", u2)], [("R", n2)])
                    act(ab[u2], PS[pzi], AF.Exp, [("ps", pzi)], [("ab", u2)])
                    mm(PS[po], vv, ab[u2], u == 0, u == len(units) - 1, [vkey, ("ab", u2)], [("ps", po)])
                    ui += 1
                g2 = gi % 2
                act(osq, PS[po], AF.Square, [("ps", po)], [("osq",)])
                mm(PS[4], ONES, osq, True, True, [("osq",), ("cmat",)], [("ps", 4)])
                rsqrt(rstd, PS[4], 1.0 / 128, [("ps", 4)], ("rstd",))
                stt(obuf[g2], PS[po], sgnT[:, l * HS + h:l * HS + h + 1], rstd, ALU.mult, ALU.mult,
                    [("ps", po), ("rstd",), ("sgnT",)], [("obuf", g2)])
                dma("pool", mix_s[RW + h * 128:RW + (h + 1) * 128, g * 512:(g + 1) * 512], obuf[g2], [("obuf", g2)],
                    [("mix_s", "sb", h, g)], ("obufst", g2))
                gi += 1
        tr.barrier()

    def phase_mix_out(l):
        phase_reset()
        _, _, Gcol = mod_cols(l, 1)
        hbuf = alloc([NCD, TT], BF16)
        wsl = [alloc([NCD, 512], BF16) for _ in range(2)]
        xo = [alloc([TT], F32) for _ in range(3)]
        wi = 0
        oi = 0
        pb = 0
        for t_i in range(NT):
            t0 = t_i * TT
            dma("sp", hbuf, mix_s[:, t0:t0 + TT].rearrange("(kc p) t -> p kc t", p=128), [("mix_all",)], [("hbuf",)],
                ("hbuf",))
            for dc in range(NCD):
                if dc % 4 == 0:
                    s = wi % 2
                    wi += 1
                    c0 = (dc // 4) * 512
                    w_ = min(512, D - c0)
                    dma("sp", wsl[s][:, :, 0:w_], wpanel("wout", l, c0, w_), wkeys("wout", l, c0, w_), [("wsl", s)],
                        ("wsl", s))
                o = (dc % 4) * 128
                bo = pb % 6
                pb += 1
                for kc in range(NCD):
                    mm(PS[bo], wsl[s][:, kc, o:o + 128], hbuf[:, kperm("wout", kc), :], kc == 0, kc == NCD - 1,
                       [("wsl", s), ("hbuf",)], [("ps", bo)])
                residual_out(bo, dc, t0, Gcol, xo, oi, xs, xs)
                oi += 1
        tr.barrier()

    def phase_final():
        phase_reset()
        bufs = {"xc": [alloc([TT], F32) for _ in range(3)], "sq": [alloc([TT], BF16) for _ in range(2)],
                "tmp": [alloc([TT], F32) for _ in range(2)], "rstd": alloc([TT], F32)}
        for t_i in range(NT):
            norm_tile(xs, t_i * TT, None, bufs, lambda ch: fgT[:, ch:ch + 1], None, out_f32=outT)
        tr.barrier()

    order = ["gu1", "d1", "win", "wout", "gu2", "d2"]
    stop = getattr(c, "stop", 10 ** 9)
    cnt = {"n": 0}

    def go():
        cnt["n"] += 1
        return cnt["n"] <= stop

    def body():
        prep_weight("gu1", 0)
        prep_weight("d1", 0)
        if not go():
            return
        if not getattr(c, "no_mod", False):
            phase_mod()
        tr.barrier()
        prep_weight("win", 0)
        for l in range(DEPTH):
            if not go():
                return
            phase_ffn(l, 1, xT if l == 0 else xs)
            if l == 0 and not getattr(c, "skip_prep", False):
                for nm in order[3:]:
                    prep_weight(nm, 0)
                for l2 in range(1, DEPTH):
                    for nm in order:
                        prep_weight(nm, l2)
            if not go():
                return
            phase_mix_in(l)
            if not go():
                return
            phase_ret(l, False)
            for h in range(HS):
                allgather(PAIRS, kx_own[h * 128:(h + 1) * 128, :], kx_pair[h * 256:(h + 1) * 256, :], [("kx_own",)],
                          [("kx_pair", h)], ("kx", l, h))
            for cc_ in range(T // 256):
                allgather(PAIRS, vx_own[cc_ * 256:(cc_ + 1) * 256, :], vx_pair[cc_ * 512:(cc_ + 1) * 512, :],
                          [("vx_own",)], [("vx_pair", cc_)], ("vx", l, cc_))
            assert RW * 128 * 4 <= (1 << 20)
            allgather(PAIRS, st_own, st_pair, [("st_own",)], [("st_pair",)], ("st", l))
            tr.barrier()
            if not go():
                return
            phase_sb(l)
            if not go():
                return
            phase_ret(l, True)
            if not go():
                return
            phase_mix_out(l)
            if not go():
                return
            phase_ffn(l, 2, xs)
        if not go():
            return
        phase_final()

    body()
    tr.barrier(everything=True)
    tr.add("pool", lambda e: e.dma_start(out=modp[0][0:1, 0:16], in_=modp[0][1:2, 0:16]), (), [("fin",)], dma_sem=("fin",))
    tr.add("pool", lambda e: e.dma_start(out=modp[0][2:3, 0:16], in_=modp[0][3:4, 0:16]), [("fin",)], [("fin2",)],
           dma_sem=("fin2",))
    tr.emit(nc, es)
    es.close()
    return nc, tr


def host_constants(cfg, core):
    c = cfg
    T, HR = c.T, c.HR
    half = core % 2
    pos = (half * T + np.arange(T)).astype(np.float32)
    halfd = 64
    inv = (ROPE_BASE ** (-np.arange(halfd, dtype=np.float32) / halfd)).astype(np.float32)
    ang = (pos[None, :] * inv[:, None]).astype(np.float32)
    cos = np.cos(ang.astype(np.float64)).astype(np.float32)
    sin = np.sin(ang.astype(np.float64)).astype(np.float32)
    cos2 = np.concatenate([cos, cos], 0)
    sin2 = np.concatenate([-sin, sin], 0)
    dk = np.float32(128.0 ** -0.5)
    ropeq = np.concatenate([cos2, sin2], 0).astype(np.float32)
    ropek = (np.concatenate([cos2, sin2], 0) * dk).astype(np.float32)
    k = np.arange(128)[:, None]
    m = np.arange(128)[None, :]
    perm = (k == (m + 64) % 128).astype(np.float32)
    negtri = -(k >= m).astype(np.float32)
    negones = -np.ones((128, 128), np.float32)
    ident = np.eye(128, dtype=np.float32)
    ones = np.ones((128, 128), np.float32)
    cmat = np.concatenate([perm, negtri, negones, ident, ones], 1)
    s = np.arange(128)[:, None]
    t = np.arange(512)[None, :]
    negm = np.concatenate([np.where(t > r * 128 + s, 0.0, NEG_BIG).astype(np.float32) for r in range(4)], 1)
    lg = np.log1p(-np.exp2(-5.0 - np.arange(HR, dtype=np.float64)))
    idx = np.arange(128, dtype=np.float64)
    diff = idx[None, :] - idx[:, None]
    dm = np.where(diff >= 0, np.exp(lg[:, None, None] * np.maximum(diff, 0.0)[None]), 0.0)
    dmask4 = np.tile(dm, (1, 1, 4)).reshape(HR * 128, 512).astype(np.float32)
    xi = np.exp(lg[:, None] * (idx[None, :] + 1.0))
    xi4 = np.tile(np.tile(xi, (1, 4))[:, None, :], (1, 128, 1)).reshape(HR * 128, 512).astype(np.float32)
    zeta = np.exp(lg[:, None] * (127.0 - idx[None, :])).T.astype(np.float32)
    return dict(ropeq=ropeq, ropek=ropek, cmat=cmat, negm=negm, dmask4=dmask4, xi4=xi4,
                zeta=np.ascontiguousarray(zeta))


def make_in_maps(cfg, x, c, ada_w, ada_b, norm_g, w_in, ret_gn_g, sb_norm_g, w_out,
                 ffn1_w_gu, ffn1_w_down, ffn2_w_gu, ffn2_w_down, final_g):
    cf = cfg
    D, T, NCD, DEPTH, HR, HS = cf.D, cf.T, cf.NCD, cf.DEPTH, cf.HR, cf.HS
    f = lambda a: np.ascontiguousarray(np.asarray(a, dtype=np.float32))
    x, c, ada_w, ada_b, norm_g = f(x), f(c), f(ada_w), f(ada_b), f(norm_g)
    ngT = f(norm_g.reshape(DEPTH * 3 * NCD, 128).T)
    rgnT = f(np.asarray(ret_gn_g, np.float32).reshape(DEPTH * HR, 128).T)
    sgnT = f(np.asarray(sb_norm_g, np.float32).reshape(DEPTH * HS, 128).T)
    fgT = f(np.asarray(final_g, np.float32).reshape(NCD, 128).T)
    ws = {"gu1": ffn1_w_gu, "d1": ffn1_w_down, "win": w_in, "wout": w_out, "gu2": ffn2_w_gu, "d2": ffn2_w_down}
    maps = []
    for core in range(N_CORES):
        b, half = core // 2, core % 2
        m = {}
        m["xT"] = f(x[b, half * T:(half + 1) * T, :].T)
        KA, KAP, KAC = cf.KA, cf.KAP, cf.KAC
        cs = c[:, core * KA:(core + 1) * KA]
        m["cT"] = f(cs.reshape(4, KAC, KAP).transpose(2, 1, 0).reshape(KAP, KAC * 4))
        m["adaw"] = f(ada_w[:, core * KA:(core + 1) * KA, :].reshape(DEPTH * KA, 9 * D))
        m["adabT"] = f(ada_b.reshape(DEPTH * 9 * D // 128, 128).T)
        sel4 = np.zeros((128, 4), np.float32)
        sel4[:, b] = 1.0
        m["sel4"] = sel4
        m["flag"] = np.full((128, 1), float(half), np.float32)
        m["ngT"], m["rgnT"], m["sgnT"], m["fgT"] = ngT, rgnT, sgnT, fgT
        for nm, w in ws.items():
            w = np.asarray(w)
            K, N = w.shape[1], w.shape[2]
            r = K // 8
            pw_ = cf.wpw(K, N)
            sh = np.asarray(w[:, core * r:(core + 1) * r, :], np.float32).reshape(DEPTH, r, N // pw_, pw_)
            m[nm] = f(sh.transpose(0, 2, 1, 3).reshape(DEPTH * (N // pw_) * r, pw_))
        m.update(host_constants(cf, core))
        maps.append(m)
    return maps


_CACHE = {}


def run_cfg(cfg, inputs):
    key = (cfg.D, cfg.S, cfg.DEPTH, cfg.HR, cfg.HS)
    if key not in _CACHE:
        _CACHE[key] = build_program(cfg)[0]
    nc = _CACHE[key]
    maps = make_in_maps(cfg, **inputs)
    res = run_bass_kernel_spmd(nc, maps, core_ids=list(range(N_CORES)))
    out = np.empty((cfg.B, cfg.S, cfg.D), np.float32)
    for core in range(N_CORES):
        b, half = core // 2, core % 2
        out[b, half * cfg.T:(half + 1) * cfg.T, :] = res.results[core]["outT"].T
    return out, res


def kernel(**inputs):
    cfg = Cfg()
    out, _ = run_cfg(cfg, inputs)
    return out
```

```python
import math
from contextlib import ExitStack

import numpy as np
import concourse.bass as bass
import concourse.mybir as mybir
from concourse.bass_utils import run_bass_kernel_spmd

F32 = mybir.dt.float32
BF16 = mybir.dt.bfloat16
AF = mybir.ActivationFunctionType
ALU = mybir.AluOpType

EPS = 1e-6
ROPE_BASE = 10000.0
NEG_BIG = -30000.0
N_CORES = 8
PAIRS = [[0, 1], [2, 3], [4, 5], [6, 7]]
QUADS = [[0, 1, 2, 3], [4, 5, 6, 7]]
STR4 = [[0, 4], [1, 5], [2, 6], [3, 7]]


class Cfg:
    def __init__(self, D=4096, S=4096, DEPTH=2, HR=16, HS=16, debug=False):
        self.D, self.S, self.DEPTH, self.HR, self.HS = D, S, DEPTH, HR, HS
        self.B = 4
        self.T = S // 2
        self.TT = min(512, self.T)
        self.NT = self.T // self.TT
        self.NCD = D // 128
        self.F = 2 * D
        self.NFC = self.F // 128
        self.RW, self.SW = HR * 128, HS * 128
        assert self.RW + self.SW == D
        self.INW = 4 * self.RW + 3 * self.SW
        self.NB = self.T // 128
        self.NG = self.T // 512
        self.KA = D // 8
        self.KAP = min(128, self.KA)
        self.KAC = self.KA // self.KAP
        self.NMOD = DEPTH * 9 * D
        self.debug = debug
        assert self.T % 512 == 0 and self.TT == 512

    @staticmethod
    def wpw(K, N):
        rows = K // 8
        return 1024 if (N % 1024 == 0 and rows * 1024 * 2 <= (1 << 20)) else 512


class Op:
    __slots__ = ("eng", "fn", "deps", "sig", "dma_sem", "inc", "idx")


class Tracker:
    ENGS = ("pe", "act", "dve", "pool", "sp")

    def __init__(self):
        self.ops = []
        self.last_writer = {}
        self.readers = {}
        self.last_on_eng = {}
        self.last_on_dsem = {}
        self.barrier_deps = {}
        self.bar_fn = None
        self.nbar = 0
        self.all_dsem = {}

    def add(self, eng, fn, reads=(), writes=(), dma_sem=None, inc=16, prefetch=False):
        op = Op()
        op.eng, op.fn, op.dma_sem, op.inc = eng, fn, dma_sem, inc
        op.idx = len(self.ops)
        op.sig = None
        deps = set()
        for k in reads:
            lw = self.last_writer.get(k)
            if lw is not None:
                deps.add(lw)
        for k in writes:
            lw = self.last_writer.get(k)
            if lw is not None:
                deps.add(lw)
            deps.update(self.readers.get(k, ()))
        if eng in self.barrier_deps:
            deps.update(self.barrier_deps.pop(eng))
        op.deps = deps
        for k in reads:
            lst = self.readers.setdefault(k, [])
            if dma_sem is None:
                lst[:] = [r for r in lst if not (self.ops[r].eng == eng and self.ops[r].dma_sem is None)]
            lst.append(op.idx)
        for k in writes:
            self.last_writer[k] = op.idx
            self.readers[k] = []
        self.ops.append(op)
        if dma_sem is not None:
            self.all_dsem[dma_sem] = op.idx
        if not prefetch:
            self.last_on_eng[eng] = op.idx
            if dma_sem is not None:
                self.last_on_dsem[dma_sem] = op.idx
        return op.idx

    def barrier(self, everything=False):
        deps = set(self.last_on_eng.values()) | set(self.last_on_dsem.values())
        if everything:
            deps |= set(self.all_dsem.values())
        if self.bar_fn is None:
            for e in self.ENGS:
                self.barrier_deps[e] = set(deps)
            return
        self.barrier_deps["sp"] = set(deps)
        self.nbar += 1
        idx = self.add("sp", self.bar_fn, (), (), dma_sem=("bar", self.nbar % 2))
        for e in self.ENGS:
            self.barrier_deps[e] = {idx}

    def emit(self, nc, es):
        ops = self.ops
        needed = set()
        for op in ops:
            for d in op.deps:
                if ops[d].eng == "pe" and op.eng == "pe" and ops[d].dma_sem is None:
                    continue
                needed.add(d)
        eng_sem = {e: es.enter_context(nc.semaphore("sem_" + e)) for e in self.ENGS}
        dsem = {}
        eng_cnt = {e: 0 for e in self.ENGS}
        dcnt = {}
        for op in ops:
            if op.idx not in needed and op.dma_sem is None:
                continue
            if op.dma_sem is not None:
                if op.dma_sem not in dsem:
                    dsem[op.dma_sem] = es.enter_context(nc.semaphore("d%d" % len(dsem)))
                    dcnt[op.dma_sem] = 0
                dcnt[op.dma_sem] += op.inc
                op.sig = (dsem[op.dma_sem], dcnt[op.dma_sem], ("d", op.dma_sem))
            else:
                eng_cnt[op.eng] += 1
                op.sig = (eng_sem[op.eng], eng_cnt[op.eng], ("e", op.eng))
        self.n_sems = len(dsem) + 5
        per_eng = {e: [] for e in self.ENGS}
        for op in ops:
            per_eng[op.eng].append(op)
        block = es.enter_context(nc.Block())

        def run(engine, lst):
            waited = {}
            for op in lst:
                w = {}
                for d in op.deps:
                    s = ops[d].sig
                    if s is None:
                        continue
                    if ops[d].eng == "pe" and op.eng == "pe" and ops[d].dma_sem is None:
                        continue
                    if w.get(s[2], (None, 0))[1] < s[1]:
                        w[s[2]] = (s[0], s[1])
                for key, (sem, val) in w.items():
                    if waited.get(key, 0) >= val:
                        continue
                    waited[key] = val
                    engine.wait_ge(sem, val)
                ins = op.fn(engine)
                if op.sig is not None:
                    if op.dma_sem is not None and op.inc == 1:
                        ins.then_inc(op.sig[0])
                    elif op.dma_sem is not None:
                        ins.then_inc(op.sig[0], op.inc)
                    else:
                        ins.then_inc(op.sig[0], 1)

        @block.tensor
        def _(e):
            run(e, per_eng["pe"])

        @block.scalar
        def _(e):
            run(e, per_eng["act"])

        @block.vector
        def _(e):
            run(e, per_eng["dve"])

        @block.gpsimd
        def _(e):
            run(e, per_eng["pool"])

        @block.sync
        def _(e):
            run(e, per_eng["sp"])


def gammas(HR):
    return [1.0 - 2.0 ** (-5.0 - h) for h in range(HR)]


def build_program(cfg):
    c = cfg
    D, T, TT, NT, NCD, F, NFC = c.D, c.T, c.TT, c.NT, c.NCD, c.F, c.NFC
    HR, HS, RW, SW, INW, NB, NG = c.HR, c.HS, c.RW, c.SW, c.INW, c.NB, c.NG
    DEPTH = c.DEPTH
    nc = bass.Bass("TRN2", target_bir_lowering=False)
    tr = Tracker()
    es = ExitStack()

    def din(name, shape, dt=F32):
        return nc.dram_tensor(name, list(shape), dt, kind="ExternalInput").ap()

    def dint(name, shape, dt):
        if c.debug and name in getattr(c, "debug_out", ()):
            return nc.dram_tensor(name, list(shape), dt, kind="ExternalOutput").ap()
        return nc.dram_tensor(name, list(shape), dt).ap()

    xT = din("xT", [D, T])
    cT = din("cT", [c.KAP, c.KAC * 4])
    adaw = din("adaw", [DEPTH * c.KA, 9 * D])
    adabT_in = din("adabT", [128, c.NMOD // 128])
    sel4_in = din("sel4", [128, 4])
    flag_in = din("flag", [128, 1])
    ngT_in = din("ngT", [128, DEPTH * 3 * NCD])
    rgnT_in = din("rgnT", [128, DEPTH * HR])
    sgnT_in = din("sgnT", [128, DEPTH * HS])
    fgT_in = din("fgT", [128, NCD])
    wdims = {"gu1": (D, 2 * F), "d1": (F, D), "win": (D, INW), "wout": (D, D), "gu2": (D, 2 * F), "d2": (F, D)}
    wsh = {}
    for nm, (K, N) in wdims.items():
        pw_ = c.wpw(K, N)
        wsh[nm] = din(nm, [DEPTH * (N // pw_) * (K // 8), pw_])
    ropeq = din("ropeq", [2 * 128, T])
    ropek = din("ropek", [2 * 128, T])
    cmat_in = din("cmat", [128, 5 * 128])
    negm_in = din("negm", [128, 4 * 512])
    dmask4_in = din("dmask4", [HR * 128, 512])
    xi4_in = din("xi4", [HR * 128, 512])
    zeta_in = din("zeta", [128, HR])
    outT = nc.dram_tensor("outT", [D, T], F32, kind="ExternalOutput").ap()

    wbf = {}
    wfull = {}
    whalf = {}
    wpw = {}
    for nm, (K, N) in wdims.items():
        rows = K // 8
        wpw[nm] = 1024 if (N % 1024 == 0 and rows * 1024 * 2 <= (1 << 20)) else 512
        assert N % wpw[nm] == 0 and rows % 128 == 0
    for l in range(DEPTH):
        for nm, (K, N) in wdims.items():
            PW = wpw[nm]
            NP_ = N // PW
            wbf[(nm, l)] = dint("wb_%s%d" % (nm, l), [NP_ * (K // 8), PW], BF16)
            whalf[(nm, l)] = dint("wh_%s%d" % (nm, l), [NP_ * (K // 2), PW], BF16)
            wfull[(nm, l)] = dint("wf_%s%d" % (nm, l), [NP_ * K, PW], BF16)
    BO = [0, 1, 4, 5, 2, 3, 6, 7]

    def kperm(nm, kcp):
        rb = wdims[nm][0] // 8 // 128
        return BO[kcp // rb] * rb + kcp % rb

    def wpanel(nm, l, c0, w):
        K, N = wdims[nm]
        PW = wpw[nm]
        pn, off = c0 // PW, c0 % PW
        assert off + w <= PW
        reg = wfull[(nm, l)][pn * K:(pn + 1) * K, off:off + w]
        return reg.rearrange("(kc p) n -> p kc n", p=128)

    xs = dint("xs", [D, T], F32)
    NB4L = 9 * NCD * 4
    modp = [dint("modp%d" % l, [128, NB4L], F32) for l in range(DEPTH)]
    modg = [dint("modg%d" % l, [8 * 128, NB4L], F32) for l in range(DEPTH)]
    modh = [dint("modh%d" % l, [4 * 128, NB4L], F32) for l in range(DEPTH)]
    qr_s = dint("qr_s", [RW, T], BF16)
    kr_s = dint("kr_s", [RW, T], BF16)
    vr_s = dint("vr_s", [T, RW], BF16)
    rg_s = dint("rg_s", [RW, T], BF16)
    qs_s = dint("qs_s", [SW, T], BF16)
    kx_own = dint("kx_own", [SW, T], BF16)
    kx_pair = dint("kx_pair", [2 * SW, T], BF16)
    vx_own = dint("vx_own", [T, SW], BF16)
    vx_pair = dint("vx_pair", [2 * T, SW], BF16)
    st_own = dint("st_own", [RW, 128], F32)
    st_pair = dint("st_pair", [2 * RW, 128], F32)
    mix_s = dint("mix_s", [D, T], BF16)

    ARENA_BYTES = 204 * 1024
    arena = es.enter_context(nc.sbuf_tensor("arena", [128, ARENA_BYTES // 4], F32))
    arena_ap = arena[:]
    state = {"off": 0, "base": 0}

    def alloc(shape_free, dt, parts=128):
        n = 1
        for s in shape_free:
            n *= s
        nbytes = n * (4 if dt == F32 else 2)
        nbytes = (nbytes + 31) // 32 * 32
        off = state["off"]
        state["off"] += nbytes
        assert state["off"] <= ARENA_BYTES, ("SBUF arena overflow", state["off"])
        v = arena_ap[0:parts, off // 4:(off + nbytes) // 4]
        if dt == BF16:
            v = v.bitcast(BF16)
        v = v[:, 0:n]
        if len(shape_free) == 2:
            v = v.rearrange("p (a b) -> p a b", b=shape_free[1])
        elif len(shape_free) == 3:
            v = v.rearrange("p (a b c) -> p a b c", b=shape_free[1], c=shape_free[2])
        return v

    def phase_reset():
        state["off"] = state["base"]

    psum = [es.enter_context(nc.psum_tensor("ps%d" % i, [128, 512], F32)) for i in range(8)]
    PS = [p[:] for p in psum]

    def mm(out, lhsT, rhs, start, stop, reads, writes):
        tr.add("pe", lambda e: e.matmul(out, lhsT, rhs, start=start, stop=stop), reads, writes)

    def transp(out, in_, ident, reads, writes):
        tr.add("pe", lambda e: e.transpose(out, in_, ident), reads, writes)

    def act(out, in_, func, reads, writes, bias=None, scale=None):
        kw = {}
        if bias is not None:
            kw["bias"] = bias
        if scale is not None:
            kw["scale"] = scale
        tr.add("act", lambda e: e.activation(out, in_, func, **kw), reads, writes)

    def tt(eng, out, in0, in1, op, reads, writes):
        tr.add(eng, lambda e: e.tensor_tensor(out, in0, in1, op), reads, writes)

    def ts(eng, out, in0, s1, s2, op0, op1, reads, writes):
        if op1 is None:
            tr.add(eng, lambda e: e.tensor_scalar(out, in0, s1, None, op0), reads, writes)
        else:
            tr.add(eng, lambda e: e.tensor_scalar(out, in0, s1, s2, op0, op1), reads, writes)

    def stt(out, in0, scalar, in1, op0, op1, reads, writes):
        tr.add("dve", lambda e: e.scalar_tensor_tensor(out, in0, scalar, in1, op0, op1), reads, writes)

    def cp(eng, out, in_, reads, writes):
        tr.add(eng, lambda e: e.tensor_copy(out, in_), reads, writes)

    def dma(eng, out, in_, reads, writes, sem, prefetch=False):
        tr.add(eng, lambda e: e.dma_start(out=out, in_=in_), reads, writes, dma_sem=sem, prefetch=prefetch)

    def memset(eng, ap, val, writes):
        tr.add(eng, lambda e: e.memset(ap, val), (), writes)

    def allgather(groups, src, dst, reads, writes, name, prefetch=False):
        tr.add("pool", lambda e: e.collective_compute("AllGather", ALU.bypass, replica_groups=groups,
                                                      ins=[src.opt()], outs=[dst.opt()]),
               reads, writes, dma_sem=("cc",), inc=1, prefetch=prefetch)

    cmat = alloc([5, 128], BF16)
    PERM, NEGTRI, NEGONES, IDENT, ONES = (cmat[:, i, :] for i in range(5))
    negm = alloc([4, 512], BF16)
    modT = alloc([DEPTH * 9 * NCD], F32)
    ngT = alloc([DEPTH * 3 * NCD], F32)
    Amod = alloc([DEPTH * 3 * NCD], F32)
    Gmod = alloc([DEPTH * 3 * NCD], F32)
    rgnT = alloc([DEPTH * HR], F32)
    sgnT = alloc([DEPTH * HS], F32)
    fgT = alloc([NCD], F32)
    zeta = alloc([HR], F32)
    flag = alloc([1], F32)
    onec = alloc([1], F32)
    epsc = alloc([1], F32)
    state["base"] = state["off"]

    dma("pool", cmat.rearrange("p a b -> p (a b)"), cmat_in, (), [("cmat",)], "c0")
    dma("pool", negm.rearrange("p a b -> p (a b)"), negm_in, (), [("negm",)], "c1")
    dma("sp", ngT, ngT_in, (), [("ngT",)], "c2")
    dma("sp", rgnT, rgnT_in, (), [("rgnT",)], "c3")
    dma("sp", sgnT, sgnT_in, (), [("sgnT",)], "c4")
    dma("sp", fgT, fgT_in, (), [("fgT",)], "c5")
    dma("sp", zeta, zeta_in, (), [("zeta",)], "c6")
    dma("sp", flag, flag_in, (), [("flag",)], "c7")
    memset("dve", onec, 1.0, [("onec",)])
    memset("dve", epsc, EPS, [("epsc",)])

    def rsqrt(out, in_, mult, reads, wkey):
        act(out, in_, AF.Ln, list(reads) + [("epsc",)], [wkey], bias=epsc[:, 0:1], scale=mult)
        act(out, out, AF.Exp, [wkey], [wkey], scale=-0.5)

    def prep_weight(nm, l):
        if getattr(c, "no_prep", False):
            return
        K, N = wdims[nm]
        rows = K // 8
        PW = wpw[nm]
        NP_ = N // PW
        tot = NP_ * rows
        src = wsh[nm][l * tot:(l + 1) * tot, :]
        dst = wbf[(nm, l)]
        wh, wf = whalf[(nm, l)], wfull[(nm, l)]
        for pn in range(NP_):
            r0 = pn * rows
            for q0 in range(0, rows, 128):
                dma("pool", dst[r0 + q0:r0 + q0 + 128, :], src[r0 + q0:r0 + q0 + 128, :], (), [("wbf", nm, l, pn, q0)],
                    ("wcast", nm, l), prefetch=True)
        for pn in range(NP_):
            r0 = pn * rows
            keys = [("wbf", nm, l, p2, q0) for p2 in range(NP_) for q0 in range(0, rows, 128)]
            allgather(QUADS, dst[r0:r0 + rows, :], wh[pn * 4 * rows:(pn + 1) * 4 * rows, :], keys,
                      [("whalf", nm, l, pn)], ("wq", nm, l, pn), prefetch=True)
            for hh in range(2):
                allgather(STR4, wh[pn * 4 * rows + hh * 2 * rows:pn * 4 * rows + (hh + 1) * 2 * rows, :],
                          wf[pn * K + hh * 4 * rows:pn * K + (hh + 1) * 4 * rows, :], [("whalf", nm, l, pn)],
                          [("wfull", nm, l, pn, hh)], ("w", nm, l, pn, hh), prefetch=True)

    def wkeys(nm, l, c0, w):
        pn = c0 // wpw[nm]
        return [("wfull", nm, l, pn, 0), ("wfull", nm, l, pn, 1)]

    def phase_mod():
        phase_reset()
        KAP, KAC = c.KAP, c.KAC
        NBLK = c.NMOD // 128
        cact = alloc([KAC * 4], F32, parts=KAP)
        cactb = alloc([KAC * 4], BF16, parts=KAP)
        dma("sp", cact, cT, (), [("cact",)], "m0")
        act(cact, cact, AF.Silu, [("cact",)], [("cact",)])
        cp("dve", cactb, cact, [("cact",)], [("cactb",)])
        NPAN = 3
        pan = [alloc([KAC, 512], BF16, parts=KAP) for _ in range(NPAN)]
        stg = [alloc([512], F32) for _ in range(2)]
        acc = alloc([NBLK * 4], F32)
        rk = [alloc([NB4L], F32) for _ in range(2)]
        sel4 = alloc([4], F32)
        adabT = alloc([NBLK], F32)
        dma("sp", sel4, sel4_in, (), [("sel4",)], "m1")
        dma("sp", adabT, adabT_in, (), [("adabT",)], "m2")
        ci = 0
        NBL = 9 * NCD
        for l in range(DEPTH):
            modp_keys = []
            for n0 in range(0, 9 * D, 512):
                s = ci % NPAN
                src = adaw[l * c.KA:(l + 1) * c.KA, n0:n0 + 512].rearrange("(kc p) n -> p kc n", p=KAP)
                dma("pool", pan[s], src, (), [("apan", s)], ("apan", s))
                for q in range(4):
                    blk = n0 // 128 + q
                    gb = l * NBL + blk
                    bank = (gb // 128) % 2
                    o = (gb % 128) * 4
                    for kc in range(KAC):
                        mm(PS[bank][0:128, o:o + 4], pan[s][:, kc, q * 128:(q + 1) * 128], cactb[:, kc * 4:(kc + 1) * 4],
                           kc == 0, kc == KAC - 1, [("apan", s), ("cactb",)], [("ps", bank)])
                    if gb % 128 == 127 or blk == NBL - 1:
                        g0 = max((gb // 128) * 128, l * NBL)
                        nb = gb - g0 + 1
                        b0 = g0 - l * NBL
                        sgi = (gb // 128) % 2
                        po = (g0 % 128) * 4
                        cp("dve", stg[sgi][:, 0:nb * 4], PS[bank][:, po:po + nb * 4], [("ps", bank)], [("stg", sgi)])
                        dma("sp", modp[l][:, b0 * 4:(b0 + nb) * 4], stg[sgi][:, 0:nb * 4], [("stg", sgi)],
                            [("modp", l, b0)], ("stg", sgi))
                        modp_keys.append(("modp", l, b0))
                ci += 1
            allgather(QUADS, modp[l], modh[l], modp_keys, [("modh", l)], ("modq", l))
            for hh in range(2):
                allgather(STR4, modh[l][hh * 256:(hh + 1) * 256, :], modg[l][hh * 512:(hh + 1) * 512, :],
                          [("modh", l)], [("modg", l, hh)], ("mod", l, hh))
        for l in range(DEPTH):
            accl = acc[:, l * NB4L:(l + 1) * NB4L]
            mk = [("modg", l, 0), ("modg", l, 1)]
            dma("sp", accl, modg[l][0:128, :], mk, [("acc", l)], ("m3", l))
            for r in range(1, 8):
                ri = (l * 7 + r) % 2
                dma("sp", rk[ri][:, 0:NB4L], modg[l][r * 128:(r + 1) * 128, :], mk, [("rk", ri)], ("rk", ri))
                tt("dve", accl, accl, rk[ri][:, 0:NB4L], ALU.add, [("acc", l), ("rk", ri)], [("acc", l)])
        acc3 = acc.rearrange("p (n b) -> p n b", b=4)
        akeys = [("acc", l) for l in range(DEPTH)]
        ts("dve", modT, acc3[:, :, 0], sel4[:, 0:1], None, ALU.mult, None, akeys + [("sel4",)], [("modT",)])
        for bb in range(1, 4):
            stt(modT, acc3[:, :, bb], sel4[:, bb:bb + 1], modT, ALU.mult, ALU.add, akeys + [("sel4",), ("modT",)],
                [("modT",)])
        tt("dve", modT, modT, adabT, ALU.add, [("modT",), ("adabT",)], [("modT",)])
        for l in range(DEPTH):
            for j in range(3):
                base = (l * 9 + 3 * j) * NCD
                sc_ = modT[:, base + NCD:base + 2 * NCD]
                gt_ = modT[:, base + 2 * NCD:base + 3 * NCD]
                o = (l * 3 + j) * NCD
                stt(Amod[:, o:o + NCD], sc_, 1.0, ngT[:, o:o + NCD], ALU.add, ALU.mult,
                    [("modT",), ("ngT",)], [("Amod",)])
                ts("dve", Gmod[:, o:o + NCD], gt_, 0.5 if j != 1 else 1.0, None, ALU.mult, None,
                   [("modT",)], [("Gmod",)])

    def mod_cols(l, j):
        o = (l * 3 + j) * NCD
        base = (l * 9 + 3 * j) * NCD
        return (lambda ch: Amod[:, o + ch:o + ch + 1], lambda ch: modT[:, base + ch:base + ch + 1],
                lambda ch: Gmod[:, o + ch:o + ch + 1])

    def norm_tile(src, t0, hbuf, bufs, Acol, Bcol, out_f32=None, tagp="n"):
        xc, sq, tmp, rstd = bufs["xc"], bufs["sq"], bufs["tmp"], bufs["rstd"]
        nx = len(xc)
        NP = getattr(c, "norm_parts", 9)
        for ch in range(NCD):
            s = ch % nx
            dma("sp", xc[s], src[ch * 128:(ch + 1) * 128, t0:t0 + TT], [("xs", ch, t0)], [("xc", s)], ("xc", s))
            if NP >= 1:
                act(sq[ch % 2], xc[s], AF.Square, [("xc", s)], [("sq", ch % 2)])
            if NP >= 2:
                mm(PS[7], ONES, sq[ch % 2], ch == 0, ch == NCD - 1, [("sq", ch % 2), ("cmat",)], [("ps", 7)])
        if NP >= 3:
            rsqrt(rstd, PS[7], 1.0 / D, [("ps", 7)], ("rstd",))
        for ch in range(NCD if NP >= 4 else 0):
            s = ch % nx
            dma("sp", xc[s], src[ch * 128:(ch + 1) * 128, t0:t0 + TT], [("xs", ch, t0)], [("xc", s)], ("xc", s))
            if out_f32 is None:
                tt("dve", tmp[ch % 2], xc[s], rstd, ALU.mult, [("xc", s), ("rstd",)], [("tmp", ch % 2)])
                act(hbuf[:, ch, :], tmp[ch % 2], AF.Identity, [("tmp", ch % 2), ("Amod",), ("modT",)], [("hbuf", ch)],
                    bias=Bcol(ch), scale=Acol(ch))
            else:
                stt(tmp[ch % 2], xc[s], Acol(ch), rstd, ALU.mult, ALU.mult, [("xc", s), ("rstd",), ("fgT",)],
                    [("tmp", ch % 2)])
                dma("pool", out_f32[ch * 128:(ch + 1) * 128, t0:t0 + TT], tmp[ch % 2], [("tmp", ch % 2)],
                    [("out", ch, t0)], ("tmpst", ch % 2))

    def residual_out(ps_bank, ch, t0, Gcol, xo, oi, src, dst):
        s = oi % len(xo)
        dma("sp", xo[s], src[ch * 128:(ch + 1) * 128, t0:t0 + TT], [("xs", ch, t0)], [("xo", s)], ("xo", s))
        stt(xo[s], PS[ps_bank], Gcol(ch), xo[s], ALU.mult, ALU.add, [("ps", ps_bank), ("xo", s), ("Gmod",)],
            [("xo", s)])
        dma("act", dst[ch * 128:(ch + 1) * 128, t0:t0 + TT], xo[s], [("xo", s)], [("xs", ch, t0)], ("xost", s))

    def phase_ffn(l, which, src):
        phase_reset()
        j = 0 if which == 1 else 2
        Acol, Bcol, Gcol = mod_cols(l, j)
        gun, dn = "gu%d" % which, "d%d" % which
        actb = alloc([NFC, TT], BF16)
        hbuf = alloc([NCD, TT], BF16)
        WSLOT = max(NCD * 2 * 256, NFC * 256)
        wsl = [alloc([WSLOT], BF16) for _ in range(2)]
        bufs = {"xc": [alloc([TT], F32) for _ in range(2)], "sq": [alloc([TT], BF16) for _ in range(2)],
                "tmp": [alloc([TT], F32) for _ in range(2)], "rstd": alloc([TT], F32)}
        sg = [alloc([TT], F32) for _ in range(2)]
        xo = [alloc([TT], F32) for _ in range(2)]
        wi = 0
        oi = 0
        pb = 0
        for t_i in range(NT):
            t0 = t_i * TT
            norm_tile(src, t0, hbuf, bufs, Acol, Bcol)
            for m in range(NFC if getattr(c, "ffn_parts", 3) >= 2 else 0):
                if m % 2 == 0:
                    s = wi % 2
                    wi += 1
                    wv = wsl[s][:, 0:NCD * 512].rearrange("p (kc g n) -> p kc g n", g=2, n=256)
                    c0 = (m // 2) * 256
                    dma("sp", wv[:, :, 0, :], wpanel(gun, l, c0, 256), wkeys(gun, l, c0, 256), [("wsl", s)],
                        ("wsl", s))
                    dma("sp", wv[:, :, 1, :], wpanel(gun, l, F + c0, 256), wkeys(gun, l, F + c0, 256),
                        [("wsl", s)], ("wsl", s))
                o = (m % 2) * 128
                bg, bu = pb % 6, (pb + 1) % 6
                pb += 2
                for kc in range(NCD):
                    mm(PS[bg], wv[:, kc, 0, o:o + 128], hbuf[:, kperm(gun, kc), :], kc == 0, kc == NCD - 1,
                       [("wsl", s)] + ([("hbuf", kk) for kk in range(NCD)] if kc == 0 else []), [("ps", bg)])
                for kc in range(NCD):
                    mm(PS[bu], wv[:, kc, 1, o:o + 128], hbuf[:, kperm(gun, kc), :], kc == 0, kc == NCD - 1, [("wsl", s)],
                       [("ps", bu)])
                act(sg[m % 2], PS[bg], AF.Silu, [("ps", bg)], [("sg", m % 2)])
                tt("dve", actb[:, m, :], sg[m % 2], PS[bu], ALU.mult, [("sg", m % 2), ("ps", bu)], [("actb", m)])
            for dc in range(NCD if getattr(c, "ffn_parts", 3) >= 3 else 0):
                if dc % 2 == 0:
                    s = wi % 2
                    wi += 1
                    wv2 = wsl[s][:, 0:NFC * 256].rearrange("p (kc n) -> p kc n", n=256)
                    c0 = (dc // 2) * 256
                    dma("sp", wv2, wpanel(dn, l, c0, 256), wkeys(dn, l, c0, 256), [("wsl", s)], ("wsl", s))
                o = (dc % 2) * 128
                bo = pb % 6
                pb += 1
                for kc in range(NFC):
                    mm(PS[bo], wv2[:, kc, o:o + 128], actb[:, kperm(dn, kc), :], kc == 0, kc == NFC - 1,
                       [("wsl", s)] + ([("actb", kk) for kk in range(NFC)] if kc == 0 else []), [("ps", bo)])
                residual_out(bo, dc, t0, Gcol, xo, oi, src, xs)
                oi += 1
        tr.barrier()

    def phase_mix_in(l):
        phase_reset()
        Acol, Bcol, _ = mod_cols(l, 1)
        hbuf = alloc([NCD, TT], BF16)
        wsl = [alloc([NCD, 512], BF16) for _ in range(2)]
        bufs = {"xc": [alloc([TT], F32) for _ in range(2)], "sq": [alloc([TT], BF16) for _ in range(2)],
                "tmp": [alloc([TT], F32) for _ in range(2)], "rstd": alloc([TT], F32)}
        rope_t = alloc([4, TT], F32)
        qsb = [alloc([TT], BF16) for _ in range(2)]
        t1 = [alloc([TT], F32) for _ in range(2)]
        t2 = [alloc([TT], F32) for _ in range(2)]
        ob = [alloc([512], BF16) for _ in range(4)]
        wi = 0
        pb = 0
        oi = 0
        DK = 128.0 ** -0.5
        for t_i in range(NT):
            t0 = t_i * TT
            norm_tile(xs, t0, hbuf, bufs, Acol, Bcol)
            dma("sp", rope_t[:, 0:2, :], ropeq[:, t0:t0 + TT].rearrange("(a p) t -> p a t", p=128), (), [("rope",)],
                ("rope",))
            dma("sp", rope_t[:, 2:4, :], ropek[:, t0:t0 + TT].rearrange("(a p) t -> p a t", p=128), (), [("rope",)],
                ("rope",))
            hreads = [("hbuf", kk) for kk in range(NCD)]
            for pn in range(INW // 512):
                s = wi % 2
                wi += 1
                col0 = pn * 512
                dma("sp", wsl[s], wpanel("win", l, col0, 512), wkeys("win", l, col0, 512), [("wsl", s)], ("wsl", s))
                seg = col0 // RW if col0 < 4 * RW else 4 + (col0 - 4 * RW) // SW
                if seg not in getattr(c, "segs", (0, 1, 2, 3, 4, 5, 6)):
                    continue
                if seg in (2, 6):
                    for tb in range(TT // 128):
                        bo = pb % 5
                        pb += 1
                        for kc in range(NCD):
                            mm(PS[bo], hbuf[:, kperm("win", kc), tb * 128:(tb + 1) * 128], wsl[s][:, kc, :], kc == 0,
                               kc == NCD - 1,
                               [("wsl", s)] + (hreads if kc == 0 else []), [("ps", bo)])
                        o_ = ob[oi % 4]
                        okey = ("ob", oi % 4)
                        if oi % 2 == 0:
                            act(o_, PS[bo], AF.Identity, [("ps", bo)], [okey])
                        else:
                            cp("dve", o_, PS[bo], [("ps", bo)], [okey])
                        r0 = t0 + tb * 128
                        if seg == 2:
                            cc0 = col0 - 2 * RW
                            dma("act", vr_s[r0:r0 + 128, cc0:cc0 + 512], o_, [okey], [("vr_s", r0, cc0)],
                                ("obst", oi % 4))
                        else:
                            cc0 = col0 - 4 * RW - 2 * SW
                            dma("act", vx_own[r0:r0 + 128, cc0:cc0 + 512], o_, [okey], [("vx_own",)],
                                ("obst", oi % 4))
                        oi += 1
                    continue
                for q in range(4):
                    col = col0 + q * 128
                    bo = pb % 5
                    pb += 1
                    for kc in range(NCD):
                        mm(PS[bo], wsl[s][:, kc, q * 128:(q + 1) * 128], hbuf[:, kperm("win", kc), :], kc == 0,
                           kc == NCD - 1, [("wsl", s)] + (hreads if kc == 0 else []), [("ps", bo)])
                    o_ = ob[oi % 4]
                    okey = ("ob", oi % 4)
                    if seg in (0, 1):
                        ri = 0 if seg == 0 else 2
                        qi = oi % 2
                        RM = getattr(c, "rope_mode", 0)
                        if RM != 2:
                            act(qsb[qi], PS[bo], AF.Identity, [("ps", bo)], [("qsb", qi)])
                        if RM == 0:
                            mm(PS[5 + qi], PERM, qsb[qi], True, True, [("qsb", qi), ("cmat",)], [("ps", 5 + qi)])
                        tt("dve", t1[qi], PS[bo], rope_t[:, ri, :], ALU.mult,
                           [("ps", bo), ("rope",)] + ([("qsb", qi)] if RM != 2 else []), [("t1", qi)])
                        if RM == 0:
                            tt("dve", t2[qi], PS[5 + qi], rope_t[:, ri + 1, :], ALU.mult, [("ps", 5 + qi), ("rope",)],
                               [("t2", qi)])
                        else:
                            tt("dve", t2[qi], PS[bo], rope_t[:, ri + 1, :], ALU.mult, [("ps", bo), ("rope",)],
                               [("t2", qi)])
                        tt(getattr(c, "rope_eng", "dve"), o_, t1[qi], t2[qi], ALU.add, [("t1", qi), ("t2", qi)], [okey])
                        dstt = qr_s if seg == 0 else kr_s
                        r0 = col - seg * RW
                        dma("act", dstt[r0:r0 + 128, t0:t0 + TT], o_, [okey], [("qk_s", seg, r0, t0)],
                            ("obst", oi % 4))
                    elif seg == 3:
                        act(o_, PS[bo], AF.Silu, [("ps", bo)], [okey])
                        r0 = col - 3 * RW
                        dma("act", rg_s[r0:r0 + 128, t0:t0 + TT], o_, [okey], [("rg_s", r0, t0)], ("obst", oi % 4))
                    elif seg == 4:
                        act(o_, PS[bo], AF.Identity, [("ps", bo)], [okey], scale=DK)
                        r0 = col - 4 * RW
                        dma("act", qs_s[r0:r0 + 128, t0:t0 + TT], o_, [okey], [("qs_s", r0, t0)], ("obst", oi % 4))
                    else:
                        cp("dve", o_, PS[bo], [("ps", bo)], [okey])
                        r0 = col - 4 * RW - SW
                        dma("act", kx_own[r0:r0 + 128, t0:t0 + TT], o_, [okey], [("kx_own",)], ("obst", oi % 4))
                    oi += 1
        tr.barrier()

    GAM = gammas(HR)

    def phase_ret(l, outputs):
        phase_reset()
        kT = [alloc([T], BF16) for _ in range(2)]
        vt = [alloc([NB, 128], BF16) for _ in range(2)]
        S = [alloc([128], F32) for _ in range(2)]
        Sb = [alloc([128], BF16) for _ in range(2)]
        kz = [alloc([128], BF16) for _ in range(2)]
        if outputs:
            qT = [alloc([T], BF16) for _ in range(2)]
            dm4 = [alloc([512], F32) for _ in range(2)]
            xi4 = [alloc([512], F32) for _ in range(2)]
            sc = [alloc([512], BF16) for _ in range(2)]
            qx = [alloc([512], BF16) for _ in range(2)]
            sinit = alloc([128], F32)
            o_bf = alloc([512], BF16)
            osq = alloc([512], BF16)
            mean = alloc([512], F32)
            m2 = alloc([512], F32)
            var = alloc([512], F32)
            cen = alloc([512], F32)
            yb = alloc([512], F32)
            rgt = [alloc([512], BF16) for _ in range(2)]
            outb = [alloc([512], BF16) for _ in range(2)]
        si = 0
        gi = 0
        for h in range(HR):
            hs = h % 2
            cd = GAM[h] ** 128
            dma("sp", kT[hs], kr_s[h * 128:(h + 1) * 128, :], [("qk_s_all",)], [("kT", hs)], ("kT", hs))
            dma("sp", vt[hs], vr_s[:, h * 128:(h + 1) * 128].rearrange("(n p) d -> p n d", p=128), [("vr_all",)],
                [("vt", hs)], ("vt", hs))
            if outputs:
                dma("sp", qT[hs], qr_s[h * 128:(h + 1) * 128, :], [("qk_s_all",)], [("qT", hs)], ("qT", hs))
                dma("sp", dm4[hs], dmask4_in[h * 128:(h + 1) * 128, :], (), [("dm4", hs)], ("dm4", hs))
                dma("sp", xi4[hs], xi4_in[h * 128:(h + 1) * 128, :], (), [("xi4", hs)], ("xi4", hs))
                dma("sp", sinit, st_pair[h * 128:(h + 1) * 128, :], [("st_pair",)], [("sinit",)], ("sinit",))
                ts("dve", S[si % 2], sinit, flag[:, 0:1], None, ALU.mult, None, [("sinit",), ("flag",)],
                   [("S", si % 2)])
                cp("pool", Sb[si % 2], S[si % 2], [("S", si % 2)], [("Sb", si % 2)])
            else:
                memset("dve", S[si % 2], 0.0, [("S", si % 2)])
            for gq in range(NG):
                if outputs:
                    for cc in range(4):
                        n = gq * 4 + cc
                        mm(PS[0][:, cc * 128:(cc + 1) * 128], kT[hs][:, n * 128:(n + 1) * 128],
                           qT[hs][:, n * 128:(n + 1) * 128], True, True, [("kT", hs), ("qT", hs)], [("ps", 0)])
                    g2 = gi % 2
                    tt("dve", sc[g2], PS[0], dm4[hs], ALU.mult, [("ps", 0), ("dm4", hs)], [("sc", g2)])
                    tt("pool", qx[g2], qT[hs][:, gq * 512:(gq + 1) * 512], xi4[hs], ALU.mult,
                       [("qT", hs), ("xi4", hs)], [("qx", g2)])
                for cc in range(4):
                    n = gq * 4 + cc
                    cur, nxt = si % 2, (si + 1) % 2
                    if outputs:
                        mm(PS[1][:, cc * 128:(cc + 1) * 128], vt[hs][:, n, :], sc[g2][:, cc * 128:(cc + 1) * 128],
                           True, False, [("vt", hs), ("sc", g2)], [("ps", 1)])
                        mm(PS[1][:, cc * 128:(cc + 1) * 128], Sb[cur], qx[g2][:, cc * 128:(cc + 1) * 128],
                           False, True, [("Sb", cur), ("qx", g2)], [("ps", 1)])
                    pk = PS[2 + si % 2].bitcast(BF16)[:, 0:128]
                    transp(pk, kT[hs][:, n * 128:(n + 1) * 128], IDENT, [("kT", hs), ("cmat",)], [("ps", 2 + si % 2)])
                    act(kz[si % 2], pk, AF.Identity, [("ps", 2 + si % 2), ("zeta",)], [("kz", si % 2)],
                        scale=zeta[:, h:h + 1])
                    mm(PS[4 + si % 2][:, 0:128], kz[si % 2], vt[hs][:, n, :], True, True, [("kz", si % 2), ("vt", hs)],
                       [("ps", 4 + si % 2)])
                    stt(S[nxt], S[cur], cd, PS[4 + si % 2][:, 0:128], ALU.mult, ALU.add,
                        [("S", cur), ("ps", 4 + si % 2)], [("S", nxt)])
                    if outputs:
                        cp("pool", Sb[nxt], S[nxt], [("S", nxt)], [("Sb", nxt)])
                    si += 1
                if outputs:
                    act(o_bf, PS[1], AF.Identity, [("ps", 1)], [("o_bf",)])
                    act(osq, PS[1], AF.Square, [("ps", 1)], [("osq",)])
                    mm(PS[6], ONES, o_bf, True, True, [("o_bf",), ("cmat",)], [("ps", 6)])
                    mm(PS[7], ONES, osq, True, True, [("osq",), ("cmat",)], [("ps", 7)])
                    ts("dve", mean, PS[6], 1.0 / 128, None, ALU.mult, None, [("ps", 6)], [("mean",)])
                    tt("pool", m2, mean, mean, ALU.mult, [("mean",)], [("m2",)])
                    stt(var, PS[7], 1.0 / 128, m2, ALU.mult, ALU.subtract, [("ps", 7), ("m2",)], [("var",)])
                    rsqrt(var, var, 1.0, [("var",)], ("var",))
                    tt("dve", cen, PS[1], mean, ALU.subtract, [("ps", 1), ("mean",)], [("cen",)])
                    stt(yb, cen, rgnT[:, l * HR + h:l * HR + h + 1], var, ALU.mult, ALU.mult,
                        [("cen",), ("var",), ("rgnT",)], [("yb",)])
                    dma("sp", rgt[g2], rg_s[h * 128:(h + 1) * 128, gq * 512:(gq + 1) * 512], [("rg_all",)],
                        [("rgt", g2)], ("rgt", g2))
                    tt("pool", outb[g2], yb, rgt[g2], ALU.mult, [("yb",), ("rgt", g2)], [("outb", g2)])
                    dma("pool", mix_s[h * 128:(h + 1) * 128, gq * 512:(gq + 1) * 512], outb[g2], [("outb", g2)],
                        [("mix_s", h, gq)], ("outbst", g2))
                    gi += 1
            if not outputs:
                dma("pool", st_own[h * 128:(h + 1) * 128, :], S[si % 2], [("S", si % 2)], [("st_own",)],
                    ("Sst", si % 2))
        tr.barrier()

    def phase_sb(l):
        phase_reset()
        qT = [alloc([T], BF16) for _ in range(2)]
        kTo = [alloc([T], BF16) for _ in range(2)]
        kTp = [alloc([T], BF16) for _ in range(2)]
        vo = [alloc([NB, 128], BF16) for _ in range(2)]
        vp = [alloc([NB, 128], BF16) for _ in range(2)]
        eb = [alloc([512], F32) for _ in range(2)]
        spb = [alloc([512], BF16) for _ in range(2)]
        R = [alloc([512], BF16) for _ in range(2)]
        ab = [alloc([512], BF16) for _ in range(2)]
        osq = alloc([512], BF16)
        rstd = alloc([512], F32)
        obuf = [alloc([512], BF16) for _ in range(2)]
        ui = 0
        gi = 0
        for h in range(HS):
            hs = h % 2
            dma("sp", qT[hs], qs_s[h * 128:(h + 1) * 128, :], [("qs_all",)], [("qT", hs)], ("qT", hs))
            dma("sp", kTo[hs], kx_own[h * 128:(h + 1) * 128, :], [("kx_own",)], [("kTo", hs)], ("kTo", hs))
            dma("sp", kTp[hs], kx_pair[h * 256:h * 256 + 128, :], [("kx_pair", h)], [("kTp", hs)], ("kTp", hs))
            dma("sp", vo[hs], vx_own[:, h * 128:(h + 1) * 128].rearrange("(n p) d -> p n d", p=128), [("vx_own",)],
                [("vo", hs)], ("vo", hs))
            vpk = []
            for cc_ in range(T // 256):
                dma("sp", vp[hs][:, 2 * cc_:2 * cc_ + 2, :],
                    vx_pair[cc_ * 512:cc_ * 512 + 256, h * 128:(h + 1) * 128].rearrange("(n p) d -> p n d", p=128),
                    [("vx_pair", cc_)], [("vp", hs)], ("vp", hs))
            ts("pool", vp[hs], vp[hs], flag[:, 0:1], None, ALU.mult, None, vpk + [("vp", hs), ("flag",)],
               vpk + [("vp", hs)])
            for g in range(NG):
                units = [("o", j) for j in range(4 * g + 3, -1, -1)] + [("p", j) for j in range(NB - 1, -1, -1)]
                qg = qT[hs][:, g * 512:(g + 1) * 512]
                po = 6 + gi % 2
                for u, (kind, j) in enumerate(units):
                    u2 = ui % 2
                    kt = (kTo if kind == "o" else kTp)[hs][:, j * 128:(j + 1) * 128]
                    vv = (vo if kind == "o" else vp)[hs][:, j, :]
                    kkey = ("kTo", hs) if kind == "o" else ("kTp", hs)
                    vkey = ("vo", hs) if kind == "o" else ("vp", hs)
                    diag = kind == "o" and j >= 4 * g
                    r = j - 4 * g
                    pz, pzi = u2, 2 + u2
                    mm(PS[pz], kt, qg, True, not diag, [kkey, ("qT", hs)], [("ps", pz)])
                    if diag:
                        mm(PS[pz], IDENT, negm[:, r, :], False, True, [("cmat",), ("negm",)], [("ps", pz)])
                    act(eb[u2], PS[pz], AF.Exp, [("ps", pz)], [("eb", u2)])
                    act(spb[u2], eb[u2], AF.Ln, [("eb", u2), ("onec",)], [("spb", u2)], bias=onec[:, 0:1])
                    mm(PS[pzi], NEGTRI, spb[u2], True, False, [("spb", u2), ("cmat",)], [("ps", pzi)])
                    if u > 0:
                        mm(PS[pzi], NEGONES, R[u2], False, False, [("R", u2)], [("ps", pzi)])
                    mm(PS[pzi], kt, qg, False, not diag, [kkey, ("qT", hs)], [("ps", pzi)])
                    if diag:
                        mm(PS[pzi], IDENT, negm[:, r, :], False, True, [("cmat",), ("negm",)], [("ps", pzi)])
                    if u + 1 < len(units):
                        n2 = (ui + 1) % 2
                        if u == 0:
                            cp("pool", R[n2], spb[u2], [("spb", u2)], [("R", n2)])
                        else:
                            tt("pool", R[n2], R[u2], spb[u2], ALU.add, [("R", u2), ("spb", u2)], [("R", n2)])
                    act(ab[u2], PS[pzi], AF.Exp, [("ps", pzi)], [("ab", u2)])
                    mm(PS[po], vv, ab[u2], u == 0, u == len(units) - 1, [vkey, ("ab", u2)], [("ps", po)])
                    ui += 1
                g2 = gi % 2
                act(osq, PS[po], AF.Square, [("ps", po)], [("osq",)])
                mm(PS[4], ONES, osq, True, True, [("osq",), ("cmat",)], [("ps", 4)])
                rsqrt(rstd, PS[4], 1.0 / 128, [("ps", 4)], ("rstd",))
                stt(obuf[g2], PS[po], sgnT[:, l * HS + h:l * HS + h + 1], rstd, ALU.mult, ALU.mult,
                    [("ps", po), ("rstd",), ("sgnT",)], [("obuf", g2)])
                dma("pool", mix_s[RW + h * 128:RW + (h + 1) * 128, g * 512:(g + 1) * 512], obuf[g2], [("obuf", g2)],
                    [("mix_s", "sb", h, g)], ("obufst", g2))
                gi += 1
        tr.barrier()

    def phase_mix_out(l):
        phase_reset()
        _, _, Gcol = mod_cols(l, 1)
        hbuf = alloc([NCD, TT], BF16)
        wsl = [alloc([NCD, 512], BF16) for _ in range(2)]
        xo = [alloc([TT], F32) for _ in range(3)]
        wi = 0
        oi = 0
        pb = 0
        for t_i in range(NT):
            t0 = t_i * TT
            dma("sp", hbuf, mix_s[:, t0:t0 + TT].rearrange("(kc p) t -> p kc t", p=128), [("mix_all",)], [("hbuf",)],
                ("hbuf",))
            for dc in range(NCD):
                if dc % 4 == 0:
                    s = wi % 2
                    wi += 1
                    c0 = (dc // 4) * 512
                    w_ = min(512, D - c0)
                    dma("sp", wsl[s][:, :, 0:w_], wpanel("wout", l, c0, w_), wkeys("wout", l, c0, w_), [("wsl", s)],
                        ("wsl", s))
                o = (dc % 4) * 128
                bo = pb % 6
                pb += 1
                for kc in range(NCD):
                    mm(PS[bo], wsl[s][:, kc, o:o + 128], hbuf[:, kperm("wout", kc), :], kc == 0, kc == NCD - 1,
                       [("wsl", s), ("hbuf",)], [("ps", bo)])
                residual_out(bo, dc, t0, Gcol, xo, oi, xs, xs)
                oi += 1
        tr.barrier()

    def phase_final():
        phase_reset()
        bufs = {"xc": [alloc([TT], F32) for _ in range(3)], "sq": [alloc([TT], BF16) for _ in range(2)],
                "tmp": [alloc([TT], F32) for _ in range(2)], "rstd": alloc([TT], F32)}
        for t_i in range(NT):
            norm_tile(xs, t_i * TT, None, bufs, lambda ch: fgT[:, ch:ch + 1], None, out_f32=outT)
        tr.barrier()

    order = ["gu1", "d1", "win", "wout", "gu2", "d2"]
    stop = getattr(c, "stop", 10 ** 9)
    cnt = {"n": 0}

    def go():
        cnt["n"] += 1
        return cnt["n"] <= stop

    def body():
        prep_weight("gu1", 0)
        prep_weight("d1", 0)
        if not go():
            return
        if not getattr(c, "no_mod", False):
            phase_mod()
        tr.barrier()
        prep_weight("win", 0)
        for l in range(DEPTH):
            if not go():
                return
            phase_ffn(l, 1, xT if l == 0 else xs)
            if l == 0 and not getattr(c, "skip_prep", False):
                for nm in order[3:]:
                    prep_weight(nm, 0)
                for l2 in range(1, DEPTH):
                    for nm in order:
                        prep_weight(nm, l2)
            if not go():
                return
            phase_mix_in(l)
            if not go():
                return
            phase_ret(l, False)
            for h in range(HS):
                allgather(PAIRS, kx_own[h * 128:(h + 1) * 128, :], kx_pair[h * 256:(h + 1) * 256, :], [("kx_own",)],
                          [("kx_pair", h)], ("kx", l, h))
            for cc_ in range(T // 256):
                allgather(PAIRS, vx_own[cc_ * 256:(cc_ + 1) * 256, :], vx_pair[cc_ * 512:(cc_ + 1) * 512, :],
                          [("vx_own",)], [("vx_pair", cc_)], ("vx", l, cc_))
            assert RW * 128 * 4 <= (1 << 20)
            allgather(PAIRS, st_own, st_pair, [("st_own",)], [("st_pair",)], ("st", l))
            tr.barrier()
            if not go():
                return
            phase_sb(l)
            if not go():
                return
            phase_ret(l, True)
            if not go():
                return
            phase_mix_out(l)
            if not go():
                return
            phase_ffn(l, 2, xs)
        if not go():
            return
        phase_final()

    body()
    tr.barrier(everything=True)
    tr.add("pool", lambda e: e.dma_start(out=modp[0][0:1, 0:16], in_=modp[0][1:2, 0:16]), (), [("fin",)], dma_sem=("fin",))
    tr.add("pool", lambda e: e.dma_start(out=modp[0][2:3, 0:16], in_=modp[0][3:4, 0:16]), [("fin",)], [("fin2",)],
           dma_sem=("fin2",))
    tr.emit(nc, es)
    es.close()
    return nc, tr


def host_constants(cfg, core):
    c = cfg
    T, HR = c.T, c.HR
    half = core % 2
    pos = (half * T + np.arange(T)).astype(np.float32)
    halfd = 64
    inv = (ROPE_BASE ** (-np.arange(halfd, dtype=np.float32) / halfd)).astype(np.float32)
    ang = (pos[None, :] * inv[:, None]).astype(np.float32)
    cos = np.cos(ang.astype(np.float64)).astype(np.float32)
    sin = np.sin(ang.astype(np.float64)).astype(np.float32)
    cos2 = np.concatenate([cos, cos], 0)
    sin2 = np.concatenate([-sin, sin], 0)
    dk = np.float32(128.0 ** -0.5)
    ropeq = np.concatenate([cos2, sin2], 0).astype(np.float32)
    ropek = (np.concatenate([cos2, sin2], 0) * dk).astype(np.float32)
    k = np.arange(128)[:, None]
    m = np.arange(128)[None, :]
    perm = (k == (m + 64) % 128).astype(np.float32)
    negtri = -(k >= m).astype(np.float32)
    negones = -np.ones((128, 128), np.float32)
    ident = np.eye(128, dtype=np.float32)
    ones = np.ones((128, 128), np.float32)
    cmat = np.concatenate([perm, negtri, negones, ident, ones], 1)
    s = np.arange(128)[:, None]
    t = np.arange(512)[None, :]
    negm = np.concatenate([np.where(t > r * 128 + s, 0.0, NEG_BIG).astype(np.float32) for r in range(4)], 1)
    lg = np.log1p(-np.exp2(-5.0 - np.arange(HR, dtype=np.float64)))
    idx = np.arange(128, dtype=np.float64)
    diff = idx[None, :] - idx[:, None]
    dm = np.where(diff >= 0, np.exp(lg[:, None, None] * np.maximum(diff, 0.0)[None]), 0.0)
    dmask4 = np.tile(dm, (1, 1, 4)).reshape(HR * 128, 512).astype(np.float32)
    xi = np.exp(lg[:, None] * (idx[None, :] + 1.0))
    xi4 = np.tile(np.tile(xi, (1, 4))[:, None, :], (1, 128, 1)).reshape(HR * 128, 512).astype(np.float32)
    zeta = np.exp(lg[:, None] * (127.0 - idx[None, :])).T.astype(np.float32)
    return dict(ropeq=ropeq, ropek=ropek, cmat=cmat, negm=negm, dmask4=dmask4, xi4=xi4,
                zeta=np.ascontiguousarray(zeta))


def make_in_maps(cfg, x, c, ada_w, ada_b, norm_g, w_in, ret_gn_g, sb_norm_g, w_out,
                 ffn1_w_gu, ffn1_w_down, ffn2_w_gu, ffn2_w_down, final_g):
    cf = cfg
    D, T, NCD, DEPTH, HR, HS = cf.D, cf.T, cf.NCD, cf.DEPTH, cf.HR, cf.HS
    f = lambda a: np.ascontiguousarray(np.asarray(a, dtype=np.float32))
    x, c, ada_w, ada_b, norm_g = f(x), f(c), f(ada_w), f(ada_b), f(norm_g)
    ngT = f(norm_g.reshape(DEPTH * 3 * NCD, 128).T)
    rgnT = f(np.asarray(ret_gn_g, np.float32).reshape(DEPTH * HR, 128).T)
    sgnT = f(np.asarray(sb_norm_g, np.float32).reshape(DEPTH * HS, 128).T)
    fgT = f(np.asarray(final_g, np.float32).reshape(NCD, 128).T)
    ws = {"gu1": ffn1_w_gu, "d1": ffn1_w_down, "win": w_in, "wout": w_out, "gu2": ffn2_w_gu, "d2": ffn2_w_down}
    maps = []
    for core in range(N_CORES):
        b, half = core // 2, core % 2
        m = {}
        m["xT"] = f(x[b, half * T:(half + 1) * T, :].T)
        KA, KAP, KAC = cf.KA, cf.KAP, cf.KAC
        cs = c[:, core * KA:(core + 1) * KA]
        m["cT"] = f(cs.reshape(4, KAC, KAP).transpose(2, 1, 0).reshape(KAP, KAC * 4))
        m["adaw"] = f(ada_w[:, core * KA:(core + 1) * KA, :].reshape(DEPTH * KA, 9 * D))
        m["adabT"] = f(ada_b.reshape(DEPTH * 9 * D // 128, 128).T)
        sel4 = np.zeros((128, 4), np.float32)
        sel4[:, b] = 1.0
        m["sel4"] = sel4
        m["flag"] = np.full((128, 1), float(half), np.float32)
        m["ngT"], m["rgnT"], m["sgnT"], m["fgT"] = ngT, rgnT, sgnT, fgT
        for nm, w in ws.items():
            w = np.asarray(w)
            K, N = w.shape[1], w.shape[2]
            r = K // 8
            pw_ = cf.wpw(K, N)
            sh = np.asarray(w[:, core * r:(core + 1) * r, :], np.float32).reshape(DEPTH, r, N // pw_, pw_)
            m[nm] = f(sh.transpose(0, 2, 1, 3).reshape(DEPTH * (N // pw_) * r, pw_))
        m.update(host_constants(cf, core))
        maps.append(m)
    return maps


_CACHE = {}


def run_cfg(cfg, inputs):
    key = (cfg.D, cfg.S, cfg.DEPTH, cfg.HR, cfg.HS)
    if key not in _CACHE:
        _CACHE[key] = build_program(cfg)[0]
    nc = _CACHE[key]
    maps = make_in_maps(cfg, **inputs)
    res = run_bass_kernel_spmd(nc, maps, core_ids=list(range(N_CORES)))
    out = np.empty((cfg.B, cfg.S, cfg.D), np.float32)
    for core in range(N_CORES):
        b, half = core // 2, core % 2
        out[b, half * cfg.T:(half + 1) * cfg.T, :] = res.results[core]["outT"].T
    return out, res


def kernel(**inputs):
    cfg = Cfg()
    out, _ = run_cfg(cfg, inputs)
    return out
```

```python
import math
from contextlib import ExitStack

import numpy as np
import concourse.bass as bass
import concourse.mybir as mybir
from concourse.bass_utils import run_bass_kernel_spmd

F32 = mybir.dt.float32
BF16 = mybir.dt.bfloat16
AF = mybir.ActivationFunctionType
ALU = mybir.AluOpType

EPS = 1e-6
ROPE_BASE = 10000.0
NEG_BIG = -30000.0
N_CORES = 8
PAIRS = [[0, 1], [2, 3], [4, 5], [6, 7]]
QUADS = [[0, 1, 2, 3], [4, 5, 6, 7]]
STR4 = [[0, 4], [1, 5], [2, 6], [3, 7]]


class Cfg:
    def __init__(self, D=4096, S=4096, DEPTH=2, HR=16, HS=16, debug=False):
        self.D, self.S, self.DEPTH, self.HR, self.HS = D, S, DEPTH, HR, HS
        self.B = 4
        self.T = S // 2
        self.TT = min(512, self.T)
        self.NT = self.T // self.TT
        self.NCD = D // 128
        self.F = 2 * D
        self.NFC = self.F // 128
        self.RW, self.SW = HR * 128, HS * 128
        assert self.RW + self.SW == D
        self.INW = 4 * self.RW + 3 * self.SW
        self.NB = self.T // 128
        self.NG = self.T // 512
        self.KA = D // 8
        self.KAP = min(128, self.KA)
        self.KAC = self.KA // self.KAP
        self.NMOD = DEPTH * 9 * D
        self.debug = debug
        assert self.T % 512 == 0 and self.TT == 512

    @staticmethod
    def wpw(K, N):
        rows = K // 8
        return 1024 if (N % 1024 == 0 and rows * 1024 * 2 <= (1 << 20)) else 512


class Op:
    __slots__ = ("eng", "fn", "deps", "sig", "dma_sem", "inc", "idx")


class Tracker:
    ENGS = ("pe", "act", "dve", "pool", "sp")

    def __init__(self):
        self.ops = []
        self.last_writer = {}
        self.readers = {}
        self.last_on_eng = {}
        self.last_on_dsem = {}
        self.barrier_deps = {}
        self.bar_fn = None
        self.nbar = 0
        self.all_dsem = {}

    def add(self, eng, fn, reads=(), writes=(), dma_sem=None, inc=16, prefetch=False):
        op = Op()
        op.eng, op.fn, op.dma_sem, op.inc = eng, fn, dma_sem, inc
        op.idx = len(self.ops)
        op.sig = None
        deps = set()
        for k in reads:
            lw = self.last_writer.get(k)
            if lw is not None:
                deps.add(lw)
        for k in writes:
            lw = self.last_writer.get(k)
            if lw is not None:
                deps.add(lw)
            deps.update(self.readers.get(k, ()))
        if eng in self.barrier_deps:
            deps.update(self.barrier_deps.pop(eng))
        op.deps = deps
        for k in reads:
            lst = self.readers.setdefault(k, [])
            if dma_sem is None:
                lst[:] = [r for r in lst if not (self.ops[r].eng == eng and self.ops[r].dma_sem is None)]
            lst.append(op.idx)
        for k in writes:
            self.last_writer[k] = op.idx
            self.readers[k] = []
        self.ops.append(op)
        if dma_sem is not None:
            self.all_dsem[dma_sem] = op.idx
        if not prefetch:
            self.last_on_eng[eng] = op.idx
            if dma_sem is not None:
                self.last_on_dsem[dma_sem] = op.idx
        return op.idx

    def barrier(self, everything=False):
        deps = set(self.last_on_eng.values()) | set(self.last_on_dsem.values())
        if everything:
            deps |= set(self.all_dsem.values())
        if self.bar_fn is None:
            for e in self.ENGS:
                self.barrier_deps[e] = set(deps)
            return
        self.barrier_deps["sp"] = set(deps)
        self.nbar += 1
        idx = self.add("sp", self.bar_fn, (), (), dma_sem=("bar", self.nbar % 2))
        for e in self.ENGS:
            self.barrier_deps[e] = {idx}

    def emit(self, nc, es):
        ops = self.ops
        needed = set()
        for op in ops:
            for d in op.deps:
                if ops[d].eng == "pe" and op.eng == "pe" and ops[d].dma_sem is None:
                    continue
                needed.add(d)
        eng_sem = {e: es.enter_context(nc.semaphore("sem_" + e)) for e in self.ENGS}
        dsem = {}
        eng_cnt = {e: 0 for e in self.ENGS}
        dcnt = {}
        for op in ops:
            if op.idx not in needed and op.dma_sem is None:
                continue
            if op.dma_sem is not None:
                if op.dma_sem not in dsem:
                    dsem[op.dma_sem] = es.enter_context(nc.semaphore("d%d" % len(dsem)))
                    dcnt[op.dma_sem] = 0
                dcnt[op.dma_sem] += op.inc
                op.sig = (dsem[op.dma_sem], dcnt[op.dma_sem], ("d", op.dma_sem))
            else:
                eng_cnt[op.eng] += 1
                op.sig = (eng_sem[op.eng], eng_cnt[op.eng], ("e", op.eng))
        self.n_sems = len(dsem) + 5
        per_eng = {e: [] for e in self.ENGS}
        for op in ops:
            per_eng[op.eng].append(op)
        block = es.enter_context(nc.Block())

        def run(engine, lst):
            waited = {}
            for op in lst:
                w = {}
                for d in op.deps:
                    s = ops[d].sig
                    if s is None:
                        continue
                    if ops[d].eng == "pe" and op.eng == "pe" and ops[d].dma_sem is None:
                        continue
                    if w.get(s[2], (None, 0))[1] < s[1]:
                        w[s[2]] = (s[0], s[1])
                for key, (sem, val) in w.items():
                    if waited.get(key, 0) >= val:
                        continue
                    waited[key] = val
                    engine.wait_ge(sem, val)
                ins = op.fn(engine)
                if op.sig is not None:
                    if op.dma_sem is not None and op.inc == 1:
                        ins.then_inc(op.sig[0])
                    elif op.dma_sem is not None:
                        ins.then_inc(op.sig[0], op.inc)
                    else:
                        ins.then_inc(op.sig[0], 1)

        @block.tensor
        def _(e):
            run(e, per_eng["pe"])

        @block.scalar
        def _(e):
            run(e, per_eng["act"])

        @block.vector
        def _(e):
            run(e, per_eng["dve"])

        @block.gpsimd
        def _(e):
            run(e, per_eng["pool"])

        @block.sync
        def _(e):
            run(e, per_eng["sp"])


def gammas(HR):
    return [1.0 - 2.0 ** (-5.0 - h) for h in range(HR)]


def build_program(cfg):
    c = cfg
    D, T, TT, NT, NCD, F, NFC = c.D, c.T, c.TT, c.NT, c.NCD, c.F, c.NFC
    HR, HS, RW, SW, INW, NB, NG = c.HR, c.HS, c.RW, c.SW, c.INW, c.NB, c.NG
    DEPTH = c.DEPTH
    nc = bass.Bass("TRN2", target_bir_lowering=False)
    tr = Tracker()
    es = ExitStack()

    def din(name, shape, dt=F32):
        return nc.dram_tensor(name, list(shape), dt, kind="ExternalInput").ap()

    def dint(name, shape, dt):
        if c.debug and name in getattr(c, "debug_out", ()):
            return nc.dram_tensor(name, list(shape), dt, kind="ExternalOutput").ap()
        return nc.dram_tensor(name, list(shape), dt).ap()

    xT = din("xT", [D, T])
    cT = din("cT", [c.KAP, c.KAC * 4])
    adaw = din("adaw", [DEPTH * c.KA, 9 * D])
    adabT_in = din("adabT", [128, c.NMOD // 128])
    sel4_in = din("sel4", [128, 4])
    flag_in = din("flag", [128, 1])
    ngT_in = din("ngT", [128, DEPTH * 3 * NCD])
    rgnT_in = din("rgnT", [128, DEPTH * HR])
    sgnT_in = din("sgnT", [128, DEPTH * HS])
    fgT_in = din("fgT", [128, NCD])
    wdims = {"gu1": (D, 2 * F), "d1": (F, D), "win": (D, INW), "wout": (D, D), "gu2": (D, 2 * F), "d2": (F, D)}
    wsh = {}
    for nm, (K, N) in wdims.items():
        pw_ = c.wpw(K, N)
        wsh[nm] = din(nm, [DEPTH * (N // pw_) * (K // 8), pw_])
    ropeq = din("ropeq", [2 * 128, T])
    ropek = din("ropek", [2 * 128, T])
    cmat_in = din("cmat", [128, 5 * 128])
    negm_in = din("negm", [128, 4 * 512])
    dmask4_in = din("dmask4", [HR * 128, 512])
    xi4_in = din("xi4", [HR * 128, 512])
    zeta_in = din("zeta", [128, HR])
    outT = nc.dram_tensor("outT", [D, T], F32, kind="ExternalOutput").ap()

    wbf = {}
    wfull = {}
    whalf = {}
    wpw = {}
    for nm, (K, N) in wdims.items():
        rows = K // 8
        wpw[nm] = 1024 if (N % 1024 == 0 and rows * 1024 * 2 <= (1 << 20)) else 512
        assert N % wpw[nm] == 0 and rows % 128 == 0
    for l in range(DEPTH):
        for nm, (K, N) in wdims.items():
            PW = wpw[nm]
            NP_ = N // PW
            wbf[(nm, l)] = dint("wb_%s%d" % (nm, l), [NP_ * (K // 8), PW], BF16)
            whalf[(nm, l)] = dint("wh_%s%d" % (nm, l), [NP_ * (K // 2), PW], BF16)
            wfull[(nm, l)] = dint("wf_%s%d" % (nm, l), [NP_ * K, PW], BF16)
    BO = [0, 1, 4, 5, 2, 3, 6, 7]

    def kperm(nm, kcp):
        rb = wdims[nm][0] // 8 // 128
        return BO[kcp // rb] * rb + kcp % rb

    def wpanel(nm, l, c0, w):
        K, N = wdims[nm]
        PW = wpw[nm]
        pn, off = c0 // PW, c0 % PW
        assert off + w <= PW
        reg = wfull[(nm, l)][pn * K:(pn + 1) * K, off:off + w]
        return reg.rearrange("(kc p) n -> p kc n", p=128)

    xs = dint("xs", [D, T], F32)
    NB4L = 9 * NCD * 4
    modp = [dint("modp%d" % l, [128, NB4L], F32) for l in range(DEPTH)]
    modg = [dint("modg%d" % l, [8 * 128, NB4L], F32) for l in range(DEPTH)]
    modh = [dint("modh%d" % l, [4 * 128, NB4L], F32) for l in range(DEPTH)]
    qr_s = dint("qr_s", [RW, T], BF16)
    kr_s = dint("kr_s", [RW, T], BF16)
    vr_s = dint("vr_s", [T, RW], BF16)
    rg_s = dint("rg_s", [RW, T], BF16)
    qs_s = dint("qs_s", [SW, T], BF16)
    kx_own = dint("kx_own", [SW, T], BF16)
    kx_pair = dint("kx_pair", [2 * SW, T], BF16)
    vx_own = dint("vx_own", [T, SW], BF16)
    vx_pair = dint("vx_pair", [2 * T, SW], BF16)
    st_own = dint("st_own", [RW, 128], F32)
    st_pair = dint("st_pair", [2 * RW, 128], F32)
    mix_s = dint("mix_s", [D, T], BF16)

    ARENA_BYTES = 204 * 1024
    arena = es.enter_context(nc.sbuf_tensor("arena", [128, ARENA_BYTES // 4], F32))
    arena_ap = arena[:]
    state = {"off": 0, "base": 0}

    def alloc(shape_free, dt, parts=128):
        n = 1
        for s in shape_free:
            n *= s
        nbytes = n * (4 if dt == F32 else 2)
        nbytes = (nbytes + 31) // 32 * 32
        off = state["off"]
        state["off"] += nbytes
        assert state["off"] <= ARENA_BYTES, ("SBUF arena overflow", state["off"])
        v = arena_ap[0:parts, off // 4:(off + nbytes) // 4]
        if dt == BF16:
            v = v.bitcast(BF16)
        v = v[:, 0:n]
        if len(shape_free) == 2:
            v = v.rearrange("p (a b) -> p a b", b=shape_free[1])
        elif len(shape_free) == 3:
            v = v.rearrange("p (a b c) -> p a b c", b=shape_free[1], c=shape_free[2])
        return v

    def phase_reset():
        state["off"] = state["base"]

    psum = [es.enter_context(nc.psum_tensor("ps%d" % i, [128, 512], F32)) for i in range(8)]
    PS = [p[:] for p in psum]

    def mm(out, lhsT, rhs, start, stop, reads, writes):
        tr.add("pe", lambda e: e.matmul(out, lhsT, rhs, start=start, stop=stop), reads, writes)

    def transp(out, in_, ident, reads, writes):
        tr.add("pe", lambda e: e.transpose(out, in_, ident), reads, writes)

    def act(out, in_, func, reads, writes, bias=None, scale=None):
        kw = {}
        if bias is not None:
            kw["bias"] = bias
        if scale is not None:
            kw["scale"] = scale
        tr.add("act", lambda e: e.activation(out, in_, func, **kw), reads, writes)

    def tt(eng, out, in0, in1, op, reads, writes):
        tr.add(eng, lambda e: e.tensor_tensor(out, in0, in1, op), reads, writes)

    def ts(eng, out, in0, s1, s2, op0, op1, reads, writes):
        if op1 is None:
            tr.add(eng, lambda e: e.tensor_scalar(out, in0, s1, None, op0), reads, writes)
        else:
            tr.add(eng, lambda e: e.tensor_scalar(out, in0, s1, s2, op0, op1), reads, writes)

    def stt(out, in0, scalar, in1, op0, op1, reads, writes):
        tr.add("dve", lambda e: e.scalar_tensor_tensor(out, in0, scalar, in1, op0, op1), reads, writes)

    def cp(eng, out, in_, reads, writes):
        tr.add(eng, lambda e: e.tensor_copy(out, in_), reads, writes)

    def dma(eng, out, in_, reads, writes, sem, prefetch=False):
        tr.add(eng, lambda e: e.dma_start(out=out, in_=in_), reads, writes, dma_sem=sem, prefetch=prefetch)

    def memset(eng, ap, val, writes):
        tr.add(eng, lambda e: e.memset(ap, val), (), writes)

    def allgather(groups, src, dst, reads, writes, name, prefetch=False):
        tr.add("pool", lambda e: e.collective_compute("AllGather", ALU.bypass, replica_groups=groups,
                                                      ins=[src.opt()], outs=[dst.opt()]),
               reads, writes, dma_sem=("cc",), inc=1, prefetch=prefetch)

    cmat = alloc([5, 128], BF16)
    PERM, NEGTRI, NEGONES, IDENT, ONES = (cmat[:, i, :] for i in range(5))
    negm = alloc([4, 512], BF16)
    modT = alloc([DEPTH * 9 * NCD], F32)
    ngT = alloc([DEPTH * 3 * NCD], F32)
    Amod = alloc([DEPTH * 3 * NCD], F32)
    Gmod = alloc([DEPTH * 3 * NCD], F32)
    rgnT = alloc([DEPTH * HR], F32)
    sgnT = alloc([DEPTH * HS], F32)
    fgT = alloc([NCD], F32)
    zeta = alloc([HR], F32)
    flag = alloc([1], F32)
    onec = alloc([1], F32)
    epsc = alloc([1], F32)
    state["base"] = state["off"]

    dma("pool", cmat.rearrange("p a b -> p (a b)"), cmat_in, (), [("cmat",)], "c0")
    dma("pool", negm.rearrange("p a b -> p (a b)"), negm_in, (), [("negm",)], "c1")
    dma("sp", ngT, ngT_in, (), [("ngT",)], "c2")
    dma("sp", rgnT, rgnT_in, (), [("rgnT",)], "c3")
    dma("sp", sgnT, sgnT_in, (), [("sgnT",)], "c4")
    dma("sp", fgT, fgT_in, (), [("fgT",)], "c5")
    dma("sp", zeta, zeta_in, (), [("zeta",)], "c6")
    dma("sp", flag, flag_in, (), [("flag",)], "c7")
    memset("dve", onec, 1.0, [("onec",)])
    memset("dve", epsc, EPS, [("epsc",)])

    def rsqrt(out, in_, mult, reads, wkey):
        act(out, in_, AF.Ln, list(reads) + [("epsc",)], [wkey], bias=epsc[:, 0:1], scale=mult)
        act(out, out, AF.Exp, [wkey], [wkey], scale=-0.5)

    def prep_weight(nm, l):
        if getattr(c, "no_prep", False):
            return
        K, N = wdims[nm]
        rows = K // 8
        PW = wpw[nm]
        NP_ = N // PW
        tot = NP_ * rows
        src = wsh[nm][l * tot:(l + 1) * tot, :]
        dst = wbf[(nm, l)]
        wh, wf = whalf[(nm, l)], wfull[(nm, l)]
        for pn in range(NP_):
            r0 = pn * rows
            for q0 in range(0, rows, 128):
                dma("pool", dst[r0 + q0:r0 + q0 + 128, :], src[r0 + q0:r0 + q0 + 128, :], (), [("wbf", nm, l, pn, q0)],
                    ("wcast", nm, l), prefetch=True)
        porder = list(range(NP_))
        if nm.startswith("gu"):
            porder = [p for pr in zip(range(NP_ // 2), range(NP_ // 2, NP_)) for p in pr]
        for pn in porder:
            r0 = pn * rows
            keys = [("wbf", nm, l, p2, q0) for p2 in range(NP_) for q0 in range(0, rows, 128)]
            allgather(QUADS, dst[r0:r0 + rows, :], wh[pn * 4 * rows:(pn + 1) * 4 * rows, :], keys,
                      [("whalf", nm, l, pn)], ("wq", nm, l, pn), prefetch=True)
            for hh in range(2):
                allgather(STR4, wh[pn * 4 * rows + hh * 2 * rows:pn * 4 * rows + (hh + 1) * 2 * rows, :],
                          wf[pn * K + hh * 4 * rows:pn * K + (hh + 1) * 4 * rows, :], [("whalf", nm, l, pn)],
                          [("wfull", nm, l, pn, hh)], ("w", nm, l, pn, hh), prefetch=True)

    def wkeys(nm, l, c0, w):
        pn = c0 // wpw[nm]
        return [("wfull", nm, l, pn, 0), ("wfull", nm, l, pn, 1)]

    def phase_mod():
        phase_reset()
        KAP, KAC = c.KAP, c.KAC
        NBLK = c.NMOD // 128
        cact = alloc([KAC * 4], F32, parts=KAP)
        cactb = alloc([KAC * 4], BF16, parts=KAP)
        dma("sp", cact, cT, (), [("cact",)], "m0")
        act(cact, cact, AF.Silu, [("cact",)], [("cact",)])
        cp("dve", cactb, cact, [("cact",)], [("cactb",)])
        NPAN = 3
        pan = [alloc([KAC, 512], BF16, parts=KAP) for _ in range(NPAN)]
        stg = [alloc([512], F32) for _ in range(2)]
        acc = alloc([NBLK * 4], F32)
        rk = [alloc([NB4L], F32) for _ in range(2)]
        sel4 = alloc([4], F32)
        adabT = alloc([NBLK], F32)
        dma("sp", sel4, sel4_in, (), [("sel4",)], "m1")
        dma("sp", adabT, adabT_in, (), [("adabT",)], "m2")
        ci = 0
        NBL = 9 * NCD
        for l in range(DEPTH):
            modp_keys = []
            for n0 in range(0, 9 * D, 512):
                s = ci % NPAN
                src = adaw[l * c.KA:(l + 1) * c.KA, n0:n0 + 512].rearrange("(kc p) n -> p kc n", p=KAP)
                dma("pool", pan[s], src, (), [("apan", s)], ("apan", s))
                for q in range(4):
                    blk = n0 // 128 + q
                    gb = l * NBL + blk
                    bank = (gb // 128) % 2
                    o = (gb % 128) * 4
                    for kc in range(KAC):
                        mm(PS[bank][0:128, o:o + 4], pan[s][:, kc, q * 128:(q + 1) * 128], cactb[:, kc * 4:(kc + 1) * 4],
                           kc == 0, kc == KAC - 1, [("apan", s), ("cactb",)], [("ps", bank)])
                    if gb % 128 == 127 or blk == NBL - 1:
                        g0 = max((gb // 128) * 128, l * NBL)
                        nb = gb - g0 + 1
                        b0 = g0 - l * NBL
                        sgi = (gb // 128) % 2
                        po = (g0 % 128) * 4
                        cp("dve", stg[sgi][:, 0:nb * 4], PS[bank][:, po:po + nb * 4], [("ps", bank)], [("stg", sgi)])
                        dma("sp", modp[l][:, b0 * 4:(b0 + nb) * 4], stg[sgi][:, 0:nb * 4], [("stg", sgi)],
                            [("modp", l, b0)], ("stg", sgi))
                        modp_keys.append(("modp", l, b0))
                ci += 1
            allgather(QUADS, modp[l], modh[l], modp_keys, [("modh", l)], ("modq", l))
            for hh in range(2):
                allgather(STR4, modh[l][hh * 256:(hh + 1) * 256, :], modg[l][hh * 512:(hh + 1) * 512, :],
                          [("modh", l)], [("modg", l, hh)], ("mod", l, hh))
        for l in range(DEPTH):
            accl = acc[:, l * NB4L:(l + 1) * NB4L]
            mk = [("modg", l, 0), ("modg", l, 1)]
            dma("sp", accl, modg[l][0:128, :], mk, [("acc", l)], ("m3", l))
            for r in range(1, 8):
                ri = (l * 7 + r) % 2
                dma("sp", rk[ri][:, 0:NB4L], modg[l][r * 128:(r + 1) * 128, :], mk, [("rk", ri)], ("rk", ri))
                tt("dve", accl, accl, rk[ri][:, 0:NB4L], ALU.add, [("acc", l), ("rk", ri)], [("acc", l)])
        acc3 = acc.rearrange("p (n b) -> p n b", b=4)
        akeys = [("acc", l) for l in range(DEPTH)]
        ts("dve", modT, acc3[:, :, 0], sel4[:, 0:1], None, ALU.mult, None, akeys + [("sel4",)], [("modT",)])
        for bb in range(1, 4):
            stt(modT, acc3[:, :, bb], sel4[:, bb:bb + 1], modT, ALU.mult, ALU.add, akeys + [("sel4",), ("modT",)],
                [("modT",)])
        tt("dve", modT, modT, adabT, ALU.add, [("modT",), ("adabT",)], [("modT",)])
        for l in range(DEPTH):
            for j in range(3):
                base = (l * 9 + 3 * j) * NCD
                sc_ = modT[:, base + NCD:base + 2 * NCD]
                gt_ = modT[:, base + 2 * NCD:base + 3 * NCD]
                o = (l * 3 + j) * NCD
                stt(Amod[:, o:o + NCD], sc_, 1.0, ngT[:, o:o + NCD], ALU.add, ALU.mult,
                    [("modT",), ("ngT",)], [("Amod",)])
                ts("dve", Gmod[:, o:o + NCD], gt_, 0.5 if j != 1 else 1.0, None, ALU.mult, None,
                   [("modT",)], [("Gmod",)])

    def mod_cols(l, j):
        o = (l * 3 + j) * NCD
        base = (l * 9 + 3 * j) * NCD
        return (lambda ch: Amod[:, o + ch:o + ch + 1], lambda ch: modT[:, base + ch:base + ch + 1],
                lambda ch: Gmod[:, o + ch:o + ch + 1])

    def norm_tile(src, t0, hbuf, bufs, Acol, Bcol, out_f32=None, tagp="n"):
        xc, sq, tmp, rstd = bufs["xc"], bufs["sq"], bufs["tmp"], bufs["rstd"]
        nx = len(xc)
        NP = getattr(c, "norm_parts", 9)
        for ch in range(NCD):
            s = ch % nx
            dma("sp", xc[s], src[ch * 128:(ch + 1) * 128, t0:t0 + TT], [("xs", ch, t0)], [("xc", s)], ("xc", s))
            if NP >= 1:
                act(sq[ch % 2], xc[s], AF.Square, [("xc", s)], [("sq", ch % 2)])
            if NP >= 2:
                mm(PS[7], ONES, sq[ch % 2], ch == 0, ch == NCD - 1, [("sq", ch % 2), ("cmat",)], [("ps", 7)])
        if NP >= 3:
            rsqrt(rstd, PS[7], 1.0 / D, [("ps", 7)], ("rstd",))
        for ch in range(NCD if NP >= 4 else 0):
            s = ch % nx
            dma("sp", xc[s], src[ch * 128:(ch + 1) * 128, t0:t0 + TT], [("xs", ch, t0)], [("xc", s)], ("xc", s))
            if out_f32 is None:
                tt("dve", tmp[ch % 2], xc[s], rstd, ALU.mult, [("xc", s), ("rstd",)], [("tmp", ch % 2)])
                act(hbuf[:, ch, :], tmp[ch % 2], AF.Identity, [("tmp", ch % 2), ("Amod",), ("modT",)], [("hbuf", ch)],
                    bias=Bcol(ch), scale=Acol(ch))
            else:
                stt(tmp[ch % 2], xc[s], Acol(ch), rstd, ALU.mult, ALU.mult, [("xc", s), ("rstd",), ("fgT",)],
                    [("tmp", ch % 2)])
                dma("pool", out_f32[ch * 128:(ch + 1) * 128, t0:t0 + TT], tmp[ch % 2], [("tmp", ch % 2)],
                    [("out", ch, t0)], ("tmpst", ch % 2))

    def residual_out(ps_bank, ch, t0, Gcol, xo, oi, src, dst):
        s = oi % len(xo)
        dma("sp", xo[s], src[ch * 128:(ch + 1) * 128, t0:t0 + TT], [("xs", ch, t0)], [("xo", s)], ("xo", s))
        stt(xo[s], PS[ps_bank], Gcol(ch), xo[s], ALU.mult, ALU.add, [("ps", ps_bank), ("xo", s), ("Gmod",)],
            [("xo", s)])
        dma("act", dst[ch * 128:(ch + 1) * 128, t0:t0 + TT], xo[s], [("xo", s)], [("xs", ch, t0)], ("xost", s))

    def phase_ffn(l, which, src):
        phase_reset()
        j = 0 if which == 1 else 2
        Acol, Bcol, Gcol = mod_cols(l, j)
        gun, dn = "gu%d" % which, "d%d" % which
        actb = alloc([NFC, TT], BF16)
        hbuf = alloc([NCD, TT], BF16)
        WSLOT = max(NCD * 2 * 256, NFC * 256)
        wsl = [alloc([WSLOT], BF16) for _ in range(2)]
        bufs = {"xc": [alloc([TT], F32) for _ in range(2)], "sq": [alloc([TT], BF16) for _ in range(2)],
                "tmp": [alloc([TT], F32) for _ in range(2)], "rstd": alloc([TT], F32)}
        sg = [alloc([TT], F32) for _ in range(2)]
        xo = [alloc([TT], F32) for _ in range(2)]
        wi = 0
        oi = 0
        pb = 0
        for t_i in range(NT):
            t0 = t_i * TT
            norm_tile(src, t0, hbuf, bufs, Acol, Bcol)
            for m in range(NFC if getattr(c, "ffn_parts", 3) >= 2 else 0):
                if m % 2 == 0:
                    s = wi % 2
                    wi += 1
                    wv = wsl[s][:, 0:NCD * 512].rearrange("p (kc g n) -> p kc g n", g=2, n=256)
                    c0 = (m // 2) * 256
                    dma("sp", wv[:, :, 0, :], wpanel(gun, l, c0, 256), wkeys(gun, l, c0, 256), [("wsl", s)],
                        ("wsl", s))
                    dma("sp", wv[:, :, 1, :], wpanel(gun, l, F + c0, 256), wkeys(gun, l, F + c0, 256),
                        [("wsl", s)], ("wsl", s))
                o = (m % 2) * 128
                bg, bu = pb % 6, (pb + 1) % 6
                pb += 2
                for kc in range(NCD):
                    mm(PS[bg], wv[:, kc, 0, o:o + 128], hbuf[:, kperm(gun, kc), :], kc == 0, kc == NCD - 1,
                       [("wsl", s)] + ([("hbuf", kk) for kk in range(NCD)] if kc == 0 else []), [("ps", bg)])
                for kc in range(NCD):
                    mm(PS[bu], wv[:, kc, 1, o:o + 128], hbuf[:, kperm(gun, kc), :], kc == 0, kc == NCD - 1, [("wsl", s)],
                       [("ps", bu)])
                act(sg[m % 2], PS[bg], AF.Silu, [("ps", bg)], [("sg", m % 2)])
                tt("dve", actb[:, m, :], sg[m % 2], PS[bu], ALU.mult, [("sg", m % 2), ("ps", bu)], [("actb", m)])
            for dc in range(NCD if getattr(c, "ffn_parts", 3) >= 3 else 0):
                if dc % 2 == 0:
                    s = wi % 2
                    wi += 1
                    wv2 = wsl[s][:, 0:NFC * 256].rearrange("p (kc n) -> p kc n", n=256)
                    c0 = (dc // 2) * 256
                    dma("sp", wv2, wpanel(dn, l, c0, 256), wkeys(dn, l, c0, 256), [("wsl", s)], ("wsl", s))
                o = (dc % 2) * 128
                bo = pb % 6
                pb += 1
                for kc in range(NFC):
                    mm(PS[bo], wv2[:, kc, o:o + 128], actb[:, kperm(dn, kc), :], kc == 0, kc == NFC - 1,
                       [("wsl", s)] + ([("actb", kk) for kk in range(NFC)] if kc == 0 else []), [("ps", bo)])
                residual_out(bo, dc, t0, Gcol, xo, oi, src, xs)
                oi += 1
        tr.barrier()

    def phase_mix_in(l):
        phase_reset()
        Acol, Bcol, _ = mod_cols(l, 1)
        hbuf = alloc([NCD, TT], BF16)
        wsl = [alloc([NCD, 512], BF16) for _ in range(2)]
        bufs = {"xc": [alloc([TT], F32) for _ in range(2)], "sq": [alloc([TT], BF16) for _ in range(2)],
                "tmp": [alloc([TT], F32) for _ in range(2)], "rstd": alloc([TT], F32)}
        rope_t = alloc([4, TT], F32)
        qsb = [alloc([TT], BF16) for _ in range(2)]
        t1 = [alloc([TT], F32) for _ in range(2)]
        t2 = [alloc([TT], F32) for _ in range(2)]
        ob = [alloc([512], BF16) for _ in range(4)]
        wi = 0
        pb = 0
        oi = 0
        DK = 128.0 ** -0.5
        for t_i in range(NT):
            t0 = t_i * TT
            norm_tile(xs, t0, hbuf, bufs, Acol, Bcol)
            dma("sp", rope_t[:, 0:2, :], ropeq[:, t0:t0 + TT].rearrange("(a p) t -> p a t", p=128), (), [("rope",)],
                ("rope",))
            dma("sp", rope_t[:, 2:4, :], ropek[:, t0:t0 + TT].rearrange("(a p) t -> p a t", p=128), (), [("rope",)],
                ("rope",))
            hreads = [("hbuf", kk) for kk in range(NCD)]
            for pn in range(INW // 512):
                s = wi % 2
                wi += 1
                col0 = pn * 512
                dma("sp", wsl[s], wpanel("win", l, col0, 512), wkeys("win", l, col0, 512), [("wsl", s)], ("wsl", s))
                seg = col0 // RW if col0 < 4 * RW else 4 + (col0 - 4 * RW) // SW
                if seg not in getattr(c, "segs", (0, 1, 2, 3, 4, 5, 6)):
                    continue
                if seg in (2, 6):
                    for tb in range(TT // 128):
                        bo = pb % 5
                        pb += 1
                        for kc in range(NCD):
                            mm(PS[bo], hbuf[:, kperm("win", kc), tb * 128:(tb + 1) * 128], wsl[s][:, kc, :], kc == 0,
                               kc == NCD - 1,
                               [("wsl", s)] + (hreads if kc == 0 else []), [("ps", bo)])
                        o_ = ob[oi % 4]
                        okey = ("ob", oi % 4)
                        if oi % 2 == 0:
                            act(o_, PS[bo], AF.Identity, [("ps", bo)], [okey])
                        else:
                            cp("dve", o_, PS[bo], [("ps", bo)], [okey])
                        r0 = t0 + tb * 128
                        if seg == 2:
                            cc0 = col0 - 2 * RW
                            dma("act", vr_s[r0:r0 + 128, cc0:cc0 + 512], o_, [okey], [("vr_s", r0, cc0)],
                                ("obst", oi % 4))
                        else:
                            cc0 = col0 - 4 * RW - 2 * SW
                            dma("act", vx_own[r0:r0 + 128, cc0:cc0 + 512], o_, [okey], [("vx_own",)],
                                ("obst", oi % 4))
                        oi += 1
                    continue
                for q in range(4):
                    col = col0 + q * 128
                    bo = pb % 5
                    pb += 1
                    for kc in range(NCD):
                        mm(PS[bo], wsl[s][:, kc, q * 128:(q + 1) * 128], hbuf[:, kperm("win", kc), :], kc == 0,
                           kc == NCD - 1, [("wsl", s)] + (hreads if kc == 0 else []), [("ps", bo)])
                    o_ = ob[oi % 4]
                    okey = ("ob", oi % 4)
                    if seg in (0, 1):
                        ri = 0 if seg == 0 else 2
                        qi = oi % 2
                        RM = getattr(c, "rope_mode", 0)
                        if RM != 2:
                            act(qsb[qi], PS[bo], AF.Identity, [("ps", bo)], [("qsb", qi)])
                        if RM == 0:
                            mm(PS[5 + qi], PERM, qsb[qi], True, True, [("qsb", qi), ("cmat",)], [("ps", 5 + qi)])
                        tt("dve", t1[qi], PS[bo], rope_t[:, ri, :], ALU.mult,
                           [("ps", bo), ("rope",)] + ([("qsb", qi)] if RM != 2 else []), [("t1", qi)])
                        if RM == 0:
                            tt("dve", t2[qi], PS[5 + qi], rope_t[:, ri + 1, :], ALU.mult, [("ps", 5 + qi), ("rope",)],
                               [("t2", qi)])
                        else:
                            tt("dve", t2[qi], PS[bo], rope_t[:, ri + 1, :], ALU.mult, [("ps", bo), ("rope",)],
                               [("t2", qi)])
                        tt(getattr(c, "rope_eng", "dve"), o_, t1[qi], t2[qi], ALU.add, [("t1", qi), ("t2", qi)], [okey])
                        dstt = qr_s if seg == 0 else kr_s
                        r0 = col - seg * RW
                        dma("act", dstt[r0:r0 + 128, t0:t0 + TT], o_, [okey], [("qk_s", seg, r0, t0)],
                            ("obst", oi % 4))
                    elif seg == 3:
                        act(o_, PS[bo], AF.Silu, [("ps", bo)], [okey])
                        r0 = col - 3 * RW
                        dma("act", rg_s[r0:r0 + 128, t0:t0 + TT], o_, [okey], [("rg_s", r0, t0)], ("obst", oi % 4))
                    elif seg == 4:
                        act(o_, PS[bo], AF.Identity, [("ps", bo)], [okey], scale=DK)
                        r0 = col - 4 * RW
                        dma("act", qs_s[r0:r0 + 128, t0:t0 + TT], o_, [okey], [("qs_s", r0, t0)], ("obst", oi % 4))
                    else:
                        cp("dve", o_, PS[bo], [("ps", bo)], [okey])
                        r0 = col - 4 * RW - SW
                        dma("act", kx_own[r0:r0 + 128, t0:t0 + TT], o_, [okey], [("kx_own",)], ("obst", oi % 4))
                    oi += 1
        tr.barrier()

    GAM = gammas(HR)

    def phase_ret(l, outputs):
        phase_reset()
        kT = [alloc([T], BF16) for _ in range(2)]
        vt = [alloc([NB, 128], BF16) for _ in range(2)]
        S = [alloc([128], F32) for _ in range(2)]
        Sb = [alloc([128], BF16) for _ in range(2)]
        kz = [alloc([128], BF16) for _ in range(2)]
        if outputs:
            qT = [alloc([T], BF16) for _ in range(2)]
            dm4 = [alloc([512], F32) for _ in range(2)]
            xi4 = [alloc([512], F32) for _ in range(2)]
            sc = [alloc([512], BF16) for _ in range(2)]
            qx = [alloc([512], BF16) for _ in range(2)]
            sinit = alloc([128], F32)
            o_bf = alloc([512], BF16)
            osq = alloc([512], BF16)
            mean = alloc([512], F32)
            m2 = alloc([512], F32)
            var = alloc([512], F32)
            cen = alloc([512], F32)
            yb = alloc([512], F32)
            rgt = [alloc([512], BF16) for _ in range(2)]
            outb = [alloc([512], BF16) for _ in range(2)]
        si = 0
        gi = 0
        for h in range(HR):
            hs = h % 2
            cd = GAM[h] ** 128
            dma("sp", kT[hs], kr_s[h * 128:(h + 1) * 128, :], [("qk_s_all",)], [("kT", hs)], ("kT", hs))
            dma("sp", vt[hs], vr_s[:, h * 128:(h + 1) * 128].rearrange("(n p) d -> p n d", p=128), [("vr_all",)],
                [("vt", hs)], ("vt", hs))
            if outputs:
                dma("sp", qT[hs], qr_s[h * 128:(h + 1) * 128, :], [("qk_s_all",)], [("qT", hs)], ("qT", hs))
                dma("sp", dm4[hs], dmask4_in[h * 128:(h + 1) * 128, :], (), [("dm4", hs)], ("dm4", hs))
                dma("sp", xi4[hs], xi4_in[h * 128:(h + 1) * 128, :], (), [("xi4", hs)], ("xi4", hs))
                dma("sp", sinit, st_pair[h * 128:(h + 1) * 128, :], [("st_pair",)], [("sinit",)], ("sinit",))
                ts("dve", S[si % 2], sinit, flag[:, 0:1], None, ALU.mult, None, [("sinit",), ("flag",)],
                   [("S", si % 2)])
                cp("pool", Sb[si % 2], S[si % 2], [("S", si % 2)], [("Sb", si % 2)])
            else:
                memset("dve", S[si % 2], 0.0, [("S", si % 2)])
            for gq in range(NG):
                if outputs:
                    for cc in range(4):
                        n = gq * 4 + cc
                        mm(PS[0][:, cc * 128:(cc + 1) * 128], kT[hs][:, n * 128:(n + 1) * 128],
                           qT[hs][:, n * 128:(n + 1) * 128], True, True, [("kT", hs), ("qT", hs)], [("ps", 0)])
                    g2 = gi % 2
                    tt("dve", sc[g2], PS[0], dm4[hs], ALU.mult, [("ps", 0), ("dm4", hs)], [("sc", g2)])
                    tt("pool", qx[g2], qT[hs][:, gq * 512:(gq + 1) * 512], xi4[hs], ALU.mult,
                       [("qT", hs), ("xi4", hs)], [("qx", g2)])
                for cc in range(4):
                    n = gq * 4 + cc
                    cur, nxt = si % 2, (si + 1) % 2
                    if outputs:
                        mm(PS[1][:, cc * 128:(cc + 1) * 128], vt[hs][:, n, :], sc[g2][:, cc * 128:(cc + 1) * 128],
                           True, False, [("vt", hs), ("sc", g2)], [("ps", 1)])
                        mm(PS[1][:, cc * 128:(cc + 1) * 128], Sb[cur], qx[g2][:, cc * 128:(cc + 1) * 128],
                           False, True, [("Sb", cur), ("qx", g2)], [("ps", 1)])
                    pk = PS[2 + si % 2].bitcast(BF16)[:, 0:128]
                    transp(pk, kT[hs][:, n * 128:(n + 1) * 128], IDENT, [("kT", hs), ("cmat",)], [("ps", 2 + si % 2)])
                    act(kz[si % 2], pk, AF.Identity, [("ps", 2 + si % 2), ("zeta",)], [("kz", si % 2)],
                        scale=zeta[:, h:h + 1])
                    mm(PS[4 + si % 2][:, 0:128], kz[si % 2], vt[hs][:, n, :], True, True, [("kz", si % 2), ("vt", hs)],
                       [("ps", 4 + si % 2)])
                    stt(S[nxt], S[cur], cd, PS[4 + si % 2][:, 0:128], ALU.mult, ALU.add,
                        [("S", cur), ("ps", 4 + si % 2)], [("S", nxt)])
                    if outputs:
                        cp("pool", Sb[nxt], S[nxt], [("S", nxt)], [("Sb", nxt)])
                    si += 1
                if outputs:
                    act(o_bf, PS[1], AF.Identity, [("ps", 1)], [("o_bf",)])
                    act(osq, PS[1], AF.Square, [("ps", 1)], [("osq",)])
                    mm(PS[6], ONES, o_bf, True, True, [("o_bf",), ("cmat",)], [("ps", 6)])
                    mm(PS[7], ONES, osq, True, True, [("osq",), ("cmat",)], [("ps", 7)])
                    ts("dve", mean, PS[6], 1.0 / 128, None, ALU.mult, None, [("ps", 6)], [("mean",)])
                    tt("pool", m2, mean, mean, ALU.mult, [("mean",)], [("m2",)])
                    stt(var, PS[7], 1.0 / 128, m2, ALU.mult, ALU.subtract, [("ps", 7), ("m2",)], [("var",)])
                    rsqrt(var, var, 1.0, [("var",)], ("var",))
                    tt("dve", cen, PS[1], mean, ALU.subtract, [("ps", 1), ("mean",)], [("cen",)])
                    stt(yb, cen, rgnT[:, l * HR + h:l * HR + h + 1], var, ALU.mult, ALU.mult,
                        [("cen",), ("var",), ("rgnT",)], [("yb",)])
                    dma("sp", rgt[g2], rg_s[h * 128:(h + 1) * 128, gq * 512:(gq + 1) * 512], [("rg_all",)],
                        [("rgt", g2)], ("rgt", g2))
                    tt("pool", outb[g2], yb, rgt[g2], ALU.mult, [("yb",), ("rgt", g2)], [("outb", g2)])
                    dma("pool", mix_s[h * 128:(h + 1) * 128, gq * 512:(gq + 1) * 512], outb[g2], [("outb", g2)],
                        [("mix_s", h, gq)], ("outbst", g2))
                    gi += 1
            if not outputs:
                dma("pool", st_own[h * 128:(h + 1) * 128, :], S[si % 2], [("S", si % 2)], [("st_own",)],
                    ("Sst", si % 2))
        tr.barrier()

    def phase_sb(l):
        phase_reset()
        qT = [alloc([T], BF16) for _ in range(2)]
        kTo = [alloc([T], BF16) for _ in range(2)]
        kTp = [alloc([T], BF16) for _ in range(2)]
        vo = [alloc([NB, 128], BF16) for _ in range(2)]
        vp = [alloc([NB, 128], BF16) for _ in range(2)]
        eb = [alloc([512], F32) for _ in range(2)]
        spb = [alloc([512], BF16) for _ in range(2)]
        R = [alloc([512], BF16) for _ in range(2)]
        ab = [alloc([512], BF16) for _ in range(2)]
        osq = alloc([512], BF16)
        rstd = alloc([512], F32)
        obuf = [alloc([512], BF16) for _ in range(2)]
        ui = 0
        gi = 0
        for h in range(HS):
            hs = h % 2
            dma("sp", qT[hs], qs_s[h * 128:(h + 1) * 128, :], [("qs_all",)], [("qT", hs)], ("qT", hs))
            dma("sp", kTo[hs], kx_own[h * 128:(h + 1) * 128, :], [("kx_own",)], [("kTo", hs)], ("kTo", hs))
            dma("sp", kTp[hs], kx_pair[h * 256:h * 256 + 128, :], [("kx_pair", h)], [("kTp", hs)], ("kTp", hs))
            dma("sp", vo[hs], vx_own[:, h * 128:(h + 1) * 128].rearrange("(n p) d -> p n d", p=128), [("vx_own",)],
                [("vo", hs)], ("vo", hs))
            vpk = []
            for cc_ in range(T // 256):
                dma("sp", vp[hs][:, 2 * cc_:2 * cc_ + 2, :],
                    vx_pair[cc_ * 512:cc_ * 512 + 256, h * 128:(h + 1) * 128].rearrange("(n p) d -> p n d", p=128),
                    [("vx_pair", cc_)], [("vp", hs)], ("vp", hs))
            ts("pool", vp[hs], vp[hs], flag[:, 0:1], None, ALU.mult, None, vpk + [("vp", hs), ("flag",)],
               vpk + [("vp", hs)])
            for g in range(NG):
                units = [("o", j) for j in range(4 * g + 3, -1, -1)] + [("p", j) for j in range(NB - 1, -1, -1)]
                qg = qT[hs][:, g * 512:(g + 1) * 512]
                po = 6 + gi % 2
                nun = len(units)
                ui0 = ui

                def unit_ctx(u):
                    kind, j = units[u]
                    u2 = (ui0 + u) % 2
                    kt = (kTo if kind == "o" else kTp)[hs][:, j * 128:(j + 1) * 128]
                    vv = (vo if kind == "o" else vp)[hs][:, j, :]
                    kkey = ("kTo", hs) if kind == "o" else ("kTp", hs)
                    vkey = ("vo", hs) if kind == "o" else ("vp", hs)
                    diag = kind == "o" and j >= 4 * g
                    return u2, kt, vv, kkey, vkey, diag, j - 4 * g

                def stage_a(u):
                    u2, kt, vv, kkey, vkey, diag, r = unit_ctx(u)
                    pz = u2
                    mm(PS[pz], kt, qg, True, not diag, [kkey, ("qT", hs)], [("ps", pz)])
                    if diag:
                        mm(PS[pz], IDENT, negm[:, r, :], False, True, [("cmat",), ("negm",)], [("ps", pz)])
                    act(eb[u2], PS[pz], AF.Exp, [("ps", pz)], [("eb", u2)])
                    act(spb[u2], eb[u2], AF.Ln, [("eb", u2), ("onec",)], [("spb", u2)], bias=onec[:, 0:1])

                def stage_b(u):
                    u2, kt, vv, kkey, vkey, diag, r = unit_ctx(u)
                    pzi = 2 + u2
                    mm(PS[pzi], NEGTRI, spb[u2], True, False, [("spb", u2), ("cmat",)], [("ps", pzi)])
                    if u > 0:
                        mm(PS[pzi], NEGONES, R[u2], False, False, [("R", u2)], [("ps", pzi)])
                    mm(PS[pzi], kt, qg, False, not diag, [kkey, ("qT", hs)], [("ps", pzi)])
                    if diag:
                        mm(PS[pzi], IDENT, negm[:, r, :], False, True, [("cmat",), ("negm",)], [("ps", pzi)])
                    if u + 1 < nun:
                        n2 = (ui0 + u + 1) % 2
                        if u == 0:
                            cp("pool", R[n2], spb[u2], [("spb", u2)], [("R", n2)])
                        else:
                            tt("pool", R[n2], R[u2], spb[u2], ALU.add, [("R", u2), ("spb", u2)], [("R", n2)])
                    act(ab[u2], PS[pzi], AF.Exp, [("ps", pzi)], [("ab", u2)])

                def stage_c(u):
                    u2, kt, vv, kkey, vkey, diag, r = unit_ctx(u)
                    mm(PS[po], vv, ab[u2], u == 0, u == nun - 1, [vkey, ("ab", u2)], [("ps", po)])

                for u in range(-1, nun + 1):
                    if u + 1 < nun:
                        stage_a(u + 1)
                    if 0 <= u < nun:
                        stage_b(u)
                    if 0 <= u - 1 < nun:
                        stage_c(u - 1)
                ui += nun
                g2 = gi % 2
                act(osq, PS[po], AF.Square, [("ps", po)], [("osq",)])
                mm(PS[4], ONES, osq, True, True, [("osq",), ("cmat",)], [("ps", 4)])
                rsqrt(rstd, PS[4], 1.0 / 128, [("ps", 4)], ("rstd",))
                stt(obuf[g2], PS[po], sgnT[:, l * HS + h:l * HS + h + 1], rstd, ALU.mult, ALU.mult,
                    [("ps", po), ("rstd",), ("sgnT",)], [("obuf", g2)])
                dma("pool", mix_s[RW + h * 128:RW + (h + 1) * 128, g * 512:(g + 1) * 512], obuf[g2], [("obuf", g2)],
                    [("mix_s", "sb", h, g)], ("obufst", g2))
                gi += 1
        tr.barrier()

    def phase_mix_out(l):
        phase_reset()
        _, _, Gcol = mod_cols(l, 1)
        hbuf = alloc([NCD, TT], BF16)
        wsl = [alloc([NCD, 512], BF16) for _ in range(2)]
        xo = [alloc([TT], F32) for _ in range(3)]
        wi = 0
        oi = 0
        pb = 0
        for t_i in range(NT):
            t0 = t_i * TT
            dma("sp", hbuf, mix_s[:, t0:t0 + TT].rearrange("(kc p) t -> p kc t", p=128), [("mix_all",)], [("hbuf",)],
                ("hbuf",))
            for dc in range(NCD):
                if dc % 4 == 0:
                    s = wi % 2
                    wi += 1
                    c0 = (dc // 4) * 512
                    w_ = min(512, D - c0)
                    dma("sp", wsl[s][:, :, 0:w_], wpanel("wout", l, c0, w_), wkeys("wout", l, c0, w_), [("wsl", s)],
                        ("wsl", s))
                o = (dc % 4) * 128
                bo = pb % 6
                pb += 1
                for kc in range(NCD):
                    mm(PS[bo], wsl[s][:, kc, o:o + 128], hbuf[:, kperm("wout", kc), :], kc == 0, kc == NCD - 1,
                       [("wsl", s), ("hbuf",)], [("ps", bo)])
                residual_out(bo, dc, t0, Gcol, xo, oi, xs, xs)
                oi += 1
        tr.barrier()

    def phase_final():
        phase_reset()
        bufs = {"xc": [alloc([TT], F32) for _ in range(3)], "sq": [alloc([TT], BF16) for _ in range(2)],
                "tmp": [alloc([TT], F32) for _ in range(2)], "rstd": alloc([TT], F32)}
        for t_i in range(NT):
            norm_tile(xs, t_i * TT, None, bufs, lambda ch: fgT[:, ch:ch + 1], None, out_f32=outT)
        tr.barrier()

    order = ["gu1", "d1", "win", "wout", "gu2", "d2"]
    stop = getattr(c, "stop", 10 ** 9)
    cnt = {"n": 0}

    def go():
        cnt["n"] += 1
        return cnt["n"] <= stop

    def body():
        prep_weight("gu1", 0)
        prep_weight("d1", 0)
        if not go():
            return
        if not getattr(c, "no_mod", False):
            phase_mod()
        tr.barrier()
        prep_weight("win", 0)
        for l in range(DEPTH):
            if not go():
                return
            phase_ffn(l, 1, xT if l == 0 else xs)
            if l == 0 and not getattr(c, "skip_prep", False):
                for nm in order[3:]:
                    prep_weight(nm, 0)
            if not go():
                return
            phase_mix_in(l)
            if not go():
                return
            phase_ret(l, False)
            for h in range(HS):
                allgather(PAIRS, kx_own[h * 128:(h + 1) * 128, :], kx_pair[h * 256:(h + 1) * 256, :], [("kx_own",)],
                          [("kx_pair", h)], ("kx", l, h))
            for cc_ in range(T // 256):
                allgather(PAIRS, vx_own[cc_ * 256:(cc_ + 1) * 256, :], vx_pair[cc_ * 512:(cc_ + 1) * 512, :],
                          [("vx_own",)], [("vx_pair", cc_)], ("vx", l, cc_))
            assert RW * 128 * 4 <= (1 << 20)
            allgather(PAIRS, st_own, st_pair, [("st_own",)], [("st_pair",)], ("st", l))
            tr.barrier()
            if not go():
                return
            phase_sb(l)
            if not go():
                return
            phase_ret(l, True)
            if l + 1 < DEPTH and not getattr(c, "skip_prep", False):
                for nm in order:
                    prep_weight(nm, l + 1)
            if not go():
                return
            phase_mix_out(l)
            if not go():
                return
            phase_ffn(l, 2, xs)
        if not go():
            return
        phase_final()

    body()
    tr.barrier(everything=True)
    tr.add("pool", lambda e: e.dma_start(out=modp[0][0:1, 0:16], in_=modp[0][1:2, 0:16]), (), [("fin",)], dma_sem=("fin",))
    tr.add("pool", lambda e: e.dma_start(out=modp[0][2:3, 0:16], in_=modp[0][3:4, 0:16]), [("fin",)], [("fin2",)],
           dma_sem=("fin2",))
    tr.emit(nc, es)
    es.close()
    return nc, tr


def host_constants(cfg, core):
    c = cfg
    T, HR = c.T, c.HR
    half = core % 2
    pos = (half * T + np.arange(T)).astype(np.float32)
    halfd = 64
    inv = (ROPE_BASE ** (-np.arange(halfd, dtype=np.float32) / halfd)).astype(np.float32)
    ang = (pos[None, :] * inv[:, None]).astype(np.float32)
    cos = np.cos(ang.astype(np.float64)).astype(np.float32)
    sin = np.sin(ang.astype(np.float64)).astype(np.float32)
    cos2 = np.concatenate([cos, cos], 0)
    sin2 = np.concatenate([-sin, sin], 0)
    dk = np.float32(128.0 ** -0.5)
    ropeq = np.concatenate([cos2, sin2], 0).astype(np.float32)
    ropek = (np.concatenate([cos2, sin2], 0) * dk).astype(np.float32)
    k = np.arange(128)[:, None]
    m = np.arange(128)[None, :]
    perm = (k == (m + 64) % 128).astype(np.float32)
    negtri = -(k >= m).astype(np.float32)
    negones = -np.ones((128, 128), np.float32)
    ident = np.eye(128, dtype=np.float32)
    ones = np.ones((128, 128), np.float32)
    cmat = np.concatenate([perm, negtri, negones, ident, ones], 1)
    s = np.arange(128)[:, None]
    t = np.arange(512)[None, :]
    negm = np.concatenate([np.where(t > r * 128 + s, 0.0, NEG_BIG).astype(np.float32) for r in range(4)], 1)
    lg = np.log1p(-np.exp2(-5.0 - np.arange(HR, dtype=np.float64)))
    idx = np.arange(128, dtype=np.float64)
    diff = idx[None, :] - idx[:, None]
    dm = np.where(diff >= 0, np.exp(lg[:, None, None] * np.maximum(diff, 0.0)[None]), 0.0)
    dmask4 = np.tile(dm, (1, 1, 4)).reshape(HR * 128, 512).astype(np.float32)
    xi = np.exp(lg[:, None] * (idx[None, :] + 1.0))
    xi4 = np.tile(np.tile(xi, (1, 4))[:, None, :], (1, 128, 1)).reshape(HR * 128, 512).astype(np.float32)
    zeta = np.exp(lg[:, None] * (127.0 - idx[None, :])).T.astype(np.float32)
    return dict(ropeq=ropeq, ropek=ropek, cmat=cmat, negm=negm, dmask4=dmask4, xi4=xi4,
                zeta=np.ascontiguousarray(zeta))


def make_in_maps(cfg, x, c, ada_w, ada_b, norm_g, w_in, ret_gn_g, sb_norm_g, w_out,
                 ffn1_w_gu, ffn1_w_down, ffn2_w_gu, ffn2_w_down, final_g):
    cf = cfg
    D, T, NCD, DEPTH, HR, HS = cf.D, cf.T, cf.NCD, cf.DEPTH, cf.HR, cf.HS
    f = lambda a: np.ascontiguousarray(np.asarray(a, dtype=np.float32))
    x, c, ada_w, ada_b, norm_g = f(x), f(c), f(ada_w), f(ada_b), f(norm_g)
    ngT = f(norm_g.reshape(DEPTH * 3 * NCD, 128).T)
    rgnT = f(np.asarray(ret_gn_g, np.float32).reshape(DEPTH * HR, 128).T)
    sgnT = f(np.asarray(sb_norm_g, np.float32).reshape(DEPTH * HS, 128).T)
    fgT = f(np.asarray(final_g, np.float32).reshape(NCD, 128).T)
    ws = {"gu1": ffn1_w_gu, "d1": ffn1_w_down, "win": w_in, "wout": w_out, "gu2": ffn2_w_gu, "d2": ffn2_w_down}
    maps = []
    for core in range(N_CORES):
        b, half = core // 2, core % 2
        m = {}
        m["xT"] = f(x[b, half * T:(half + 1) * T, :].T)
        KA, KAP, KAC = cf.KA, cf.KAP, cf.KAC
        cs = c[:, core * KA:(core + 1) * KA]
        m["cT"] = f(cs.reshape(4, KAC, KAP).transpose(2, 1, 0).reshape(KAP, KAC * 4))
        m["adaw"] = f(ada_w[:, core * KA:(core + 1) * KA, :].reshape(DEPTH * KA, 9 * D))
        m["adabT"] = f(ada_b.reshape(DEPTH * 9 * D // 128, 128).T)
        sel4 = np.zeros((128, 4), np.float32)
        sel4[:, b] = 1.0
        m["sel4"] = sel4
        m["flag"] = np.full((128, 1), float(half), np.float32)
        m["ngT"], m["rgnT"], m["sgnT"], m["fgT"] = ngT, rgnT, sgnT, fgT
        for nm, w in ws.items():
            w = np.asarray(w)
            K, N = w.shape[1], w.shape[2]
            r = K // 8
            pw_ = cf.wpw(K, N)
            sh = np.asarray(w[:, core * r:(core + 1) * r, :], np.float32).reshape(DEPTH, r, N // pw_, pw_)
            m[nm] = f(sh.transpose(0, 2, 1, 3).reshape(DEPTH * (N // pw_) * r, pw_))
        m.update(host_constants(cf, core))
        maps.append(m)
    return maps


_CACHE = {}


def run_cfg(cfg, inputs):
    key = (cfg.D, cfg.S, cfg.DEPTH, cfg.HR, cfg.HS)
    if key not in _CACHE:
        _CACHE[key] = build_program(cfg)[0]
    nc = _CACHE[key]
    maps = make_in_maps(cfg, **inputs)
    res = run_bass_kernel_spmd(nc, maps, core_ids=list(range(N_CORES)))
    out = np.empty((cfg.B, cfg.S, cfg.D), np.float32)
    for core in range(N_CORES):
        b, half = core // 2, core % 2
        out[b, half * cfg.T:(half + 1) * cfg.T, :] = res.results[core]["outT"].T
    return out, res


def kernel(**inputs):
    cfg = Cfg()
    out, _ = run_cfg(cfg, inputs)
    return out
```

```python
import math
from contextlib import ExitStack

import numpy as np
import concourse.bass as bass
import concourse.mybir as mybir
from concourse.bass_utils import run_bass_kernel_spmd

F32 = mybir.dt.float32
BF16 = mybir.dt.bfloat16
AF = mybir.ActivationFunctionType
ALU = mybir.AluOpType

EPS = 1e-6
ROPE_BASE = 10000.0
NEG_BIG = -30000.0
N_CORES = 8
PAIRS = [[0, 1], [2, 3], [4, 5], [6, 7]]
QUADS = [[0, 1, 2, 3], [4, 5, 6, 7]]
STR4 = [[0, 4], [1, 5], [2, 6], [3, 7]]


class Cfg:
    def __init__(self, D=4096, S=4096, DEPTH=2, HR=16, HS=16, debug=False):
        self.D, self.S, self.DEPTH, self.HR, self.HS = D, S, DEPTH, HR, HS
        self.B = 4
        self.T = S // 2
        self.TT = min(512, self.T)
        self.NT = self.T // self.TT
        self.NCD = D // 128
        self.F = 2 * D
        self.NFC = self.F // 128
        self.RW, self.SW = HR * 128, HS * 128
        assert self.RW + self.SW == D
        self.INW = 4 * self.RW + 3 * self.SW
        self.NB = self.T // 128
        self.NG = self.T // 512
        self.KA = D // 8
        self.KAP = min(128, self.KA)
        self.KAC = self.KA // self.KAP
        self.NMOD = DEPTH * 9 * D
        self.debug = debug
        assert self.T % 512 == 0 and self.TT == 512

    @staticmethod
    def wpw(K, N):
        rows = K // 8
        return 1024 if (N % 1024 == 0 and rows * 1024 * 2 <= (1 << 20)) else 512


class Op:
    __slots__ = ("eng", "fn", "deps", "sig", "dma_sem", "inc", "idx")


class Tracker:
    ENGS = ("pe", "act", "dve", "pool", "sp")

    def __init__(self):
        self.ops = []
        self.last_writer = {}
        self.readers = {}
        self.last_on_eng = {}
        self.last_on_dsem = {}
        self.barrier_deps = {}
        self.bar_fn = None
        self.nbar = 0
        self.all_dsem = {}

    def add(self, eng, fn, reads=(), writes=(), dma_sem=None, inc=16, prefetch=False):
        op = Op()
        op.eng, op.fn, op.dma_sem, op.inc = eng, fn, dma_sem, inc
        op.idx = len(self.ops)
        op.sig = None
        deps = set()
        for k in reads:
            lw = self.last_writer.get(k)
            if lw is not None:
                deps.add(lw)
        for k in writes:
            lw = self.last_writer.get(k)
            if lw is not None:
                deps.add(lw)
            deps.update(self.readers.get(k, ()))
        if eng in self.barrier_deps:
            deps.update(self.barrier_deps.pop(eng))
        op.deps = deps
        for k in reads:
            lst = self.readers.setdefault(k, [])
            if dma_sem is None:
                lst[:] = [r for r in lst if not (self.ops[r].eng == eng and self.ops[r].dma_sem is None)]
            lst.append(op.idx)
        for k in writes:
            self.last_writer[k] = op.idx
            self.readers[k] = []
        self.ops.append(op)
        if dma_sem is not None:
            self.all_dsem[dma_sem] = op.idx
        if not prefetch:
            self.last_on_eng[eng] = op.idx
            if dma_sem is not None:
                self.last_on_dsem[dma_sem] = op.idx
        return op.idx

    def barrier(self, everything=False):
        deps = set(self.last_on_eng.values()) | set(self.last_on_dsem.values())
        if everything:
            deps |= set(self.all_dsem.values())
        if self.bar_fn is None:
            for e in self.ENGS:
                self.barrier_deps[e] = set(deps)
            return
        self.barrier_deps["sp"] = set(deps)
        self.nbar += 1
        idx = self.add("sp", self.bar_fn, (), (), dma_sem=("bar", self.nbar % 2))
        for e in self.ENGS:
            self.barrier_deps[e] = {idx}

    def emit(self, nc, es):
        ops = self.ops
        needed = set()
        for op in ops:
            for d in op.deps:
                if ops[d].eng == "pe" and op.eng == "pe" and ops[d].dma_sem is None:
                    continue
                needed.add(d)
        eng_sem = {e: es.enter_context(nc.semaphore("sem_" + e)) for e in self.ENGS}
        dsem = {}
        eng_cnt = {e: 0 for e in self.ENGS}
        dcnt = {}
        for op in ops:
            if op.idx not in needed and op.dma_sem is None:
                continue
            if op.dma_sem is not None:
                if op.dma_sem not in dsem:
                    dsem[op.dma_sem] = es.enter_context(nc.semaphore("d%d" % len(dsem)))
                    dcnt[op.dma_sem] = 0
                dcnt[op.dma_sem] += op.inc
                op.sig = (dsem[op.dma_sem], dcnt[op.dma_sem], ("d", op.dma_sem))
            else:
                eng_cnt[op.eng] += 1
                op.sig = (eng_sem[op.eng], eng_cnt[op.eng], ("e", op.eng))
        self.n_sems = len(dsem) + 5
        per_eng = {e: [] for e in self.ENGS}
        for op in ops:
            per_eng[op.eng].append(op)
        block = es.enter_context(nc.Block())

        def run(engine, lst):
            waited = {}
            for op in lst:
                w = {}
                for d in op.deps:
                    s = ops[d].sig
                    if s is None:
                        continue
                    if ops[d].eng == "pe" and op.eng == "pe" and ops[d].dma_sem is None:
                        continue
                    if w.get(s[2], (None, 0))[1] < s[1]:
                        w[s[2]] = (s[0], s[1])
                for key, (sem, val) in w.items():
                    if waited.get(key, 0) >= val:
                        continue
                    waited[key] = val
                    engine.wait_ge(sem, val)
                ins = op.fn(engine)
                if op.sig is not None:
                    if op.dma_sem is not None and op.inc == 1:
                        ins.then_inc(op.sig[0])
                    elif op.dma_sem is not None:
                        ins.then_inc(op.sig[0], op.inc)
                    else:
                        ins.then_inc(op.sig[0], 1)

        @block.tensor
        def _(e):
            run(e, per_eng["pe"])

        @block.scalar
        def _(e):
            run(e, per_eng["act"])

        @block.vector
        def _(e):
            run(e, per_eng["dve"])

        @block.gpsimd
        def _(e):
            run(e, per_eng["pool"])

        @block.sync
        def _(e):
            run(e, per_eng["sp"])


def gammas(HR):
    return [1.0 - 2.0 ** (-5.0 - h) for h in range(HR)]


def build_program(cfg):
    c = cfg
    D, T, TT, NT, NCD, F, NFC = c.D, c.T, c.TT, c.NT, c.NCD, c.F, c.NFC
    HR, HS, RW, SW, INW, NB, NG = c.HR, c.HS, c.RW, c.SW, c.INW, c.NB, c.NG
    DEPTH = c.DEPTH
    nc = bass.Bass("TRN2", target_bir_lowering=False)
    tr = Tracker()
    es = ExitStack()

    def din(name, shape, dt=F32):
        return nc.dram_tensor(name, list(shape), dt, kind="ExternalInput").ap()

    def dint(name, shape, dt):
        if c.debug and name in getattr(c, "debug_out", ()):
            return nc.dram_tensor(name, list(shape), dt, kind="ExternalOutput").ap()
        return nc.dram_tensor(name, list(shape), dt).ap()

    xT = din("xT", [D, T])
    cT = din("cT", [c.KAP, c.KAC * 4])
    adaw = din("adaw", [DEPTH * c.KA, 9 * D])
    adabT_in = din("adabT", [128, c.NMOD // 128])
    sel4_in = din("sel4", [128, 4])
    flag_in = din("flag", [128, 1])
    ngT_in = din("ngT", [128, DEPTH * 3 * NCD])
    rgnT_in = din("rgnT", [128, DEPTH * HR])
    sgnT_in = din("sgnT", [128, DEPTH * HS])
    fgT_in = din("fgT", [128, NCD])
    wdims = {"gu1": (D, 2 * F), "d1": (F, D), "win": (D, INW), "wout": (D, D), "gu2": (D, 2 * F), "d2": (F, D)}
    wsh = {}
    for nm, (K, N) in wdims.items():
        pw_ = c.wpw(K, N)
        wsh[nm] = din(nm, [DEPTH * (N // pw_) * (K // 8), pw_])
    ropeq = din("ropeq", [2 * 128, T])
    ropek = din("ropek", [2 * 128, T])
    cmat_in = din("cmat", [128, 5 * 128])
    negm_in = din("negm", [128, 4 * 512])
    dmask4_in = din("dmask4", [HR * 128, 512])
    xi4_in = din("xi4", [HR * 128, 512])
    zeta_in = din("zeta", [128, HR])
    outT = nc.dram_tensor("outT", [D, T], F32, kind="ExternalOutput").ap()

    wbf = {}
    wfull = {}
    whalf = {}
    wpw = {}
    for nm, (K, N) in wdims.items():
        rows = K // 8
        wpw[nm] = 1024 if (N % 1024 == 0 and rows * 1024 * 2 <= (1 << 20)) else 512
        assert N % wpw[nm] == 0 and rows % 128 == 0
    for l in range(DEPTH):
        for nm, (K, N) in wdims.items():
            PW = wpw[nm]
            NP_ = N // PW
            wbf[(nm, l)] = dint("wb_%s%d" % (nm, l), [NP_ * (K // 8), PW], BF16)
            whalf[(nm, l)] = dint("wh_%s%d" % (nm, l), [NP_ * (K // 2), PW], BF16)
            wfull[(nm, l)] = dint("wf_%s%d" % (nm, l), [NP_ * K, PW], BF16)
    BO = [0, 1, 4, 5, 2, 3, 6, 7]

    def kperm(nm, kcp):
        rb = wdims[nm][0] // 8 // 128
        return BO[kcp // rb] * rb + kcp % rb

    def wpanel(nm, l, c0, w):
        K, N = wdims[nm]
        PW = wpw[nm]
        pn, off = c0 // PW, c0 % PW
        assert off + w <= PW
        reg = wfull[(nm, l)][pn * K:(pn + 1) * K, off:off + w]
        return reg.rearrange("(kc p) n -> p kc n", p=128)

    xs = dint("xs", [D, T], F32)
    NB4L = 9 * NCD * 4
    modp = [dint("modp%d" % l, [128, NB4L], F32) for l in range(DEPTH)]
    modg = [dint("modg%d" % l, [8 * 128, NB4L], F32) for l in range(DEPTH)]
    modh = [dint("modh%d" % l, [4 * 128, NB4L], F32) for l in range(DEPTH)]
    qr_s = dint("qr_s", [RW, T], BF16)
    kr_s = dint("kr_s", [RW, T], BF16)
    vr_s = dint("vr_s", [T, RW], BF16)
    rg_s = dint("rg_s", [RW, T], BF16)
    qs_s = dint("qs_s", [SW, T], BF16)
    kx_own = dint("kx_own", [SW, T], BF16)
    kx_pair = dint("kx_pair", [2 * SW, T], BF16)
    vx_own = dint("vx_own", [T, SW], BF16)
    vx_pair = dint("vx_pair", [2 * T, SW], BF16)
    st_own = dint("st_own", [RW, 128], F32)
    st_pair = dint("st_pair", [2 * RW, 128], F32)
    mix_s = dint("mix_s", [D, T], BF16)

    ARENA_BYTES = 204 * 1024
    arena = es.enter_context(nc.sbuf_tensor("arena", [128, ARENA_BYTES // 4], F32))
    arena_ap = arena[:]
    state = {"off": 0, "base": 0}

    def alloc(shape_free, dt, parts=128):
        n = 1
        for s in shape_free:
            n *= s
        nbytes = n * (4 if dt == F32 else 2)
        nbytes = (nbytes + 31) // 32 * 32
        off = state["off"]
        state["off"] += nbytes
        assert state["off"] <= ARENA_BYTES, ("SBUF arena overflow", state["off"])
        v = arena_ap[0:parts, off // 4:(off + nbytes) // 4]
        if dt == BF16:
            v = v.bitcast(BF16)
        v = v[:, 0:n]
        if len(shape_free) == 2:
            v = v.rearrange("p (a b) -> p a b", b=shape_free[1])
        elif len(shape_free) == 3:
            v = v.rearrange("p (a b c) -> p a b c", b=shape_free[1], c=shape_free[2])
        return v

    def phase_reset():
        state["off"] = state["base"]

    psum = [es.enter_context(nc.psum_tensor("ps%d" % i, [128, 512], F32)) for i in range(8)]
    PS = [p[:] for p in psum]

    def mm(out, lhsT, rhs, start, stop, reads, writes):
        tr.add("pe", lambda e: e.matmul(out, lhsT, rhs, start=start, stop=stop), reads, writes)

    def transp(out, in_, ident, reads, writes):
        tr.add("pe", lambda e: e.transpose(out, in_, ident), reads, writes)

    def act(out, in_, func, reads, writes, bias=None, scale=None):
        kw = {}
        if bias is not None:
            kw["bias"] = bias
        if scale is not None:
            kw["scale"] = scale
        tr.add("act", lambda e: e.activation(out, in_, func, **kw), reads, writes)

    def tt(eng, out, in0, in1, op, reads, writes):
        tr.add(eng, lambda e: e.tensor_tensor(out, in0, in1, op), reads, writes)

    def ts(eng, out, in0, s1, s2, op0, op1, reads, writes):
        if op1 is None:
            tr.add(eng, lambda e: e.tensor_scalar(out, in0, s1, None, op0), reads, writes)
        else:
            tr.add(eng, lambda e: e.tensor_scalar(out, in0, s1, s2, op0, op1), reads, writes)

    def stt(out, in0, scalar, in1, op0, op1, reads, writes):
        tr.add("dve", lambda e: e.scalar_tensor_tensor(out, in0, scalar, in1, op0, op1), reads, writes)

    def cp(eng, out, in_, reads, writes):
        tr.add(eng, lambda e: e.tensor_copy(out, in_), reads, writes)

    def dma(eng, out, in_, reads, writes, sem, prefetch=False):
        tr.add(eng, lambda e: e.dma_start(out=out, in_=in_), reads, writes, dma_sem=sem, prefetch=prefetch)

    def memset(eng, ap, val, writes):
        tr.add(eng, lambda e: e.memset(ap, val), (), writes)

    def allgather(groups, src, dst, reads, writes, name, prefetch=False):
        tr.add("pool", lambda e: e.collective_compute("AllGather", ALU.bypass, replica_groups=groups,
                                                      ins=[src.opt()], outs=[dst.opt()]),
               reads, writes, dma_sem=("cc",), inc=1, prefetch=prefetch)

    cmat = alloc([5, 128], BF16)
    PERM, NEGTRI, NEGONES, IDENT, ONES = (cmat[:, i, :] for i in range(5))
    negm = alloc([4, 512], BF16)
    modT = alloc([DEPTH * 9 * NCD], F32)
    ngT = alloc([DEPTH * 3 * NCD], F32)
    Amod = alloc([DEPTH * 3 * NCD], F32)
    Gmod = alloc([DEPTH * 3 * NCD], F32)
    rgnT = alloc([DEPTH * HR], F32)
    sgnT = alloc([DEPTH * HS], F32)
    fgT = alloc([NCD], F32)
    zeta = alloc([HR], F32)
    flag = alloc([1], F32)
    onec = alloc([1], F32)
    epsc = alloc([1], F32)
    state["base"] = state["off"]

    dma("pool", cmat.rearrange("p a b -> p (a b)"), cmat_in, (), [("cmat",)], "c0")
    dma("pool", negm.rearrange("p a b -> p (a b)"), negm_in, (), [("negm",)], "c1")
    dma("sp", ngT, ngT_in, (), [("ngT",)], "c2")
    dma("sp", rgnT, rgnT_in, (), [("rgnT",)], "c3")
    dma("sp", sgnT, sgnT_in, (), [("sgnT",)], "c4")
    dma("sp", fgT, fgT_in, (), [("fgT",)], "c5")
    dma("sp", zeta, zeta_in, (), [("zeta",)], "c6")
    dma("sp", flag, flag_in, (), [("flag",)], "c7")
    memset("dve", onec, 1.0, [("onec",)])
    memset("dve", epsc, EPS, [("epsc",)])

    def rsqrt(out, in_, mult, reads, wkey):
        act(out, in_, AF.Ln, list(reads) + [("epsc",)], [wkey], bias=epsc[:, 0:1], scale=mult)
        act(out, out, AF.Exp, [wkey], [wkey], scale=-0.5)

    def prep_weight(nm, l, what="both"):
        if getattr(c, "no_prep", False):
            return
        K, N = wdims[nm]
        rows = K // 8
        PW = wpw[nm]
        NP_ = N // PW
        tot = NP_ * rows
        src = wsh[nm][l * tot:(l + 1) * tot, :]
        dst = wbf[(nm, l)]
        wh, wf = whalf[(nm, l)], wfull[(nm, l)]
        for pn in range(NP_ if what in ("both", "cast") else 0):
            r0 = pn * rows
            for q0 in range(0, rows, 128):
                dma("pool", dst[r0 + q0:r0 + q0 + 128, :], src[r0 + q0:r0 + q0 + 128, :], (), [("wbf", nm, l, pn, q0)],
                    ("wcast", nm, l), prefetch=True)
        if what == "cast":
            return
        porder = list(range(NP_))
        if nm.startswith("gu"):
            porder = [p for pr in zip(range(NP_ // 2), range(NP_ // 2, NP_)) for p in pr]
        for pn in porder:
            r0 = pn * rows
            keys = [("wbf", nm, l, p2, q0) for p2 in range(NP_) for q0 in range(0, rows, 128)]
            allgather(QUADS, dst[r0:r0 + rows, :], wh[pn * 4 * rows:(pn + 1) * 4 * rows, :], keys,
                      [("whalf", nm, l, pn)], ("wq", nm, l, pn), prefetch=True)
            for hh in range(2):
                allgather(STR4, wh[pn * 4 * rows + hh * 2 * rows:pn * 4 * rows + (hh + 1) * 2 * rows, :],
                          wf[pn * K + hh * 4 * rows:pn * K + (hh + 1) * 4 * rows, :], [("whalf", nm, l, pn)],
                          [("wfull", nm, l, pn, hh)], ("w", nm, l, pn, hh), prefetch=True)

    def wkeys(nm, l, c0, w):
        pn = c0 // wpw[nm]
        return [("wfull", nm, l, pn, 0), ("wfull", nm, l, pn, 1)]

    def phase_mod():
        phase_reset()
        KAP, KAC = c.KAP, c.KAC
        NBLK = c.NMOD // 128
        cact = alloc([KAC * 4], F32, parts=KAP)
        cactb = alloc([KAC * 4], BF16, parts=KAP)
        dma("sp", cact, cT, (), [("cact",)], "m0")
        act(cact, cact, AF.Silu, [("cact",)], [("cact",)])
        cp("dve", cactb, cact, [("cact",)], [("cactb",)])
        NPAN = 3
        pan = [alloc([KAC, 512], BF16, parts=KAP) for _ in range(NPAN)]
        stg = [alloc([512], F32) for _ in range(2)]
        acc = alloc([NBLK * 4], F32)
        rk = [alloc([NB4L], F32) for _ in range(2)]
        sel4 = alloc([4], F32)
        adabT = alloc([NBLK], F32)
        dma("sp", sel4, sel4_in, (), [("sel4",)], "m1")
        dma("sp", adabT, adabT_in, (), [("adabT",)], "m2")
        ci = 0
        NBL = 9 * NCD
        for l in range(DEPTH):
            modp_keys = []
            for n0 in range(0, 9 * D, 512):
                s = ci % NPAN
                src = adaw[l * c.KA:(l + 1) * c.KA, n0:n0 + 512].rearrange("(kc p) n -> p kc n", p=KAP)
                dma("pool", pan[s], src, (), [("apan", s)], ("apan", s))
                for q in range(4):
                    blk = n0 // 128 + q
                    gb = l * NBL + blk
                    bank = (gb // 128) % 2
                    o = (gb % 128) * 4
                    for kc in range(KAC):
                        mm(PS[bank][0:128, o:o + 4], pan[s][:, kc, q * 128:(q + 1) * 128], cactb[:, kc * 4:(kc + 1) * 4],
                           kc == 0, kc == KAC - 1, [("apan", s), ("cactb",)], [("ps", bank)])
                    if gb % 128 == 127 or blk == NBL - 1:
                        g0 = max((gb // 128) * 128, l * NBL)
                        nb = gb - g0 + 1
                        b0 = g0 - l * NBL
                        sgi = (gb // 128) % 2
                        po = (g0 % 128) * 4
                        cp("dve", stg[sgi][:, 0:nb * 4], PS[bank][:, po:po + nb * 4], [("ps", bank)], [("stg", sgi)])
                        dma("sp", modp[l][:, b0 * 4:(b0 + nb) * 4], stg[sgi][:, 0:nb * 4], [("stg", sgi)],
                            [("modp", l, b0)], ("stg", sgi))
                        modp_keys.append(("modp", l, b0))
                ci += 1
            allgather(QUADS, modp[l], modh[l], modp_keys, [("modh", l)], ("modq", l))
            for hh in range(2):
                allgather(STR4, modh[l][hh * 256:(hh + 1) * 256, :], modg[l][hh * 512:(hh + 1) * 512, :],
                          [("modh", l)], [("modg", l, hh)], ("mod", l, hh))
        for l in range(DEPTH):
            accl = acc[:, l * NB4L:(l + 1) * NB4L]
            mk = [("modg", l, 0), ("modg", l, 1)]
            dma("sp", accl, modg[l][0:128, :], mk, [("acc", l)], ("m3", l))
            for r in range(1, 8):
                ri = (l * 7 + r) % 2
                dma("sp", rk[ri][:, 0:NB4L], modg[l][r * 128:(r + 1) * 128, :], mk, [("rk", ri)], ("rk", ri))
                tt("dve", accl, accl, rk[ri][:, 0:NB4L], ALU.add, [("acc", l), ("rk", ri)], [("acc", l)])
        acc3 = acc.rearrange("p (n b) -> p n b", b=4)
        akeys = [("acc", l) for l in range(DEPTH)]
        ts("dve", modT, acc3[:, :, 0], sel4[:, 0:1], None, ALU.mult, None, akeys + [("sel4",)], [("modT",)])
        for bb in range(1, 4):
            stt(modT, acc3[:, :, bb], sel4[:, bb:bb + 1], modT, ALU.mult, ALU.add, akeys + [("sel4",), ("modT",)],
                [("modT",)])
        tt("dve", modT, modT, adabT, ALU.add, [("modT",), ("adabT",)], [("modT",)])
        for l in range(DEPTH):
            for j in range(3):
                base = (l * 9 + 3 * j) * NCD
                sc_ = modT[:, base + NCD:base + 2 * NCD]
                gt_ = modT[:, base + 2 * NCD:base + 3 * NCD]
                o = (l * 3 + j) * NCD
                stt(Amod[:, o:o + NCD], sc_, 1.0, ngT[:, o:o + NCD], ALU.add, ALU.mult,
                    [("modT",), ("ngT",)], [("Amod",)])
                ts("dve", Gmod[:, o:o + NCD], gt_, 0.5 if j != 1 else 1.0, None, ALU.mult, None,
                   [("modT",)], [("Gmod",)])

    def mod_cols(l, j):
        o = (l * 3 + j) * NCD
        base = (l * 9 + 3 * j) * NCD
        return (lambda ch: Amod[:, o + ch:o + ch + 1], lambda ch: modT[:, base + ch:base + ch + 1],
                lambda ch: Gmod[:, o + ch:o + ch + 1])

    def norm_tile(src, t0, hbuf, bufs, Acol, Bcol, out_f32=None, tagp="n"):
        xc, sq, tmp, rstd = bufs["xc"], bufs["sq"], bufs["tmp"], bufs["rstd"]
        nx = len(xc)
        NP = getattr(c, "norm_parts", 9)
        for ch in range(NCD):
            s = ch % nx
            dma("sp", xc[s], src[ch * 128:(ch + 1) * 128, t0:t0 + TT], [("xs", ch, t0)], [("xc", s)], ("xc", s))
            if NP >= 1:
                act(sq[ch % 2], xc[s], AF.Square, [("xc", s)], [("sq", ch % 2)])
            if NP >= 2:
                mm(PS[7], ONES, sq[ch % 2], ch == 0, ch == NCD - 1, [("sq", ch % 2), ("cmat",)], [("ps", 7)])
        if NP >= 3:
            rsqrt(rstd, PS[7], 1.0 / D, [("ps", 7)], ("rstd",))
        for ch in range(NCD if NP >= 4 else 0):
            s = ch % nx
            dma("sp", xc[s], src[ch * 128:(ch + 1) * 128, t0:t0 + TT], [("xs", ch, t0)], [("xc", s)], ("xc", s))
            if out_f32 is None:
                tt("dve", tmp[ch % 2], xc[s], rstd, ALU.mult, [("xc", s), ("rstd",)], [("tmp", ch % 2)])
                act(hbuf[:, ch, :], tmp[ch % 2], AF.Identity, [("tmp", ch % 2), ("Amod",), ("modT",)], [("hbuf", ch)],
                    bias=Bcol(ch), scale=Acol(ch))
            else:
                stt(tmp[ch % 2], xc[s], Acol(ch), rstd, ALU.mult, ALU.mult, [("xc", s), ("rstd",), ("fgT",)],
                    [("tmp", ch % 2)])
                dma("pool", out_f32[ch * 128:(ch + 1) * 128, t0:t0 + TT], tmp[ch % 2], [("tmp", ch % 2)],
                    [("out", ch, t0)], ("tmpst", ch % 2))

    def residual_out(ps_bank, ch, t0, Gcol, xo, oi, src, dst):
        s = oi % len(xo)
        dma("sp", xo[s], src[ch * 128:(ch + 1) * 128, t0:t0 + TT], [("xs", ch, t0)], [("xo", s)], ("xo", s))
        stt(xo[s], PS[ps_bank], Gcol(ch), xo[s], ALU.mult, ALU.add, [("ps", ps_bank), ("xo", s), ("Gmod",)],
            [("xo", s)])
        dma("act", dst[ch * 128:(ch + 1) * 128, t0:t0 + TT], xo[s], [("xo", s)], [("xs", ch, t0)], ("xost", s))

    def phase_ffn(l, which, src):
        phase_reset()
        j = 0 if which == 1 else 2
        Acol, Bcol, Gcol = mod_cols(l, j)
        gun, dn = "gu%d" % which, "d%d" % which
        actb = alloc([NFC, TT], BF16)
        hbuf = alloc([NCD, TT], BF16)
        WSLOT = max(NCD * 2 * 256, NFC * 256)
        wsl = [alloc([WSLOT], BF16) for _ in range(2)]
        bufs = {"xc": [alloc([TT], F32) for _ in range(2)], "sq": [alloc([TT], BF16) for _ in range(2)],
                "tmp": [alloc([TT], F32) for _ in range(2)], "rstd": alloc([TT], F32)}
        sg = [alloc([TT], F32) for _ in range(2)]
        xo = [alloc([TT], F32) for _ in range(2)]
        wi = 0
        oi = 0
        pb = 0
        for t_i in range(NT):
            t0 = t_i * TT
            norm_tile(src, t0, hbuf, bufs, Acol, Bcol)
            for m in range(NFC if getattr(c, "ffn_parts", 3) >= 2 else 0):
                if m % 2 == 0:
                    s = wi % 2
                    wi += 1
                    wv = wsl[s][:, 0:NCD * 512].rearrange("p (kc g n) -> p kc g n", g=2, n=256)
                    c0 = (m // 2) * 256
                    dma("sp", wv[:, :, 0, :], wpanel(gun, l, c0, 256), wkeys(gun, l, c0, 256), [("wsl", s)],
                        ("wsl", s))
                    dma("sp", wv[:, :, 1, :], wpanel(gun, l, F + c0, 256), wkeys(gun, l, F + c0, 256),
                        [("wsl", s)], ("wsl", s))
                o = (m % 2) * 128
                bg, bu = pb % 6, (pb + 1) % 6
                pb += 2
                for kc in range(NCD):
                    mm(PS[bg], wv[:, kc, 0, o:o + 128], hbuf[:, kperm(gun, kc), :], kc == 0, kc == NCD - 1,
                       [("wsl", s)] + ([("hbuf", kk) for kk in range(NCD)] if kc == 0 else []), [("ps", bg)])
                for kc in range(NCD):
                    mm(PS[bu], wv[:, kc, 1, o:o + 128], hbuf[:, kperm(gun, kc), :], kc == 0, kc == NCD - 1, [("wsl", s)],
                       [("ps", bu)])
                act(sg[m % 2], PS[bg], AF.Silu, [("ps", bg)], [("sg", m % 2)])
                tt("dve", actb[:, m, :], sg[m % 2], PS[bu], ALU.mult, [("sg", m % 2), ("ps", bu)], [("actb", m)])
            for dc in range(NCD if getattr(c, "ffn_parts", 3) >= 3 else 0):
                if dc % 2 == 0:
                    s = wi % 2
                    wi += 1
                    wv2 = wsl[s][:, 0:NFC * 256].rearrange("p (kc n) -> p kc n", n=256)
                    c0 = (dc // 2) * 256
                    dma("sp", wv2, wpanel(dn, l, c0, 256), wkeys(dn, l, c0, 256), [("wsl", s)], ("wsl", s))
                o = (dc % 2) * 128
                bo = pb % 6
                pb += 1
                for kc in range(NFC):
                    mm(PS[bo], wv2[:, kc, o:o + 128], actb[:, kperm(dn, kc), :], kc == 0, kc == NFC - 1,
                       [("wsl", s)] + ([("actb", kk) for kk in range(NFC)] if kc == 0 else []), [("ps", bo)])
                residual_out(bo, dc, t0, Gcol, xo, oi, src, xs)
                oi += 1
        tr.barrier()

    def phase_mix_in(l):
        phase_reset()
        Acol, Bcol, _ = mod_cols(l, 1)
        hbuf = alloc([NCD, TT], BF16)
        wsl = [alloc([NCD, 512], BF16) for _ in range(2)]
        bufs = {"xc": [alloc([TT], F32) for _ in range(2)], "sq": [alloc([TT], BF16) for _ in range(2)],
                "tmp": [alloc([TT], F32) for _ in range(2)], "rstd": alloc([TT], F32)}
        rope_t = alloc([4, TT], F32)
        qsb = [alloc([TT], BF16) for _ in range(2)]
        t1 = [alloc([TT], F32) for _ in range(2)]
        t2 = [alloc([TT], F32) for _ in range(2)]
        ob = [alloc([512], BF16) for _ in range(4)]
        wi = 0
        pb = 0
        oi = 0
        DK = 128.0 ** -0.5
        for t_i in range(NT):
            t0 = t_i * TT
            norm_tile(xs, t0, hbuf, bufs, Acol, Bcol)
            dma("sp", rope_t[:, 0:2, :], ropeq[:, t0:t0 + TT].rearrange("(a p) t -> p a t", p=128), (), [("rope",)],
                ("rope",))
            dma("sp", rope_t[:, 2:4, :], ropek[:, t0:t0 + TT].rearrange("(a p) t -> p a t", p=128), (), [("rope",)],
                ("rope",))
            hreads = [("hbuf", kk) for kk in range(NCD)]
            for pn in range(INW // 512):
                s = wi % 2
                wi += 1
                col0 = pn * 512
                dma("sp", wsl[s], wpanel("win", l, col0, 512), wkeys("win", l, col0, 512), [("wsl", s)], ("wsl", s))
                seg = col0 // RW if col0 < 4 * RW else 4 + (col0 - 4 * RW) // SW
                if seg not in getattr(c, "segs", (0, 1, 2, 3, 4, 5, 6)):
                    continue
                if seg in (2, 6):
                    for tb in range(TT // 128):
                        bo = pb % 5
                        pb += 1
                        for kc in range(NCD):
                            mm(PS[bo], hbuf[:, kperm("win", kc), tb * 128:(tb + 1) * 128], wsl[s][:, kc, :], kc == 0,
                               kc == NCD - 1,
                               [("wsl", s)] + (hreads if kc == 0 else []), [("ps", bo)])
                        o_ = ob[oi % 4]
                        okey = ("ob", oi % 4)
                        if oi % 2 == 0:
                            act(o_, PS[bo], AF.Identity, [("ps", bo)], [okey])
                        else:
                            cp("dve", o_, PS[bo], [("ps", bo)], [okey])
                        r0 = t0 + tb * 128
                        if seg == 2:
                            cc0 = col0 - 2 * RW
                            dma("act", vr_s[r0:r0 + 128, cc0:cc0 + 512], o_, [okey], [("vr_s", r0, cc0)],
                                ("obst", oi % 4))
                        else:
                            cc0 = col0 - 4 * RW - 2 * SW
                            dma("act", vx_own[r0:r0 + 128, cc0:cc0 + 512], o_, [okey], [("vx_own",)],
                                ("obst", oi % 4))
                        oi += 1
                    continue
                for q in range(4):
                    col = col0 + q * 128
                    bo = pb % 5
                    pb += 1
                    for kc in range(NCD):
                        mm(PS[bo], wsl[s][:, kc, q * 128:(q + 1) * 128], hbuf[:, kperm("win", kc), :], kc == 0,
                           kc == NCD - 1, [("wsl", s)] + (hreads if kc == 0 else []), [("ps", bo)])
                    o_ = ob[oi % 4]
                    okey = ("ob", oi % 4)
                    if seg in (0, 1):
                        ri = 0 if seg == 0 else 2
                        qi = oi % 2
                        RM = getattr(c, "rope_mode", 0)
                        if RM != 2:
                            act(qsb[qi], PS[bo], AF.Identity, [("ps", bo)], [("qsb", qi)])
                        if RM == 0:
                            mm(PS[5 + qi], PERM, qsb[qi], True, True, [("qsb", qi), ("cmat",)], [("ps", 5 + qi)])
                        tt("dve", t1[qi], PS[bo], rope_t[:, ri, :], ALU.mult,
                           [("ps", bo), ("rope",)] + ([("qsb", qi)] if RM != 2 else []), [("t1", qi)])
                        if RM == 0:
                            tt("dve", t2[qi], PS[5 + qi], rope_t[:, ri + 1, :], ALU.mult, [("ps", 5 + qi), ("rope",)],
                               [("t2", qi)])
                        else:
                            tt("dve", t2[qi], PS[bo], rope_t[:, ri + 1, :], ALU.mult, [("ps", bo), ("rope",)],
                               [("t2", qi)])
                        tt(getattr(c, "rope_eng", "dve"), o_, t1[qi], t2[qi], ALU.add, [("t1", qi), ("t2", qi)], [okey])
                        dstt = qr_s if seg == 0 else kr_s
                        r0 = col - seg * RW
                        dma("act", dstt[r0:r0 + 128, t0:t0 + TT], o_, [okey], [("qk_s", seg, r0, t0)],
                            ("obst", oi % 4))
                    elif seg == 3:
                        act(o_, PS[bo], AF.Silu, [("ps", bo)], [okey])
                        r0 = col - 3 * RW
                        dma("act", rg_s[r0:r0 + 128, t0:t0 + TT], o_, [okey], [("rg_s", r0, t0)], ("obst", oi % 4))
                    elif seg == 4:
                        act(o_, PS[bo], AF.Identity, [("ps", bo)], [okey], scale=DK)
                        r0 = col - 4 * RW
                        dma("act", qs_s[r0:r0 + 128, t0:t0 + TT], o_, [okey], [("qs_s", r0, t0)], ("obst", oi % 4))
                    else:
                        cp("dve", o_, PS[bo], [("ps", bo)], [okey])
                        r0 = col - 4 * RW - SW
                        dma("act", kx_own[r0:r0 + 128, t0:t0 + TT], o_, [okey], [("kx_own",)], ("obst", oi % 4))
                    oi += 1
        tr.barrier()

    GAM = gammas(HR)

    def phase_ret(l, outputs):
        phase_reset()
        kT = [alloc([T], BF16) for _ in range(2)]
        vt = [alloc([NB, 128], BF16) for _ in range(2)]
        S = [alloc([128], F32) for _ in range(2)]
        Sb = [alloc([128], BF16) for _ in range(2)]
        kz = [alloc([128], BF16) for _ in range(2)]
        if outputs:
            qT = [alloc([T], BF16) for _ in range(2)]
            dm4 = [alloc([512], F32) for _ in range(2)]
            xi4 = [alloc([512], F32) for _ in range(2)]
            sc = [alloc([512], BF16) for _ in range(2)]
            qx = [alloc([512], BF16) for _ in range(2)]
            sinit = alloc([128], F32)
            o_bf = alloc([512], BF16)
            osq = alloc([512], BF16)
            mean = alloc([512], F32)
            m2 = alloc([512], F32)
            var = alloc([512], F32)
            cen = alloc([512], F32)
            yb = alloc([512], F32)
            rgt = [alloc([512], BF16) for _ in range(2)]
            outb = [alloc([512], BF16) for _ in range(2)]
        si = 0
        gi = 0
        for h in range(HR):
            hs = h % 2
            cd = GAM[h] ** 128
            dma("sp", kT[hs], kr_s[h * 128:(h + 1) * 128, :], [("qk_s_all",)], [("kT", hs)], ("kT", hs))
            dma("sp", vt[hs], vr_s[:, h * 128:(h + 1) * 128].rearrange("(n p) d -> p n d", p=128), [("vr_all",)],
                [("vt", hs)], ("vt", hs))
            if outputs:
                dma("sp", qT[hs], qr_s[h * 128:(h + 1) * 128, :], [("qk_s_all",)], [("qT", hs)], ("qT", hs))
                dma("sp", dm4[hs], dmask4_in[h * 128:(h + 1) * 128, :], (), [("dm4", hs)], ("dm4", hs))
                dma("sp", xi4[hs], xi4_in[h * 128:(h + 1) * 128, :], (), [("xi4", hs)], ("xi4", hs))
                dma("sp", sinit, st_pair[h * 128:(h + 1) * 128, :], [("st_pair",)], [("sinit",)], ("sinit",))
                ts("dve", S[si % 2], sinit, flag[:, 0:1], None, ALU.mult, None, [("sinit",), ("flag",)],
                   [("S", si % 2)])
                cp("pool", Sb[si % 2], S[si % 2], [("S", si % 2)], [("Sb", si % 2)])
            else:
                memset("dve", S[si % 2], 0.0, [("S", si % 2)])
            for gq in range(NG):
                if outputs:
                    for cc in range(4):
                        n = gq * 4 + cc
                        mm(PS[0][:, cc * 128:(cc + 1) * 128], kT[hs][:, n * 128:(n + 1) * 128],
                           qT[hs][:, n * 128:(n + 1) * 128], True, True, [("kT", hs), ("qT", hs)], [("ps", 0)])
                    g2 = gi % 2
                    tt("dve", sc[g2], PS[0], dm4[hs], ALU.mult, [("ps", 0), ("dm4", hs)], [("sc", g2)])
                    tt("pool", qx[g2], qT[hs][:, gq * 512:(gq + 1) * 512], xi4[hs], ALU.mult,
                       [("qT", hs), ("xi4", hs)], [("qx", g2)])
                for cc in range(4):
                    n = gq * 4 + cc
                    cur, nxt = si % 2, (si + 1) % 2
                    if outputs:
                        mm(PS[1][:, cc * 128:(cc + 1) * 128], vt[hs][:, n, :], sc[g2][:, cc * 128:(cc + 1) * 128],
                           True, False, [("vt", hs), ("sc", g2)], [("ps", 1)])
                        mm(PS[1][:, cc * 128:(cc + 1) * 128], Sb[cur], qx[g2][:, cc * 128:(cc + 1) * 128],
                           False, True, [("Sb", cur), ("qx", g2)], [("ps", 1)])
                    pk = PS[2 + si % 2].bitcast(BF16)[:, 0:128]
                    transp(pk, kT[hs][:, n * 128:(n + 1) * 128], IDENT, [("kT", hs), ("cmat",)], [("ps", 2 + si % 2)])
                    act(kz[si % 2], pk, AF.Identity, [("ps", 2 + si % 2), ("zeta",)], [("kz", si % 2)],
                        scale=zeta[:, h:h + 1])
                    mm(PS[4 + si % 2][:, 0:128], kz[si % 2], vt[hs][:, n, :], True, True, [("kz", si % 2), ("vt", hs)],
                       [("ps", 4 + si % 2)])
                    stt(S[nxt], S[cur], cd, PS[4 + si % 2][:, 0:128], ALU.mult, ALU.add,
                        [("S", cur), ("ps", 4 + si % 2)], [("S", nxt)])
                    if outputs:
                        cp("pool", Sb[nxt], S[nxt], [("S", nxt)], [("Sb", nxt)])
                    si += 1
                if outputs:
                    act(o_bf, PS[1], AF.Identity, [("ps", 1)], [("o_bf",)])
                    act(osq, PS[1], AF.Square, [("ps", 1)], [("osq",)])
                    mm(PS[6], ONES, o_bf, True, True, [("o_bf",), ("cmat",)], [("ps", 6)])
                    mm(PS[7], ONES, osq, True, True, [("osq",), ("cmat",)], [("ps", 7)])
                    ts("dve", mean, PS[6], 1.0 / 128, None, ALU.mult, None, [("ps", 6)], [("mean",)])
                    tt("pool", m2, mean, mean, ALU.mult, [("mean",)], [("m2",)])
                    stt(var, PS[7], 1.0 / 128, m2, ALU.mult, ALU.subtract, [("ps", 7), ("m2",)], [("var",)])
                    rsqrt(var, var, 1.0, [("var",)], ("var",))
                    tt("dve", cen, PS[1], mean, ALU.subtract, [("ps", 1), ("mean",)], [("cen",)])
                    stt(yb, cen, rgnT[:, l * HR + h:l * HR + h + 1], var, ALU.mult, ALU.mult,
                        [("cen",), ("var",), ("rgnT",)], [("yb",)])
                    dma("sp", rgt[g2], rg_s[h * 128:(h + 1) * 128, gq * 512:(gq + 1) * 512], [("rg_all",)],
                        [("rgt", g2)], ("rgt", g2))
                    tt("pool", outb[g2], yb, rgt[g2], ALU.mult, [("yb",), ("rgt", g2)], [("outb", g2)])
                    dma("pool", mix_s[h * 128:(h + 1) * 128, gq * 512:(gq + 1) * 512], outb[g2], [("outb", g2)],
                        [("mix_s", h, gq)], ("outbst", g2))
                    gi += 1
            if not outputs:
                dma("pool", st_own[h * 128:(h + 1) * 128, :], S[si % 2], [("S", si % 2)], [("st_own",)],
                    ("Sst", si % 2))
        tr.barrier()

    def phase_sb(l):
        phase_reset()
        qT = [alloc([T], BF16) for _ in range(2)]
        kTo = [alloc([T], BF16) for _ in range(2)]
        kTp = [alloc([T], BF16) for _ in range(2)]
        vo = [alloc([NB, 128], BF16) for _ in range(2)]
        vp = [alloc([NB, 128], BF16) for _ in range(2)]
        eb = [alloc([512], F32) for _ in range(2)]
        spb = [alloc([512], BF16) for _ in range(2)]
        R = [alloc([512], BF16) for _ in range(2)]
        ab = [alloc([512], BF16) for _ in range(2)]
        osq = alloc([512], BF16)
        rstd = alloc([512], F32)
        obuf = [alloc([512], BF16) for _ in range(2)]
        ui = 0
        gi = 0
        for h in range(HS):
            hs = h % 2
            dma("sp", qT[hs], qs_s[h * 128:(h + 1) * 128, :], [("qs_all",)], [("qT", hs)], ("qT", hs))
            dma("sp", kTo[hs], kx_own[h * 128:(h + 1) * 128, :], [("kx_own",)], [("kTo", hs)], ("kTo", hs))
            dma("sp", kTp[hs], kx_pair[h * 256:h * 256 + 128, :], [("kx_pair", h)], [("kTp", hs)], ("kTp", hs))
            dma("sp", vo[hs], vx_own[:, h * 128:(h + 1) * 128].rearrange("(n p) d -> p n d", p=128), [("vx_own",)],
                [("vo", hs)], ("vo", hs))
            vpk = []
            for cc_ in range(T // 256):
                dma("sp", vp[hs][:, 2 * cc_:2 * cc_ + 2, :],
                    vx_pair[cc_ * 512:cc_ * 512 + 256, h * 128:(h + 1) * 128].rearrange("(n p) d -> p n d", p=128),
                    [("vx_pair", cc_)], [("vp", hs)], ("vp", hs))
            ts("pool", vp[hs], vp[hs], flag[:, 0:1], None, ALU.mult, None, vpk + [("vp", hs), ("flag",)],
               vpk + [("vp", hs)])
            for g in range(NG):
                units = [("o", j) for j in range(4 * g + 3, -1, -1)] + [("p", j) for j in range(NB - 1, -1, -1)]
                qg = qT[hs][:, g * 512:(g + 1) * 512]
                po = 6 + gi % 2
                nun = len(units)
                ui0 = ui

                def unit_ctx(u):
                    kind, j = units[u]
                    u2 = (ui0 + u) % 2
                    kt = (kTo if kind == "o" else kTp)[hs][:, j * 128:(j + 1) * 128]
                    vv = (vo if kind == "o" else vp)[hs][:, j, :]
                    kkey = ("kTo", hs) if kind == "o" else ("kTp", hs)
                    vkey = ("vo", hs) if kind == "o" else ("vp", hs)
                    diag = kind == "o" and j >= 4 * g
                    return u2, kt, vv, kkey, vkey, diag, j - 4 * g

                def stage_a(u):
                    u2, kt, vv, kkey, vkey, diag, r = unit_ctx(u)
                    pz = u2
                    mm(PS[pz], kt, qg, True, not diag, [kkey, ("qT", hs)], [("ps", pz)])
                    if diag:
                        mm(PS[pz], IDENT, negm[:, r, :], False, True, [("cmat",), ("negm",)], [("ps", pz)])
                    act(eb[u2], PS[pz], AF.Exp, [("ps", pz)], [("eb", u2)])
                    act(spb[u2], eb[u2], AF.Ln, [("eb", u2), ("onec",)], [("spb", u2)], bias=onec[:, 0:1])

                def stage_b(u):
                    u2, kt, vv, kkey, vkey, diag, r = unit_ctx(u)
                    pzi = 2 + u2
                    mm(PS[pzi], NEGTRI, spb[u2], True, False, [("spb", u2), ("cmat",)], [("ps", pzi)])
                    if u > 0:
                        mm(PS[pzi], NEGONES, R[u2], False, False, [("R", u2)], [("ps", pzi)])
                    mm(PS[pzi], kt, qg, False, not diag, [kkey, ("qT", hs)], [("ps", pzi)])
                    if diag:
                        mm(PS[pzi], IDENT, negm[:, r, :], False, True, [("cmat",), ("negm",)], [("ps", pzi)])
                    if u + 1 < nun:
                        n2 = (ui0 + u + 1) % 2
                        if u == 0:
                            cp("pool", R[n2], spb[u2], [("spb", u2)], [("R", n2)])
                        else:
                            tt("pool", R[n2], R[u2], spb[u2], ALU.add, [("R", u2), ("spb", u2)], [("R", n2)])
                    act(ab[u2], PS[pzi], AF.Exp, [("ps", pzi)], [("ab", u2)])

                def stage_c(u):
                    u2, kt, vv, kkey, vkey, diag, r = unit_ctx(u)
                    mm(PS[po], vv, ab[u2], u == 0, u == nun - 1, [vkey, ("ab", u2)], [("ps", po)])

                for u in range(-1, nun + 1):
                    if u + 1 < nun:
                        stage_a(u + 1)
                    if 0 <= u < nun:
                        stage_b(u)
                    if 0 <= u - 1 < nun:
                        stage_c(u - 1)
                ui += nun
                g2 = gi % 2
                act(osq, PS[po], AF.Square, [("ps", po)], [("osq",)])
                mm(PS[4], ONES, osq, True, True, [("osq",), ("cmat",)], [("ps", 4)])
                rsqrt(rstd, PS[4], 1.0 / 128, [("ps", 4)], ("rstd",))
                stt(obuf[g2], PS[po], sgnT[:, l * HS + h:l * HS + h + 1], rstd, ALU.mult, ALU.mult,
                    [("ps", po), ("rstd",), ("sgnT",)], [("obuf", g2)])
                dma("pool", mix_s[RW + h * 128:RW + (h + 1) * 128, g * 512:(g + 1) * 512], obuf[g2], [("obuf", g2)],
                    [("mix_s", "sb", h, g)], ("obufst", g2))
                gi += 1
        tr.barrier()

    def phase_mix_out(l):
        phase_reset()
        _, _, Gcol = mod_cols(l, 1)
        hbuf = alloc([NCD, TT], BF16)
        wsl = [alloc([NCD, 512], BF16) for _ in range(2)]
        xo = [alloc([TT], F32) for _ in range(3)]
        wi = 0
        oi = 0
        pb = 0
        for t_i in range(NT):
            t0 = t_i * TT
            dma("sp", hbuf, mix_s[:, t0:t0 + TT].rearrange("(kc p) t -> p kc t", p=128), [("mix_all",)], [("hbuf",)],
                ("hbuf",))
            for dc in range(NCD):
                if dc % 4 == 0:
                    s = wi % 2
                    wi += 1
                    c0 = (dc // 4) * 512
                    w_ = min(512, D - c0)
                    dma("sp", wsl[s][:, :, 0:w_], wpanel("wout", l, c0, w_), wkeys("wout", l, c0, w_), [("wsl", s)],
                        ("wsl", s))
                o = (dc % 4) * 128
                bo = pb % 6
                pb += 1
                for kc in range(NCD):
                    mm(PS[bo], wsl[s][:, kc, o:o + 128], hbuf[:, kperm("wout", kc), :], kc == 0, kc == NCD - 1,
                       [("wsl", s), ("hbuf",)], [("ps", bo)])
                residual_out(bo, dc, t0, Gcol, xo, oi, xs, xs)
                oi += 1
        tr.barrier()

    def phase_final():
        phase_reset()
        bufs = {"xc": [alloc([TT], F32) for _ in range(3)], "sq": [alloc([TT], BF16) for _ in range(2)],
                "tmp": [alloc([TT], F32) for _ in range(2)], "rstd": alloc([TT], F32)}
        for t_i in range(NT):
            norm_tile(xs, t_i * TT, None, bufs, lambda ch: fgT[:, ch:ch + 1], None, out_f32=outT)
        tr.barrier()

    order = ["gu1", "d1", "win", "wout", "gu2", "d2"]
    stop = getattr(c, "stop", 10 ** 9)
    cnt = {"n": 0}

    def go():
        cnt["n"] += 1
        return cnt["n"] <= stop

    def body():
        prep_weight("gu1", 0, "cast")
        prep_weight("d1", 0, "cast")
        if not go():
            return
        if not getattr(c, "no_mod", False):
            phase_mod()
        prep_weight("gu1", 0, "gather")
        prep_weight("d1", 0, "gather")
        tr.barrier()
        prep_weight("win", 0)
        for l in range(DEPTH):
            if not go():
                return
            phase_ffn(l, 1, xT if l == 0 else xs)
            if l == 0 and not getattr(c, "skip_prep", False):
                for nm in order[3:]:
                    prep_weight(nm, 0)
            if not go():
                return
            phase_mix_in(l)
            if not go():
                return
            phase_ret(l, False)
            for h in range(HS):
                allgather(PAIRS, kx_own[h * 128:(h + 1) * 128, :], kx_pair[h * 256:(h + 1) * 256, :], [("kx_own",)],
                          [("kx_pair", h)], ("kx", l, h))
            for cc_ in range(T // 256):
                allgather(PAIRS, vx_own[cc_ * 256:(cc_ + 1) * 256, :], vx_pair[cc_ * 512:(cc_ + 1) * 512, :],
                          [("vx_own",)], [("vx_pair", cc_)], ("vx", l, cc_))
            assert RW * 128 * 4 <= (1 << 20)
            allgather(PAIRS, st_own, st_pair, [("st_own",)], [("st_pair",)], ("st", l))
            tr.barrier()
            if not go():
                return
            phase_sb(l)
            if not go():
                return
            phase_ret(l, True)
            if l + 1 < DEPTH and not getattr(c, "skip_prep", False):
                for nm in order:
                    prep_weight(nm, l + 1)
            if not go():
                return
            phase_mix_out(l)
            if not go():
                return
            phase_ffn(l, 2, xs)
        if not go():
            return
        phase_final()

    body()
    tr.barrier(everything=True)
    tr.add("pool", lambda e: e.dma_start(out=modp[0][0:1, 0:16], in_=modp[0][1:2, 0:16]), (), [("fin",)], dma_sem=("fin",))
    tr.add("pool", lambda e: e.dma_start(out=modp[0][2:3, 0:16], in_=modp[0][3:4, 0:16]), [("fin",)], [("fin2",)],
           dma_sem=("fin2",))
    tr.emit(nc, es)
    es.close()
    return nc, tr


def host_constants(cfg, core):
    c = cfg
    T, HR = c.T, c.HR
    half = core % 2
    pos = (half * T + np.arange(T)).astype(np.float32)
    halfd = 64
    inv = (ROPE_BASE ** (-np.arange(halfd, dtype=np.float32) / halfd)).astype(np.float32)
    ang = (pos[None, :] * inv[:, None]).astype(np.float32)
    cos = np.cos(ang.astype(np.float64)).astype(np.float32)
    sin = np.sin(ang.astype(np.float64)).astype(np.float32)
    cos2 = np.concatenate([cos, cos], 0)
    sin2 = np.concatenate([-sin, sin], 0)
    dk = np.float32(128.0 ** -0.5)
    ropeq = np.concatenate([cos2, sin2], 0).astype(np.float32)
    ropek = (np.concatenate([cos2, sin2], 0) * dk).astype(np.float32)
    k = np.arange(128)[:, None]
    m = np.arange(128)[None, :]
    perm = (k == (m + 64) % 128).astype(np.float32)
    negtri = -(k >= m).astype(np.float32)
    negones = -np.ones((128, 128), np.float32)
    ident = np.eye(128, dtype=np.float32)
    ones = np.ones((128, 128), np.float32)
    cmat = np.concatenate([perm, negtri, negones, ident, ones], 1)
    s = np.arange(128)[:, None]
    t = np.arange(512)[None, :]
    negm = np.concatenate([np.where(t > r * 128 + s, 0.0, NEG_BIG).astype(np.float32) for r in range(4)], 1)
    lg = np.log1p(-np.exp2(-5.0 - np.arange(HR, dtype=np.float64)))
    idx = np.arange(128, dtype=np.float64)
    diff = idx[None, :] - idx[:, None]
    dm = np.where(diff >= 0, np.exp(lg[:, None, None] * np.maximum(diff, 0.0)[None]), 0.0)
    dmask4 = np.tile(dm, (1, 1, 4)).reshape(HR * 128, 512).astype(np.float32)
    xi = np.exp(lg[:, None] * (idx[None, :] + 1.0))
    xi4 = np.tile(np.tile(xi, (1, 4))[:, None, :], (1, 128, 1)).reshape(HR * 128, 512).astype(np.float32)
    zeta = np.exp(lg[:, None] * (127.0 - idx[None, :])).T.astype(np.float32)
    return dict(ropeq=ropeq, ropek=ropek, cmat=cmat, negm=negm, dmask4=dmask4, xi4=xi4,
                zeta=np.ascontiguousarray(zeta))


def make_in_maps(cfg, x, c, ada_w, ada_b, norm_g, w_in, ret_gn_g, sb_norm_g, w_out,
                 ffn1_w_gu, ffn1_w_down, ffn2_w_gu, ffn2_w_down, final_g):
    cf = cfg
    D, T, NCD, DEPTH, HR, HS = cf.D, cf.T, cf.NCD, cf.DEPTH, cf.HR, cf.HS
    f = lambda a: np.ascontiguousarray(np.asarray(a, dtype=np.float32))
    x, c, ada_w, ada_b, norm_g = f(x), f(c), f(ada_w), f(ada_b), f(norm_g)
    ngT = f(norm_g.reshape(DEPTH * 3 * NCD, 128).T)
    rgnT = f(np.asarray(ret_gn_g, np.float32).reshape(DEPTH * HR, 128).T)
    sgnT = f(np.asarray(sb_norm_g, np.float32).reshape(DEPTH * HS, 128).T)
    fgT = f(np.asarray(final_g, np.float32).reshape(NCD, 128).T)
    ws = {"gu1": ffn1_w_gu, "d1": ffn1_w_down, "win": w_in, "wout": w_out, "gu2": ffn2_w_gu, "d2": ffn2_w_down}
    maps = []
    for core in range(N_CORES):
        b, half = core // 2, core % 2
        m = {}
        m["xT"] = f(x[b, half * T:(half + 1) * T, :].T)
        KA, KAP, KAC = cf.KA, cf.KAP, cf.KAC
        cs = c[:, core * KA:(core + 1) * KA]
        m["cT"] = f(cs.reshape(4, KAC, KAP).transpose(2, 1, 0).reshape(KAP, KAC * 4))
        m["adaw"] = f(ada_w[:, core * KA:(core + 1) * KA, :].reshape(DEPTH * KA, 9 * D))
        m["adabT"] = f(ada_b.reshape(DEPTH * 9 * D // 128, 128).T)
        sel4 = np.zeros((128, 4), np.float32)
        sel4[:, b] = 1.0
        m["sel4"] = sel4
        m["flag"] = np.full((128, 1), float(half), np.float32)
        m["ngT"], m["rgnT"], m["sgnT"], m["fgT"] = ngT, rgnT, sgnT, fgT
        for nm, w in ws.items():
            w = np.asarray(w)
            K, N = w.shape[1], w.shape[2]
            r = K // 8
            pw_ = cf.wpw(K, N)
            sh = np.asarray(w[:, core * r:(core + 1) * r, :], np.float32).reshape(DEPTH, r, N // pw_, pw_)
            m[nm] = f(sh.transpose(0, 2, 1, 3).reshape(DEPTH * (N // pw_) * r, pw_))
        m.update(host_constants(cf, core))
        maps.append(m)
    return maps


_CACHE = {}


def run_cfg(cfg, inputs):
    key = (cfg.D, cfg.S, cfg.DEPTH, cfg.HR, cfg.HS)
    if key not in _CACHE:
        _CACHE[key] = build_program(cfg)[0]
    nc = _CACHE[key]
    maps = make_in_maps(cfg, **inputs)
    res = run_bass_kernel_spmd(nc, maps, core_ids=list(range(N_CORES)))
    out = np.empty((cfg.B, cfg.S, cfg.D), np.float32)
    for core in range(N_CORES):
        b, half = core // 2, core % 2
        out[b, half * cfg.T:(half + 1) * cfg.T, :] = res.results[core]["outT"].T
    return out, res


def kernel(**inputs):
    cfg = Cfg()
    out, _ = run_cfg(cfg, inputs)
    return out
```
